# Optimizing a Trainium2 kernel written in Bass

```python
import jax, jax.numpy as jnp
from jax import lax
import numpy as np

D_MODEL = 1024
BATCH = 4
SEQ = 4096
DEPTH = 1
DEC_BATCH = 128
DEC_SEQ = 8
PAST_LEN = 8192
PAGE_SIZE = 128

MIX_W = D_MODEL
ATT_W = MIX_W // 2
MLSTM_W = MIX_W - ATT_W
HEAD_DIM = 64
N_HEADS = ATT_W // HEAD_DIM
N_KV_HEADS = 2
GQA_GROUP = N_HEADS // N_KV_HEADS
KV_W = N_KV_HEADS * HEAD_DIM
WINDOW = 128
BLOCK = WINDOW
M_HEADS = 4
M_HEAD_DIM = MLSTM_W // M_HEADS
CONV_W = 4
N_META = 16
META_PAD = BLOCK - N_META
D_FF = -(-8 * D_MODEL // (3 * 256)) * 256
IN_SPLITS = (ATT_W, KV_W, KV_W, MLSTM_W, MLSTM_W, MLSTM_W, M_HEADS, M_HEADS)
IN_W = sum(IN_SPLITS)
ALIBI_SLOPES = tuple(2.0 ** (-8.0 * (h + 1) / N_HEADS) for h in range(N_HEADS))
DEEPNORM_ALPHA = (2.0 * DEPTH) ** 0.25
DEEPNORM_BETA = (8.0 * DEPTH) ** -0.25
EPS = 1e-5

kernel_name = 'hymba_swa_sink_mlstm_deepnorm_step'


def layer_norm(x, g, b):
    xf = x.astype(jnp.float32)
    mu = jnp.mean(xf, axis=-1, keepdims=True)
    var = jnp.mean(jnp.square(xf - mu), axis=-1, keepdims=True)
    y = (xf - mu) * lax.rsqrt(var + EPS) * g.astype(jnp.float32) + b.astype(jnp.float32)
    return y.astype(x.dtype)


def head_norm(h, g):
    hf = h.astype(jnp.float32)
    mu = jnp.mean(hf, axis=-1, keepdims=True)
    var = jnp.mean(jnp.square(hf - mu), axis=-1, keepdims=True)
    y = ((hf - mu) * lax.rsqrt(var + EPS)).reshape(h.shape[:-2] + (-1,))
    return (y * g.astype(jnp.float32)).astype(h.dtype)


def split_proj(x, w_in):
    z = x @ w_in
    bounds = np.cumsum(IN_SPLITS)[:-1].tolist()
    return jnp.split(z, bounds, axis=-1)


def causal_conv(ext, w_conv, b_conv):
    l = ext.shape[1] - (CONV_W - 1)
    y = b_conv
    for j in range(CONV_W):
        y = y + ext[:, j:j + l] * w_conv[j]
    return y


def swa_attend(q, k, v, key_valid, sinks):
    f32 = jnp.float32
    lq, lk = q.shape[2], k.shape[2]
    s = jnp.einsum('nbqkgd,nbskd->nbkgqs', q.astype(f32), k.astype(f32)) * (HEAD_DIM ** -0.5)
    dist = WINDOW + jnp.arange(lq)[:, None] - jnp.arange(lk)[None, :]
    band = (dist >= 0) & (dist < WINDOW)
    slopes = jnp.asarray(ALIBI_SLOPES, f32).reshape(N_KV_HEADS, GQA_GROUP)
    alibi = -slopes[:, :, None, None] * dist.astype(f32)
    mask = band[None] & key_valid[:, None, :]
    s = jnp.where(mask[None, :, None, None], s + alibi, -jnp.inf)
    sink = sinks.astype(f32).reshape(N_KV_HEADS, GQA_GROUP)[None, None, :, :, None, None]
    mx = jnp.maximum(jnp.max(s, axis=-1, keepdims=True), sink)
    p = jnp.exp(s - mx)
    p = p / (jnp.sum(p, axis=-1, keepdims=True) + jnp.exp(sink - mx))
    return jnp.einsum('nbkgqs,nbskd->nbqkgd', p, v.astype(f32)).astype(v.dtype)


def mlstm_inputs(c_act, vm, i_pre, f_pre, w_mq, w_mk, b_i, b_f):
    n, l, _ = c_act.shape
    ca = c_act.reshape(n, l, M_HEADS, M_HEAD_DIM)
    q = jnp.einsum('nlhd,hde->nhle', ca, w_mq)
    k = jnp.einsum('nlhd,hde->nhle', ca, w_mk) * (M_HEAD_DIM ** -0.5)
    v = vm.reshape(n, l, M_HEADS, M_HEAD_DIM).transpose(0, 2, 1, 3)
    logi = (i_pre + b_i).astype(jnp.float32).transpose(0, 2, 1)
    logf = jax.nn.log_sigmoid((f_pre + b_f).astype(jnp.float32)).transpose(0, 2, 1)
    return q, k, v, logi, logf


def mlstm_chunk(C_s, n_s, m_s, q, k, v, logi, logf):
    f32 = jnp.float32
    C_s, n_s, m_s = C_s.astype(f32), n_s.astype(f32), m_s.astype(f32)
    q, k, v = q.astype(f32), k.astype(f32), v.astype(f32)
    l = q.shape[-2]
    b = jnp.cumsum(logf, axis=-1)
    causal = jnp.tril(jnp.ones((l, l), bool))
    dmat = jnp.where(causal, b[..., :, None] - b[..., None, :] + logi[..., None, :], -jnp.inf)
    m_inter = b + m_s[..., None]
    m_t = jnp.maximum(m_inter, jnp.max(dmat, axis=-1))
    w_inter = jnp.exp(m_inter - m_t)
    sc = jnp.einsum('nhtk,nhsk->nhts', q, k) * jnp.exp(dmat - m_t[..., None])
    num = w_inter[..., None] * jnp.einsum('nhvk,nhtk->nhtv', C_s, q) + jnp.einsum('nhts,nhsv->nhtv', sc, v)
    den = w_inter * jnp.einsum('nhk,nhtk->nht', n_s, q) + jnp.sum(sc, axis=-1)
    h = num / jnp.maximum(jnp.abs(den), jnp.exp(-m_t))[..., None]
    m_end = m_t[..., -1]
    decay = jnp.exp(b[..., -1] + m_s - m_end)
    wk = jnp.exp(b[..., -1:] - b + logi - m_end[..., None])
    C_new = decay[..., None, None] * C_s + jnp.einsum('nhs,nhsv,nhsk->nhvk', wk, v, k)
    n_new = decay[..., None] * n_s + jnp.einsum('nhs,nhsk->nhk', wk, k)
    return h, (C_new, n_new, m_end)


def mix_and_ffn(x, att, h_m, o_pre, g_attn, g_mlstm, w_out, ln1_g, ln1_b, w_gate, w_up, w_down, ln2_g, ln2_b):
    o = jax.nn.sigmoid(o_pre).reshape(h_m.shape)
    y = jnp.concatenate([head_norm(att, g_attn), head_norm(o * h_m, g_mlstm)], axis=-1)
    x1 = layer_norm(DEEPNORM_ALPHA * x + y @ w_out, ln1_g, ln1_b)
    ffn = (jax.nn.silu(x1 @ w_gate) * (x1 @ w_up)) @ w_down
    return layer_norm(DEEPNORM_ALPHA * x1 + ffn, ln2_g, ln2_b)


def prompt_layer(x, valid, key_valid, w_in, w_conv, b_conv, w_mq, w_mk, b_i, b_f, attn_sinks,
                 g_attn, g_mlstm, w_out, ln1_g, ln1_b, w_gate, w_up, w_down, ln2_g, ln2_b):
    n, p, _ = x.shape
    nb = p // BLOCK
    q, k, v, c, vm, o_pre, i_pre, f_pre = split_proj(x, w_in)
    k = k.reshape(n, p, N_KV_HEADS, HEAD_DIM)
    v = v.reshape(n, p, N_KV_HEADS, HEAD_DIM)

    def band_blocks(t):
        tp = jnp.pad(t, ((0, 0), (BLOCK, 0), (0, 0), (0, 0))).reshape(n, nb + 1, BLOCK, N_KV_HEADS, HEAD_DIM)
        return jnp.concatenate([tp[:, :-1], tp[:, 1:]], axis=2)

    att = swa_attend(q.reshape(n, nb, BLOCK, N_KV_HEADS, GQA_GROUP, HEAD_DIM),
                     band_blocks(k), band_blocks(v), key_valid, attn_sinks).reshape(n, p, N_HEADS, HEAD_DIM)

    c = jnp.where(valid[None, :, None], c, jnp.zeros_like(c))
    c_act = jax.nn.silu(causal_conv(jnp.pad(c, ((0, 0), (CONV_W - 1, 0), (0, 0))), w_conv, b_conv))
    qm, km, vmh, logi, logf = mlstm_inputs(c_act, vm, i_pre, f_pre, w_mq, w_mk, b_i, b_f)
    logi = jnp.where(valid, logi, -jnp.inf)
    logf = jnp.where(valid, logf, 0.0)

    def chunks(t):
        return jnp.moveaxis(t.reshape(t.shape[:2] + (nb, BLOCK) + t.shape[3:]), 2, 0)

    f32 = jnp.float32
    init = (jnp.zeros((n, M_HEADS, M_HEAD_DIM, M_HEAD_DIM), f32),
            jnp.zeros((n, M_HEADS, M_HEAD_DIM), f32),
            jnp.zeros((n, M_HEADS), f32))

    def step(carry, xs):
        h, new = mlstm_chunk(carry[0], carry[1], carry[2], xs[0], xs[1], xs[2], xs[3], xs[4])
        return new, h

    (C_f, n_f, m_f), hs = lax.scan(step, init, (chunks(qm), chunks(km), chunks(vmh), chunks(logi), chunks(logf)))
    h_m = jnp.moveaxis(hs, 0, 2).reshape(n, M_HEADS, p, M_HEAD_DIM).transpose(0, 2, 1, 3).astype(x.dtype)
    y = mix_and_ffn(x, att, h_m, o_pre, g_attn, g_mlstm, w_out, ln1_g, ln1_b, w_gate, w_up, w_down, ln2_g, ln2_b)
    return y, (k[:, -WINDOW:], v[:, -WINDOW:], c[:, -(CONV_W - 1):], C_f, n_f, m_f)


def sample_layer(x, cache_k, cache_v, state_conv, state_C, state_n, state_m, w_in, w_conv, b_conv, w_mq, w_mk,
                 b_i, b_f, attn_sinks, g_attn, g_mlstm, w_out, ln1_g, ln1_b, w_gate, w_up, w_down, ln2_g, ln2_b):
    n, l, _ = x.shape
    q, k, v, c, vm, o_pre, i_pre, f_pre = split_proj(x, w_in)
    k_all = jnp.concatenate([cache_k, k.reshape(n, l, N_KV_HEADS, HEAD_DIM)], axis=1)
    v_all = jnp.concatenate([cache_v, v.reshape(n, l, N_KV_HEADS, HEAD_DIM)], axis=1)
    key_valid = jnp.ones((1, WINDOW + l), bool)
    att = swa_attend(q.reshape(n, 1, l, N_KV_HEADS, GQA_GROUP, HEAD_DIM), k_all[:, None], v_all[:, None],
                     key_valid, attn_sinks).reshape(n, l, N_HEADS, HEAD_DIM)

    ext = jnp.concatenate([state_conv, c], axis=1)
    c_act = jax.nn.silu(causal_conv(ext, w_conv, b_conv))
    qm, km, vmh, logi, logf = mlstm_inputs(c_act, vm, i_pre, f_pre, w_mq, w_mk, b_i, b_f)
    h, (C_f, n_f, m_f) = mlstm_chunk(state_C, state_n, state_m, qm, km, vmh, logi, logf)
    h_m = h.transpose(0, 2, 1, 3).astype(x.dtype)
    y = mix_and_ffn(x, att, h_m, o_pre, g_attn, g_mlstm, w_out, ln1_g, ln1_b, w_gate, w_up, w_down, ln2_g, ln2_b)
    return y, (k_all[:, -WINDOW:], v_all[:, -WINDOW:], ext[:, -(CONV_W - 1):], C_f, n_f, m_f)


def setup_inputs(seed: int = 0) -> dict:
    key = jax.random.key(seed)
    ks = jax.random.split(key, 32)
    f32 = jnp.float32

    def nrm(k, shape, scale):
        return jax.random.normal(k, shape, f32) * scale

    return {
        'x_prompt': nrm(ks[0], (BATCH, SEQ, D_MODEL), 1.0),
        'x_sample': nrm(ks[1], (DEC_BATCH, DEC_SEQ, D_MODEL), 1.0),
        'cache_k': nrm(ks[2], (DEPTH, DEC_BATCH, WINDOW, N_KV_HEADS, HEAD_DIM), 1.0),
        'cache_v': nrm(ks[3], (DEPTH, DEC_BATCH, WINDOW, N_KV_HEADS, HEAD_DIM), 1.0),
        'state_conv': nrm(ks[4], (DEPTH, DEC_BATCH, CONV_W - 1, MLSTM_W), 1.0),
        'state_C': nrm(ks[5], (DEPTH, DEC_BATCH, M_HEADS, M_HEAD_DIM, M_HEAD_DIM), 0.1),
        'state_n': nrm(ks[6], (DEPTH, DEC_BATCH, M_HEADS, M_HEAD_DIM), 0.1),
        'state_m': nrm(ks[7], (DEPTH, DEC_BATCH, M_HEADS), 1.0),
        'meta_tokens': nrm(ks[8], (N_META, D_MODEL), 1.0),
        'w_in': nrm(ks[9], (DEPTH, D_MODEL, IN_W), D_MODEL ** -0.5),
        'w_conv': nrm(ks[10], (DEPTH, CONV_W, MLSTM_W), CONV_W ** -0.5),
        'b_conv': nrm(ks[11], (DEPTH, MLSTM_W), 0.02),
        'w_mq': nrm(ks[12], (DEPTH, M_HEADS, M_HEAD_DIM, M_HEAD_DIM), M_HEAD_DIM ** -0.5),
        'w_mk': nrm(ks[13], (DEPTH, M_HEADS, M_HEAD_DIM, M_HEAD_DIM), M_HEAD_DIM ** -0.5),
        'b_i': nrm(ks[14], (DEPTH, M_HEADS), 0.1),
        'b_f': jnp.broadcast_to(jnp.linspace(3.0, 6.0, M_HEADS, dtype=f32), (DEPTH, M_HEADS)) + nrm(ks[15], (DEPTH, M_HEADS), 0.01),
        'attn_sinks': nrm(ks[16], (DEPTH, N_HEADS), 0.5),
        'g_attn': 1.0 + nrm(ks[17], (DEPTH, ATT_W), 0.02),
        'g_mlstm': 1.0 + nrm(ks[18], (DEPTH, MLSTM_W), 0.02),
        'w_out': nrm(ks[19], (DEPTH, MIX_W, D_MODEL), MIX_W ** -0.5 * DEEPNORM_BETA),
        'ln1_g': 1.0 + nrm(ks[20], (DEPTH, D_MODEL), 0.02),
        'ln1_b': nrm(ks[21], (DEPTH, D_MODEL), 0.02),
        'w_gate': nrm(ks[22], (DEPTH, D_MODEL, D_FF), D_MODEL ** -0.5),
        'w_up': nrm(ks[23], (DEPTH, D_MODEL, D_FF), D_MODEL ** -0.5),
        'w_down': nrm(ks[24], (DEPTH, D_FF, D_MODEL), D_FF ** -0.5 * DEEPNORM_BETA),
        'ln2_g': 1.0 + nrm(ks[25], (DEPTH, D_MODEL), 0.02),
        'ln2_b': nrm(ks[26], (DEPTH, D_MODEL), 0.02),
    }


def reference(x_prompt, x_sample, cache_k, cache_v, state_conv, state_C, state_n, state_m, meta_tokens,
              w_in, w_conv, b_conv, w_mq, w_mk, b_i, b_f, attn_sinks, g_attn, g_mlstm, w_out,
              ln1_g, ln1_b, w_gate, w_up, w_down, ln2_g, ln2_b):
    b, s, d = x_prompt.shape
    p = s + BLOCK
    meta = jnp.broadcast_to(meta_tokens[None].astype(x_prompt.dtype), (b, N_META, d))
    xp = jnp.concatenate([jnp.zeros((b, META_PAD, d), x_prompt.dtype), meta, x_prompt], axis=1)
    valid = jnp.arange(p) >= META_PAD
    nb = p // BLOCK
    key_pos = (jnp.arange(nb)[:, None] - 1) * BLOCK + jnp.arange(2 * BLOCK)[None, :]
    key_valid = key_pos >= META_PAD
    xs = x_sample
    p_states, s_states = [], []
    for layer in range(DEPTH):
        lw = (w_in[layer], w_conv[layer], b_conv[layer], w_mq[layer], w_mk[layer], b_i[layer], b_f[layer],
              attn_sinks[layer], g_attn[layer], g_mlstm[layer], w_out[layer], ln1_g[layer], ln1_b[layer],
              w_gate[layer], w_up[layer], w_down[layer], ln2_g[layer], ln2_b[layer])
        xp, ps = prompt_layer(xp, valid, key_valid, *lw)
        xs, ss = sample_layer(xs, cache_k[layer], cache_v[layer], state_conv[layer], state_C[layer],
                              state_n[layer], state_m[layer], *lw)
        p_states.append(ps)
        s_states.append(ss)
    pk, pv, pconv, pC, pn, pm = [jnp.stack(t) for t in zip(*p_states)]
    sk, sv, sconv, sC, sn, sm = [jnp.stack(t) for t in zip(*s_states)]
    y_prompt = xp[:, BLOCK:]
    return (y_prompt, xs, pk, pv, pconv, pC, pn, pm, sk, sv, sconv, sC, sn, sm)
```

```python
import contextlib
import os
import numpy as np
import ml_dtypes
import concourse.bass as bass
import concourse.mybir as mybir
from concourse.bass_utils import run_bass_kernel_spmd

F32 = mybir.dt.float32
BF16 = mybir.dt.bfloat16
AF = mybir.ActivationFunctionType
ALU = mybir.AluOpType
AX = mybir.AxisListType

ENGS = ("pe", "act", "dve", "pool", "sp")

D = 1024
KC = 8
T = 128
IN_W = 2312
OFF_Q, OFF_K, OFF_V, OFF_C, OFF_VM, OFF_O, OFF_I, OFF_F = 0, 512, 640, 768, 1280, 1792, 2304, 2308
DFF = 2816
NJ = 22
ALPHA = float(2.0 ** 0.25)
EPS = 1e-5
NEGM = -30000.0
NPRE = 17
NFULL = 16
SLOPES = [2.0 ** (-(h + 1)) for h in range(8)]
KSCALE = float(128.0 ** -0.5)


class TT:
    __slots__ = ("name", "lw", "rd", "rd_dma")

    def __init__(self, name):
        self.name = name
        self.lw = None
        self.rd = []
        self.rd_dma = []


class DSem:
    __slots__ = ("h", "count")

    def __init__(self, h):
        self.h = h
        self.count = 0


class Op:
    __slots__ = ("eng", "fn", "deps", "odeps", "signal", "sem", "semval", "is_dma", "ndma", "idx")


class Prog:
    def __init__(self, nc):
        self.nc = nc
        self.ops = {e: [] for e in ENGS}
        self.out_dmas = []
        self.nops = 0

    def emit(self, eng, fn, reads=(), writes=(), dsem=None, ndma=0, is_out=False):
        op = Op()
        op.eng = eng
        op.fn = fn
        op.signal = False
        op.is_dma = dsem is not None
        op.ndma = ndma
        op.sem = None
        op.semval = 0
        raw = set()
        oth = set()
        for t in reads:
            if t.lw is not None:
                raw.add(t.lw)
        for t in writes:
            if t.lw is not None:
                oth.add(t.lw)
            for r in t.rd:
                oth.add(r)
            for r in t.rd_dma:
                oth.add(r)
        deps = []
        odeps = []
        for d in raw | oth:
            if (not d.is_dma) and (not op.is_dma) and d.eng == eng and eng == "pe":
                odeps.append(d)
                continue
            deps.append(d)
            d.signal = True
        op.deps = deps
        op.odeps = odeps
        op.idx = self.nops
        self.nops += 1
        if op.is_dma:
            dsem.count += 16 * ndma
            op.sem = dsem
            op.semval = dsem.count
        for t in reads:
            if op.is_dma:
                t.rd_dma.append(op)
            else:
                t.rd.append(op)
        for t in writes:
            t.lw = op
            t.rd = []
            t.rd_dma = []
        self.ops[eng].append(op)
        if is_out:
            self.out_dmas.append(op)
        return op

    def schedule(self):
        if os.environ.get("KSCHED", "1") == "0":
            return
        allops = [o for e in ENGS for o in self.ops[e]]
        allops.sort(key=lambda o: o.idx)

        class _Ret:
            def then_inc(self, *a, **k):
                return self

        class _Mock:
            def __init__(self):
                self.n = 0
                self.f32 = False
                self.nd = 0

            def __getattr__(self, name):
                def f(*a, **k):
                    for cand in (k.get("out", a[0] if a else None), k.get("in_", None), k.get("in0", None)):
                        try:
                            n = 1
                            for v in cand.shape[1:]:
                                n *= int(v)
                            self.n = max(self.n, n)
                        except Exception:
                            pass
                    src = k.get("lhsT", k.get("in_", None))
                    try:
                        if name in ("matmul", "transpose") and src is not None and src.dtype == F32:
                            self.f32 = True
                    except Exception:
                        pass
                    self.nd += 1
                    return _Ret()
                return f

        cost = {}
        dsize = {}
        for o in allops:
            m = _Mock()
            try:
                if o.is_dma:
                    o.fn(m, None)
                else:
                    o.fn(m)
            except Exception:
                pass
            n = m.n if m.n else 128
            if o.is_dma:
                c = 0.15 * max(1, o.ndma)
                dsize[id(o)] = n * max(1, o.ndma)
            elif o.eng == "pe":
                c = 0.035 + n / 2400.0 * (4.0 if m.f32 else 1.0)
            elif o.eng == "act":
                c = 0.25 + n * 0.00085
            elif o.eng == "dve":
                c = 0.17 + n * 0.0011
            else:
                c = 0.35 + n * 0.0022
            cost[id(o)] = c
        succ = {}
        indeg = {}
        for o in allops:
            ds = set(o.deps) | set(o.odeps)
            indeg[id(o)] = len(ds)
            for d in ds:
                succ.setdefault(id(d), []).append(o)
        finish = {}
        ready_at = {id(o): 0.0 for o in allops}
        free = {e: 0.0 for e in ENGS}
        newq = {e: [] for e in ENGS}
        ready = [o for o in allops if indeg[id(o)] == 0]
        nleft = len(allops)
        while nleft:
            best = None
            bkey = None
            for o in ready:
                st = max(ready_at[id(o)], free[o.eng])
                key = (st, o.idx)
                if bkey is None or key < bkey:
                    best, bkey = o, key
            o = best
            ready.remove(o)
            st = bkey[0]
            if o.is_dma:
                free[o.eng] = st + cost[id(o)]
                fin_t = st + 2.5 + dsize.get(id(o), 128) * 0.0017
            else:
                free[o.eng] = st + cost[id(o)]
                fin_t = free[o.eng]
            finish[id(o)] = fin_t
            newq[o.eng].append(o)
            nleft -= 1
            for sct in succ.get(id(o), ()):
                lat = 0.05 if (sct.eng == o.eng and not o.is_dma) else 0.2
                t = fin_t + lat
                if t > ready_at[id(sct)]:
                    ready_at[id(sct)] = t
                indeg[id(sct)] -= 1
                if indeg[id(sct)] == 0:
                    ready.append(sct)
        self.ops = newq

    def lower(self, esems):
        nc = self.nc
        self.schedule()
        fin = Op()
        fin.eng = "sp"
        fin.fn = None
        fin.signal = False
        fin.is_dma = False
        fin.ndma = 0
        fin.deps = list(self.out_dmas)
        fin.odeps = []
        fin.idx = self.nops
        fin.sem = None
        fin.semval = 0
        self.ops["sp"].append(fin)
        for e in ENGS:
            c = 0
            for op in self.ops[e]:
                if op.is_dma:
                    continue
                if op.signal:
                    c += 1
                    op.semval = c
                    op.sem = esems[e]

        def replay(ename, e):
            waited = {}
            for op in self.ops[ename]:
                need = {}
                for d in op.deps:
                    h = d.sem.h if d.is_dma else d.sem
                    v = d.semval
                    key = id(h)
                    if key not in need or need[key][1] < v:
                        need[key] = (h, v)
                for key, (h, v) in need.items():
                    if waited.get(key, 0) < v:
                        e.wait_ge(h, v)
                        waited[key] = v
                if op.fn is None:
                    continue
                if op.is_dma:
                    op.fn(e, op.sem.h)
                else:
                    ins = op.fn(e)
                    if op.signal:
                        ins.then_inc(op.sem, 1)

        with nc.Block() as block:
            @block.tensor
            def _(e):
                replay("pe", e)

            @block.scalar
            def _(e):
                replay("act", e)

            @block.vector
            def _(e):
                replay("dve", e)

            @block.gpsimd
            def _(e):
                replay("pool", e)

            @block.sync
            def _(e):
                replay("sp", e)


def _tables():
    t = {}
    t["identf"] = np.eye(128, dtype=np.float32)
    BIG = -1.0e6
    i = np.arange(128)[:, None]
    s = np.arange(256)[None, :]
    dist = 128 + i - s
    nd = np.where((dist >= 0) & (dist < 128), -dist.astype(np.float32), BIG).astype(np.float32)
    t["nd_std"] = nd
    ndf = nd.copy()
    ndf[:, 0:112] = BIG
    t["nd_meta"] = ndf
    tt = np.arange(128)
    l = (tt % 8)[:, None]
    j = (tt // 8)[:, None]
    sc = np.arange(128)[None, :]
    dc = 128 + l - sc
    ndc = np.where(dc < 128, -dc.astype(np.float32), BIG)
    lp = (tt % 8)[None, :]
    jp = (tt // 8)[None, :]
    dn = l - lp
    ndn = np.where((j == jp) & (dn >= 0), -dn.astype(np.float32), BIG)
    t["nd_smp"] = np.concatenate([ndc, ndn], axis=1).astype(np.float32)
    ss = np.arange(128)[:, None]
    tq = np.arange(128)[None, :]
    t["mT_std"] = np.where(ss <= tq, 0.0, NEGM).astype(np.float32)
    t["mT_smp"] = np.where((ss // 8 == tq // 8) & (ss <= tq), 0.0, NEGM).astype(np.float32)
    r = np.ones((4, 128), np.float32)
    r[:, 0] = 0.0
    t["rst_p"] = r
    rn = np.zeros((4, 128), np.float32)
    rn[:, 0] = -1.0e30
    t["rsn_p"] = rn
    r = np.ones((4, 128), np.float32)
    r[:, 0::8] = 0.0
    t["rst_s"] = r
    rn = np.zeros((4, 128), np.float32)
    rn[:, 0::8] = -1.0e30
    t["rsn_s"] = rn
    sel = np.zeros((4, 4, 128), np.float32)
    for h in range(4):
        sel[h, h, :] = 1.0
    t["selrows"] = sel.reshape(4, 512)
    t["i4"] = np.eye(4, dtype=np.float32)
    m16 = np.zeros((128, 16), np.float32)
    m16[np.arange(128), np.arange(128) // 8] = 1.0
    t["m16"] = m16
    sl = np.zeros((128, 16), np.float32)
    sl[np.arange(16) * 8 + 7, np.arange(16)] = 1.0
    t["sellast"] = sl
    qs = np.zeros((128, 4), np.float32)
    for c in range(4):
        qs[0:64, c] = 0.125 / SLOPES[c]
        qs[64:128, c] = 0.125 / SLOPES[4 + c]
    t["qscale"] = qs
    t["invslope"] = np.tile(np.array([1.0 / v for v in SLOPES], np.float32)[None, :], (128, 1))
    return t


TABLE_SHAPES = {
    "identf": (128, 128), "nd_std": (128, 256), "nd_smp": (128, 256),
    "mT_std": (128, 128), "mT_smp": (128, 128), "rst_p": (4, 128), "rsn_p": (4, 128),
    "rst_s": (4, 128), "rsn_s": (4, 128), "selrows": (4, 512), "i4": (4, 4), "m16": (128, 16),
    "sellast": (128, 16), "qscale": (128, 4), "invslope": (128, 8),
}

IN_SHAPES = {
    "xs": (NFULL * 128, D), "xpre": (NPRE * 128, D), "vrow": (4, NPRE * 128),
    "nd_first": (128, 256),
    "xsm": (128, D), "ck": (16, 128, 128), "cv": (16, 128, 128), "scv": (48, 512),
    "sC": (16, 4, 128, 128), "sn": (64, 128), "smm": (16, 4),
    "w_in": (D, IN_W), "w_conv": (4, 512), "b_conv": (1, 512), "w_mq": (4, 128, 128), "w_mk": (4, 128, 128),
    "b_i": (4, 1), "b_f": (4, 1), "sinks": (1, 8), "g_attn": (1, 512), "g_mlstm": (1, 512), "w_out": (D, D),
    "ln1_g": (1, D), "ln1_b": (1, D), "w_gate": (D, DFF), "w_up": (D, DFF), "w_down": (DFF, D),
    "ln2_g": (1, D), "ln2_b": (1, D),
}
IN_SHAPES.update(TABLE_SHAPES)

OUT_SHAPES = {
    "y": (NFULL * 128, D), "ys": (128, D), "pk": (128, 128), "pv": (128, 128), "pconv": (3, 512),
    "pC": (4, 128, 128), "pn": (4, 128), "pm": (4, 1),
    "sk": (16, 128, 128), "sv": (16, 128, 128), "sconv": (16, 3, 512), "sCo": (16, 4, 128, 128),
    "sno": (16, 4, 128), "smo": (16, 4),
}


class Buf:
    __slots__ = ("t", "tt", "ds", "dg")

    def __init__(self, t, tt, ds=None, dg=None):
        self.t = t
        self.tt = tt
        self.ds = ds
        self.dg = dg


def build_program():
    nc = bass.Bass("TRN2", target_bir_lowering=False)
    P = Prog(nc)
    es = contextlib.ExitStack()
    with es:
        I = {n: nc.dram_tensor(n, list(s), F32, kind="ExternalInput") for n, s in IN_SHAPES.items()}
        O = {n: nc.dram_tensor(n, list(s), F32, kind="ExternalOutput") for n, s in OUT_SHAPES.items()}
        IA = {n: h.ap() for n, h in I.items()}
        OA = {n: h.ap() for n, h in O.items()}

        def new_sem(name):
            return es.enter_context(nc.semaphore(name))

        esems = {e: new_sem("s_" + e) for e in ENGS}

        def mk(name, shape, dt=F32, dma=False):
            t = es.enter_context(nc.sbuf_tensor("sb_" + name, list(shape), dt))
            return Buf(t, TT(name), DSem(new_sem("d_" + name)) if dma else None)

        def pstride(buf):
            return buf.t[:].ap[0][0]

        pbs = []
        for i in range(8):
            t = es.enter_context(nc.psum_tensor("pb%d" % i, [128, 512], F32))
            pbs.append(Buf(t, TT("pb%d" % i)))
        free_banks = list(pbs)

        def bank():
            assert free_banks, "PSUM banks exhausted"
            return free_banks.pop(0)

        def release(*bks):
            for b in bks:
                assert b not in free_banks
                free_banks.append(b)

        identf = mk("identf", [128, 128])
        identb = mk("identb", [128, 128], BF16)
        nd_std = mk("nd_std", [128, 256], BF16)
        nd_first = mk("nd_first", [128, 256], BF16)
        nd_smp = mk("nd_smp", [128, 256], BF16)
        mT_std = mk("mT_std", [128, 128], BF16)
        mT_smp = mk("mT_smp", [128, 128], BF16)
        invslope = mk("invslope", [128, 8])
        sinkq = mk("sinkq", [128, 8])
        rowc = mk("rowc", [4, 4, 128])
        selrows = mk("selrows", [4, 512])
        i4 = mk("i4", [4, 4])
        ones4 = mk("ones4", [4, 128])
        m16f = mk("m16f", [128, 16])
        m16b = mk("m16b", [128, 16], BF16)
        sellast = mk("sellast", [128, 16])
        qscale = mk("qscale", [128, 4])
        wconv = mk("wconv", [128, 4, 4])
        bconv = mk("bconv", [128, 4])
        bif = mk("bif", [4, 2])
        sinkb = mk("sinkb", [128, 8])
        gcol = mk("gcol", [128, 8])
        lnp = mk("lnp", [128, 4, 1024], F32, dma=True)
        wmq = mk("wmq", [128, 4, 128], BF16)
        wmk = mk("wmk", [128, 4, 128], BF16)
        cst = mk("cst", [128, 4])
        cds = DSem(new_sem("d_const"))
        cds2 = DSem(new_sem("d_const2"))

        def load_consts(e, s):
            def d(out, in_, slow=False):
                if slow:
                    e.dma_start(out=out, in_=in_, allow_slow_non_contiguous=True).then_inc(s, 16)
                else:
                    e.dma_start(out=out, in_=in_).then_inc(s, 16)
            d(identf.t[:], IA["identf"])
            d(invslope.t[:], IA["invslope"])
            d(rowc.t[:, 0, :], IA["rst_p"])
            d(rowc.t[:, 1, :], IA["rsn_p"])
            d(rowc.t[:, 2, :], IA["rst_s"])
            d(rowc.t[:, 3, :], IA["rsn_s"])
            d(selrows.t[:], IA["selrows"])
            d(i4.t[:], IA["i4"])
            d(m16f.t[:], IA["m16"])
            d(sellast.t[:], IA["sellast"])
            d(qscale.t[:], IA["qscale"])
            for j in range(4):
                d(wconv.t[:, :, j], bass.AP(I["w_conv"], j * 512, [[1, 128], [128, 4]]), slow=True)
            d(bconv.t[:], bass.AP(I["b_conv"], 0, [[1, 128], [128, 4]]), slow=True)
            d(bif.t[:, 0:1], IA["b_i"])
            d(bif.t[:, 1:2], IA["b_f"])
            d(sinkb.t[:], IA["sinks"].to_broadcast([128, 8]))
            d(gcol.t[:, 0:4], bass.AP(I["g_attn"], 0, [[1, 128], [128, 4]]), slow=True)
            d(gcol.t[:, 4:8], bass.AP(I["g_mlstm"], 0, [[1, 128], [128, 4]]), slow=True)
            d(lnp.t[:, 0, :], IA["ln1_g"].to_broadcast([128, 1024]))
            d(lnp.t[:, 1, :], IA["ln1_b"].to_broadcast([128, 1024]))
            d(lnp.t[:, 2, :], IA["ln2_g"].to_broadcast([128, 1024]))
            d(lnp.t[:, 3, :], IA["ln2_b"].to_broadcast([128, 1024]))
        NCONST = 25
        const_tts = [b.tt for b in (identf, invslope, rowc, selrows, i4, m16f,
                                    sellast, qscale, wconv, bconv, bif, sinkb, gcol, lnp)]
        P.emit("sp", load_consts, writes=const_tts, dsem=cds, ndma=NCONST)

        def load_consts2(e, s):
            e.dma_start(out=wmq.t[:], in_=IA["w_mq"].rearrange("h d e -> d h e")).then_inc(s, 16)
            e.dma_start(out=wmk.t[:], in_=IA["w_mk"].rearrange("h d e -> d h e")).then_inc(s, 16)
            e.dma_start(out=m16b.t[:], in_=IA["m16"]).then_inc(s, 16)
            e.dma_start(out=nd_std.t[:], in_=IA["nd_std"]).then_inc(s, 16)
            e.dma_start(out=nd_first.t[:], in_=IA["nd_first"]).then_inc(s, 16)
            e.dma_start(out=nd_smp.t[:], in_=IA["nd_smp"]).then_inc(s, 16)
            e.dma_start(out=mT_std.t[:], in_=IA["mT_std"]).then_inc(s, 16)
            e.dma_start(out=mT_smp.t[:], in_=IA["mT_smp"]).then_inc(s, 16)
        P.emit("pool", load_consts2, writes=[wmq.tt, wmk.tt, m16b.tt, nd_std.tt, nd_first.tt, nd_smp.tt, mT_std.tt, mT_smp.tt], dsem=cds2, ndma=8)
        P.emit("dve", lambda e: e.tensor_tensor(out=sinkq.t[:], in0=sinkb.t[:], in1=invslope.t[:], op=ALU.mult),
               reads=[sinkb.tt, invslope.tt], writes=[sinkq.tt])
        P.emit("dve", lambda e: e.tensor_copy(out=identb.t[:], in_=identf.t[:]), reads=[identf.tt], writes=[identb.tt])
        P.emit("dve", lambda e: e.memset(ones4.t[:], 1.0), writes=[ones4.tt])
        P.emit("dve", lambda e: e.memset(cst.t[:, 0:1], 1.0), writes=[cst.tt])
        P.emit("dve", lambda e: e.memset(cst.t[:, 1:2], EPS), writes=[cst.tt])

        xf = mk("xf", [128, 4, 1024], F32, dma=True)
        xfB = [Buf(xf.t, TT("xf_b%d" % i), DSem(new_sem("d_xfb%d" % i))) for i in range(4)]
        xfH = [Buf(xf.t, TT("xf_h%d" % i)) for i in range(4)]

        def XB(b):
            return [xfB[b], xfH[b]]

        def XBall(n):
            return xfB[0:n] + xfH[0:n]
        actT = mk("actT", [128, 8, 512], BF16)
        qT = mk("qT", [128, 4, 512], BF16)
        kT = mk("kT", [128, 640], BF16)
        cT = mk("cT", [128, 4 * 515], F32)
        cact = mk("cact", [128, 4, 512], BF16)
        v_bf = mk("v_bf", [128, 5, 128], BF16)
        vm_ext = mk("vm_ext", [128, 4, 4, 129], BF16)
        osig = mk("osig", [128, 4, 512], BF16)
        kvf = mk("kvf", [128, 256], F32, dma=True)
        mqT = mk("mqT", [128, 4, 512], BF16)
        mkT = mk("mkT", [128, 4, 512], BF16)
        mkw = [mk("mkw%d" % i, [128, 4, 128], BF16) for i in range(2)]
        fall = mk("fall", [4, 512], F32)
        logiall = mk("logiall", [4, 512], F32)
        vrow = mk("vrow", [4, 512], F32, dma=True)
        T2all = mk("T2all", [4, 512], F32)
        Ball = mk("Ball", [4, 512], F32)
        NEGUall = mk("NEGUall", [4, 512], F32)
        RPKall = mk("RPKall", [4, 5, 512], F32)
        RTMP = mk("RTMP", [4, 128])
        msall = mk("msall", [4, 8], F32, dma=True)
        ms16 = mk("ms16", [4, 16], F32, dma=True)
        colsall = mk("colsall", [128, 4, 20])
        decBall = mk("decBall", [128, 64])
        dexp = mk("dexp", [4, 64])
        pexp = [mk("pexp%d" % i, [128, 256], BF16) for i in range(3)]
        PT = [mk("PT%d" % i, [128, 256], BF16) for i in range(3)]
        stat = [mk("stat%d" % i, [128, 6, 8]) for i in range(2)]
        att = mk("att", [128, 512])
        hn_a = mk("hn_a", [128, 4, 8])
        hn_m = mk("hn_m", [128, 4, 8])
        xc = mk("xc", [128, 512])
        sq = mk("sq", [128, 512])
        tmpE = mk("tmpE", [128, 512], F32, dma=True)
        DTt = mk("DTt", [128, 512], F32, dma=True)
        scT = mk("scT", [128, 512], BF16)
        intsb = mk("intsb", [128, 4, 129])
        tot = mk("tot", [128, 4, 129])
        hm = mk("hm", [128, 512], F32, dma=True)
        mst = mk("mst", [128, 3, 4])
        Cext = mk("Cext", [128, 4, 129], F32, dma=True)
        Cbf = mk("Cbf", [128, 4, 129], BF16)
        ybuf = mk("ybuf", [128, 4, 1024], BF16)
        lnst = mk("lnst", [128, 4, 12])
        lnstB = [Buf(lnst.t, TT("lnst_b%d" % i)) for i in range(4)]
        h1T = mk("h1T", [128, 22, 512], BF16, dma=True)
        NS = 3
        ring = [mk("ring%d" % i, [128, 4096], BF16, dma=True) for i in range(NS)]
        ring_i = [0]
        wtmp = [xc, sq]
        wtmp2 = [tmpE, DTt]
        junkB = [Buf(sq.t[:].bitcast(BF16), sq.tt), Buf(xc.t[:].bitcast(BF16), xc.tt)]
        cfin = tmpE
        ctok = DTt
        sc48 = Buf(hm.t[0:48, :], hm.tt, hm.ds)
        xfb = xf.t[:].rearrange("p b d -> p (b d)").bitcast(BF16)
        ckb = Buf(xfb[:, 2048:4096].rearrange("p (j k) -> p j k", j=16), xfB[1].tt, DSem(new_sem("d_ckb")))
        cVb = Buf(xfb[:, 4096:6144].rearrange("p (j k) -> p j k", j=16), xfB[2].tt, DSem(new_sem("d_cvb")))
        cKT = Buf(xfb[:, 6144:8192].rearrange("p (j k) -> p j k", j=16), xfB[3].tt, None)
        ybf = ybuf.t[:].rearrange("p b d -> p (b d)")
        ZA = Buf(ybf[:, 1024:3072], ybuf.tt, None,
                 bass.AP(ybuf.t, 1024, [[ybuf.t[:].ap[0][0], 128], [136, 16], [1, 8]]))
        h1f = h1T.t[:].rearrange("p j t -> p (j t)")
        PZ = Buf(h1f[:, 8192:10240], h1T.tt, None,
                 bass.AP(h1T.t, 8192, [[h1T.t[:].ap[0][0], 128], [136, 16], [1, 8]]))
        ZM = mk("ZM", [128, 16 * 128], BF16)
        ZM.dg = bass.AP(ZM.t, 0, [[ZM.t[:].ap[0][0], 128], [136, 16], [1, 8]])
        VZ = mk("VZ", [128, 16, 128], BF16)
        CTb = Buf(cT.t[:].bitcast(BF16)[:, 2048:4112].rearrange("p (j k) -> p j k", j=16), cT.tt)
        snat = mk("snat", [64, 128], F32, dma=True)
        snT = mk("snT", [128, 64], F32)
        n16 = mk("n16", [16, 4, 128], F32, dma=True)
        s16 = mk("s16", [16, 8], F32, dma=True)
        csm = mk("csm", [128, 4, 128], F32)
        lnp_flat = lnp.t[:].rearrange("p a d -> p (a d)")
        lnpB = Buf(lnp.t, TT("lnpB"), DSem(new_sem("d_lnpB")))
        cnb = [lnp, lnpB]

        def Cnat_view(i):
            return lnp_flat[:, i * 2048:(i + 1) * 2048].rearrange("p (j k) -> p j k", j=16)

        def E(eng, fn, reads=(), writes=()):
            return P.emit(eng, fn, reads=[b.tt for b in reads], writes=[b.tt for b in writes])

        def DMA(eng, fn, n, reads=(), writes=(), ds=None, is_out=False):
            return P.emit(eng, fn, reads=[b.tt for b in reads], writes=[b.tt for b in writes], dsem=ds, ndma=n,
                          is_out=is_out)

        def next_slot():
            s = ring[ring_i[0]]
            ring_i[0] = (ring_i[0] + 1) % NS
            return s

        w_in_r = IA["w_in"].rearrange("(c p) n -> p c n", p=128)
        w_out_r = IA["w_out"].rearrange("(c p) n -> p c n", p=128)
        w_gate_r = IA["w_gate"].rearrange("(c p) n -> p c n", p=128)
        w_up_r = IA["w_up"].rearrange("(c p) n -> p c n", p=128)
        w_down_r = IA["w_down"].rearrange("(j p) n -> p j n", p=128)

        def load_piece(pairs, extra_writes=()):
            slot = next_slot()

            def fn(e, s, slot=slot, pairs=pairs):
                for dfn, src in pairs:
                    e.dma_start(out=dfn(slot.t), in_=src).then_inc(s, 16)
            DMA("pool", fn, len(pairs), writes=[slot] + list(extra_writes), ds=slot.ds)
            return slot

        def v3(t, off, a, b_, c=None):
            ps = t[:].ap[0][0]
            if c is None:
                return bass.AP(t, off, [[ps, 128], [b_, a], [1, b_]])
            return bass.AP(t, off, [[ps, 128], [b_ * c, a], [c, b_], [1, c]])

        def load_q():
            pairs = []
            for hf in range(2):
                for c in range(4):
                    def dfn(t, hf=hf, c=c):
                        return bass.AP(t, c * 128 + hf * 64, [[t[:].ap[0][0], 128], [512, 8], [1, 64]])
                    src = bass.AP(I["w_in"], OFF_Q + hf * 256 + c * 64, [[IN_W, 128], [128 * IN_W, 8], [1, 64]])
                    pairs.append((dfn, src))
            return load_piece(pairs)

        def load_kvif():
            def d0(t):
                return v3(t, 0, 8, 264)[:, :, 0:256]

            def d1(t):
                return v3(t, 0, 8, 264)[:, :, 256:264]
            return load_piece([(d0, w_in_r[:, :, OFF_K:OFF_K + 256]), (d1, w_in_r[:, :, OFF_I:OFF_I + 8])])

        def load_cols512(src_r, off, extra_writes=()):
            return load_piece([(lambda t: v3(t, 0, 8, 512), src_r[:, :, off:off + 512])], extra_writes)

        wgb = nc.dram_tensor("w_gate_bf", [D, DFF], BF16, kind="Internal")
        wub = nc.dram_tensor("w_up_bf", [D, DFF], BF16, kind="Internal")
        wgb_r = wgb.ap().rearrange("(c p) n -> p c n", p=128)
        wub_r = wub.ap().rearrange("(c p) n -> p c n", p=128)
        conv_bufs = [Buf(None, TT("wcv%d" % k), DSem(new_sem("d_wcv%d" % k))) for k in range(4)]
        fences = [Buf(None, TT("fence%d" % k)) for k in range(5)]

        def convert_gu(k):
            r0, r1 = 256 * k, 256 * (k + 1)

            def fn(e, s):
                e.dma_start(out=wgb.ap()[r0:r1, :], in_=IA["w_gate"][r0:r1, :]).then_inc(s, 16)
                e.dma_start(out=wub.ap()[r0:r1, :], in_=IA["w_up"][r0:r1, :]).then_inc(s, 16)
            DMA("pool", fn, 2, reads=[fences[k]], writes=[conv_bufs[k]], ds=conv_bufs[k].ds)

        def load_gu(jj):
            slot = next_slot()

            def fn(e, s, slot=slot):
                e.dma_start(out=v3(slot.t, 0, 8, 256), in_=wgb_r[:, :, jj * 256:(jj + 1) * 256]).then_inc(s, 16)
                e.dma_start(out=v3(slot.t, 2048, 8, 256), in_=wub_r[:, :, jj * 256:(jj + 1) * 256]).then_inc(s, 16)
            DMA("sp", fn, 2, reads=conv_bufs, writes=[slot], ds=slot.ds)
            return slot

        def load_wd(e8):
            return load_piece([(lambda t: v3(t, 0, 22, 128), w_down_r[:, :, e8 * 128:(e8 + 1) * 128])])

        def load_wd_rows(p, nj):
            return load_piece([(lambda t: v3(t, 0, nj, 1024), w_down_r[:, p * 4:p * 4 + nj, :])])

        E("dve", lambda e: e.memset(Cext.t[:], 0.0), writes=[Cext])
        E("dve", lambda e: e.memset(Cbf.t[:], 0.0), writes=[Cbf])
        E("dve", lambda e: e.memset(msall.t[:], 0.0), writes=[msall])
        E("dve", lambda e: e.memset(cT.t[:], 0.0), writes=[cT])
        E("dve", lambda e: e.memset(vm_ext.t[:], 1.0), writes=[vm_ext])
        E("dve", lambda e: e.memset(kT.t[:], 0.0), writes=[kT])
        E("dve", lambda e: e.memset(v_bf.t[:], 0.0), writes=[v_bf])
        E("pool", lambda e: e.memset(ZM.t[:], 0.0), writes=[ZM])

        g = type("G", (), {})()
        g.__dict__.update(dict(locals()))
        _emit_groups(g)
        assert len(free_banks) == 8
        P.lower(esems)
    return nc


def _emit_groups(g):
    P, E, DMA, bank, release, v3 = g.P, g.E, g.DMA, g.bank, g.release, g.v3
    I, IA, OA = g.I, g.IA, g.OA
    xf, actT, qT, kT, cT, cact = g.xf, g.actT, g.qT, g.kT, g.cT, g.cact
    identf, identb = g.identf, g.identb
    HS = [0, 4, 1, 5, 2, 6, 3, 7]

    def diag(buf):
        return buf.dg

    def zs(buf, j):
        return buf.t[:, j * 128:(j + 1) * 128]

    ps_cT = cT.t[:].ap[0][0]
    hcount = [0]

    def headnorm(srcbuf, nh, hd, dst_ap, sqb, hn):
        n = nh * hd
        x3 = srcbuf.t[:, 0:n].rearrange("p (h d) -> p h d", h=nh)
        sq3 = sqb.t[:, 0:n].rearrange("p (h d) -> p h d", h=nh)
        E("dve", lambda e: e.tensor_reduce(out=hn.t[:, 0, 0:nh], in_=x3, axis=AX.X, op=ALU.add), reads=[srcbuf], writes=[hn])
        E("dve", lambda e: e.tensor_scalar(out=hn.t[:, 1, 0:nh], in0=hn.t[:, 0, 0:nh], scalar1=1.0 / hd, scalar2=None,
                                           op0=ALU.mult), reads=[hn], writes=[hn])
        yield
        E("pool", lambda e: e.tensor_tensor(out=x3, in0=x3, in1=hn.t[:, 1, 0:nh].unsqueeze(2).to_broadcast([128, nh, hd]),
                                            op=ALU.subtract), reads=[srcbuf, hn], writes=[srcbuf])
        E("pool", lambda e: e.tensor_tensor(out=sqb.t[:, 0:n], in0=srcbuf.t[:, 0:n], in1=srcbuf.t[:, 0:n], op=ALU.mult), reads=[srcbuf], writes=[sqb])
        yield
        E("dve", lambda e: e.tensor_reduce(out=hn.t[:, 2, 0:nh], in_=sq3, axis=AX.X, op=ALU.add), reads=[sqb], writes=[hn])
        E("act", lambda e: e.activation(out=hn.t[:, 3, 0:nh], in_=hn.t[:, 2, 0:nh], func=AF.Ln, bias=g.cst.t[:, 1:2],
                                        scale=1.0 / hd), reads=[hn, g.cst], writes=[hn])
        E("act", lambda e: e.activation(out=hn.t[:, 3, 0:nh], in_=hn.t[:, 3, 0:nh], func=AF.Exp, scale=-0.5),
          reads=[hn], writes=[hn])
        yield
        E("pool", lambda e: e.tensor_tensor(out=dst_ap.rearrange("p (h d) -> p h d", h=nh), in0=x3,
                                            in1=hn.t[:, 3, 0:nh].unsqueeze(2).to_broadcast([128, nh, hd]), op=ALU.mult),
          reads=[srcbuf, hn], writes=[g.ybuf])

    def layernorm(b, gi, lnst_slot):
        ls = g.lnst
        lsb = g.lnstB[b]
        xb = g.XB(b)
        k = lnst_slot
        row = xf.t[:, b, :]
        jk = g.junkB[b % 2]
        E("act", lambda e: e.activation(out=jk.t[:], in_=row, func=AF.Identity, accum_out=ls.t[:, k, 0:1]),
          reads=xb, writes=[jk, lsb])
        E("act", lambda e: e.activation(out=jk.t[:], in_=row, func=AF.Square, accum_out=ls.t[:, k, 2:3]),
          reads=xb, writes=[jk, lsb])
        E("dve", lambda e: e.tensor_scalar(out=ls.t[:, k, 3:4], in0=ls.t[:, k, 0:1], scalar1=1.0 / 1024, scalar2=None,
                                           op0=ALU.mult), reads=[lsb], writes=[lsb])
        E("dve", lambda e: e.tensor_tensor(out=ls.t[:, k, 4:5], in0=ls.t[:, k, 3:4], in1=ls.t[:, k, 3:4], op=ALU.mult),
          reads=[lsb], writes=[lsb])
        E("dve", lambda e: e.scalar_tensor_tensor(out=ls.t[:, k, 5:6], in0=ls.t[:, k, 2:3], scalar=1.0 / 1024,
                                                  in1=ls.t[:, k, 4:5], op0=ALU.mult, op1=ALU.subtract),
          reads=[lsb], writes=[lsb])
        E("act", lambda e: e.activation(out=ls.t[:, k, 6:7], in_=ls.t[:, k, 5:6], func=AF.Ln, bias=g.cst.t[:, 1:2], scale=1.0),
          reads=[lsb, g.cst], writes=[lsb])
        E("act", lambda e: e.activation(out=ls.t[:, k, 7:8], in_=ls.t[:, k, 6:7], func=AF.Exp, scale=-0.5),
          reads=[lsb], writes=[lsb])
        E("dve", lambda e: e.scalar_tensor_tensor(out=ls.t[:, k, 8:9], in0=ls.t[:, k, 3:4], scalar=-1.0,
                                                  in1=ls.t[:, k, 7:8], op0=ALU.mult, op1=ALU.mult),
          reads=[lsb], writes=[lsb])
        E("act", lambda e: e.activation(out=row, in_=row, func=AF.Identity, bias=ls.t[:, k, 8:9], scale=ls.t[:, k, 7:8]),
          reads=xb + [lsb], writes=xb)
        E("dve", lambda e: e.tensor_tensor(out=row, in0=row, in1=g.lnp.t[:, 2 * gi, :], op=ALU.mult),
          reads=xb + [g.lnp, g.lnpB], writes=xb)
        E("pool", lambda e: e.tensor_tensor(out=xf.t[:, b, 0:512], in0=xf.t[:, b, 0:512], in1=g.lnp.t[:, 2 * gi + 1, 0:512], op=ALU.add),
          reads=[g.xfB[b], g.lnp, g.lnpB], writes=[g.xfB[b]])
        E("dve", lambda e: e.tensor_tensor(out=xf.t[:, b, 512:1024], in0=xf.t[:, b, 512:1024], in1=g.lnp.t[:, 2 * gi + 1, 512:1024], op=ALU.add),
          reads=[g.xfH[b], g.lnp, g.lnpB], writes=[g.xfH[b]])

    def run_group(kind, xsrc, nb, vr_off=0, want_kv=True, ndt_first=None, save_kv=False, ydst=None, fence_i=None):
        W = nb * 128
        smp = kind == "smp"
        pre = kind == "pre"
        rst = g.rowc.t[:, 2, :] if smp else g.rowc.t[:, 0, :]
        rsn = g.rowc.t[:, 3, :] if smp else g.rowc.t[:, 1, :]
        mT = g.mT_smp if smp else g.mT_std

        for b in range(nb):
            DMA("sp", lambda e, s, b=b: e.dma_start(out=xf.t[:, b, :], in_=xsrc[b * 128:(b + 1) * 128, :]).then_inc(s, 16),
                1, writes=g.XB(b), ds=g.xfB[b].ds)
        if pre:
            def ldv(e, s):
                e.dma_start(out=g.vrow.t[:, 0:W], in_=IA["vrow"][:, vr_off:vr_off + W]).then_inc(s, 16)
            DMA("sp", ldv, 1, writes=[g.vrow], ds=g.vrow.ds)
        for c in range(8):
            bk = bank()
            for b in range(nb):
                E("pe", lambda e, bk=bk, b=b, c=c: e.transpose(out=bk.t[:, b * 128:(b + 1) * 128],
                                                               in_=xf.t[:, b, c * 128:(c + 1) * 128], identity=identf.t[:]),
                  reads=g.XB(b) + [identf], writes=[bk])
            E("act", lambda e, bk=bk, c=c: e.copy(out=actT.t[:, c, 0:W], in_=bk.t[:, 0:W]), reads=[bk], writes=[actT])
            release(bk)

        slot = g.load_cols512(g.w_in_r, OFF_C)
        sv = v3(slot.t, 0, 8, 512)
        for h in range(4):
            bk = bank()
            for kc in range(8):
                E("pe", lambda e, bk=bk, kc=kc, h=h, sv=sv: e.matmul(bk.t[:, 0:W], lhsT=sv[:, kc, h * 128:(h + 1) * 128],
                                                                    rhs=actT.t[:, kc, 0:W], start=(kc == 0), stop=(kc == 7)),
                  reads=[slot, actT], writes=[bk])
            if smp:
                E("dve", lambda e, bk=bk, h=h: e.tensor_copy(out=bass.AP(cT.t, h * 176 + 3, [[ps_cT, 128], [11, 16], [1, 8]]),
                                                             in_=bk.t[:, 0:128].rearrange("p (j l) -> p j l", j=16)),
                  reads=[bk], writes=[cT])
                E("dve", lambda e, bk=bk, h=h: e.tensor_copy(out=g.csm.t[:, h, :], in_=bk.t[:, 0:128]), reads=[bk], writes=[g.csm])
            else:
                E("dve", lambda e, bk=bk, h=h: e.tensor_copy(out=cT.t[:, h * 515 + 3:h * 515 + 3 + W], in_=bk.t[:, 0:W]),
                  reads=[bk], writes=[cT])
            release(bk)
        slot = g.load_kvif()
        sv = v3(slot.t, 0, 8, 264)
        if want_kv:
            bk = bank()
            for kc in range(8):
                E("pe", lambda e, bk=bk, kc=kc, sv=sv: e.matmul(bk.t[:, 0:W], lhsT=sv[:, kc, 0:128], rhs=actT.t[:, kc, 0:W],
                                                               start=(kc == 0), stop=(kc == 7)), reads=[slot, actT], writes=[bk])
            E("dve", lambda e, bk=bk: e.tensor_copy(out=kT.t[:, 128:128 + W], in_=bk.t[:, 0:W]), reads=[bk], writes=[kT])
            release(bk)
        bki = bank()
        bkf = bank()
        for kc in range(8):
            E("pe", lambda e, kc=kc, sv=sv, bki=bki: e.matmul(bki.t[0:4, 0:W], lhsT=sv[:, kc, 256:260], rhs=actT.t[:, kc, 0:W],
                                                             start=(kc == 0), stop=(kc == 7)), reads=[slot, actT], writes=[bki])
        for kc in range(8):
            E("pe", lambda e, kc=kc, sv=sv, bkf=bkf: e.matmul(bkf.t[0:4, 0:W], lhsT=sv[:, kc, 260:264], rhs=actT.t[:, kc, 0:W],
                                                             start=(kc == 0), stop=(kc == 7)), reads=[slot, actT], writes=[bkf])
        E("act", lambda e: e.activation(out=g.logiall.t[:, 0:W], in_=bki.t[0:4, 0:W], func=AF.Identity, bias=g.bif.t[:, 0:1], scale=1.0),
          reads=[bki, g.bif], writes=[g.logiall])
        E("act", lambda e: e.activation(out=g.fall.t[:, 0:W], in_=bkf.t[0:4, 0:W], func=AF.Identity, bias=g.bif.t[:, 1:2], scale=1.0),
          reads=[bkf, g.bif], writes=[g.fall])
        release(bki, bkf)
        if want_kv:
            for b in range(nb):
                bk = bank()
                for kc in range(8):
                    E("pe", lambda e, bk=bk, kc=kc, b=b, sv=sv: e.matmul(bk.t[:, 0:256], lhsT=actT.t[:, kc, b * 128:(b + 1) * 128],
                                                                        rhs=sv[:, kc, 0:256], start=(kc == 0), stop=(kc == 7)),
                      reads=[slot, actT], writes=[bk])
                E("dve", lambda e, bk=bk, b=b: e.tensor_copy(out=g.v_bf.t[:, b + 1, :], in_=bk.t[:, 128:256]), reads=[bk], writes=[g.v_bf])
                if save_kv and b == nb - 1:
                    E("dve", lambda e, bk=bk: e.tensor_copy(out=g.kvf.t[:], in_=bk.t[:, 0:256]), reads=[bk], writes=[g.kvf])
                release(bk)
        fall, logiall, T2, Bl, NU, RPK, msall = g.fall, g.logiall, g.T2all, g.Ball, g.NEGUall, g.RPKall, g.msall
        cl, dB = g.colsall, g.decBall

        def _proj_lane():
            if not pre:
                slot = g.load_q()
                sv = v3(slot.t, 0, 8, 512)
                for c in range(4):
                    bk = bank()
                    for kc in range(8):
                        E("pe", lambda e, bk=bk, kc=kc, c=c, sv=sv: e.matmul(bk.t[:, 0:W], lhsT=sv[:, kc, c * 128:(c + 1) * 128],
                                                                            rhs=actT.t[:, kc, 0:W], start=(kc == 0), stop=(kc == 7)),
                          reads=[slot, actT], writes=[bk])
                    E("act", lambda e, bk=bk, c=c: e.activation(out=qT.t[:, c, 0:W], in_=bk.t[:, 0:W], func=AF.Identity,
                                                                scale=g.qscale.t[:, c:c + 1]),
                      reads=[bk, g.qscale], writes=[qT])
                    release(bk)
                    yield
            slot = g.load_cols512(g.w_in_r, OFF_VM, [g.fences[fence_i]] if fence_i is not None else [])
            sv = v3(slot.t, 0, 8, 512)
            for b in range(nb):
                bk = bank()
                for kc in range(8):
                    E("pe", lambda e, bk=bk, kc=kc, b=b, sv=sv: e.matmul(bk.t[:, 0:512], lhsT=actT.t[:, kc, b * 128:(b + 1) * 128],
                                                                        rhs=sv[:, kc, :], start=(kc == 0), stop=(kc == 7)),
                      reads=[slot, actT], writes=[bk])
                eng = "act" if b % 2 == 0 else "dve"
                if eng == "act":
                    E("act", lambda e, bk=bk, b=b: e.copy(out=g.vm_ext.t[:, b, :, 0:128], in_=bk.t[:, 0:512].rearrange("p (h d) -> p h d", h=4)),
                      reads=[bk], writes=[g.vm_ext])
                else:
                    E("dve", lambda e, bk=bk, b=b: e.tensor_copy(out=g.vm_ext.t[:, b, :, 0:128], in_=bk.t[:, 0:512].rearrange("p (h d) -> p h d", h=4)),
                      reads=[bk], writes=[g.vm_ext])
                release(bk)
                yield
            if not pre:
                slot = g.load_cols512(g.w_in_r, OFF_O)
                sv = v3(slot.t, 0, 8, 512)
                for b in range(nb):
                    bk = bank()
                    for kc in range(8):
                        E("pe", lambda e, bk=bk, kc=kc, b=b, sv=sv: e.matmul(bk.t[:, 0:512], lhsT=actT.t[:, kc, b * 128:(b + 1) * 128],
                                                                            rhs=sv[:, kc, :], start=(kc == 0), stop=(kc == 7)),
                          reads=[slot, actT], writes=[bk])
                    wt = (g.att, g.hm)[b % 2]
                    E("act", lambda e, bk=bk, wt=wt: e.activation(out=wt.t[:], in_=bk.t[:, 0:512], func=AF.Tanh, scale=0.5),
                      reads=[bk], writes=[wt])
                    E("dve", lambda e, wt=wt, b=b: e.tensor_scalar(out=g.osig.t[:, b, :], in0=wt.t[:], scalar1=0.5, scalar2=0.5,
                                                                  op0=ALU.mult, op1=ALU.add), reads=[wt], writes=[g.osig])
                    release(bk)
                    yield

            yield

        def _conv_lane():
            def csrc(h, j):
                if smp:
                    return bass.AP(cT.t, h * 176 + j, [[ps_cT, 128], [11, 16], [1, 8]])
                return cT.t[:, h * 515 + j:h * 515 + j + W]

            def tv(buf):
                if smp:
                    return buf.t[:, 0:128].rearrange("p (j l) -> p j l", j=16)
                return buf.t[:, 0:W]

            for h in range(4):
                acc = g.wtmp[h % 2]
                th = g.wtmp2[h % 2]
                E("act", lambda e, h=h, acc=acc: e.activation(out=tv(acc), in_=csrc(h, 3), func=AF.Identity,
                                                              bias=g.bconv.t[:, h:h + 1], scale=g.wconv.t[:, h, 3:4]),
                  reads=[cT, g.bconv, g.wconv], writes=[acc])
                yield
                for j in range(3):
                    E("dve", lambda e, h=h, j=j, acc=acc: e.scalar_tensor_tensor(out=tv(acc), in0=csrc(h, j), scalar=g.wconv.t[:, h, j:j + 1],
                                                                                in1=tv(acc), op0=ALU.mult, op1=ALU.add),
                      reads=[cT, g.wconv, acc], writes=[acc])
                    yield
                E("act", lambda e, acc=acc, th=th: e.activation(out=th.t[:, 0:W], in_=acc.t[:, 0:W], func=AF.Tanh, scale=0.5),
                  reads=[acc], writes=[th])
                yield
                E("dve", lambda e, h=h, acc=acc, th=th: e.scalar_tensor_tensor(out=cact.t[:, h, 0:W], in0=th.t[:, 0:W], scalar=1.0,
                                                                              in1=acc.t[:, 0:W], op0=ALU.add, op1=ALU.mult),
                  reads=[th, acc], writes=[cact])
                yield

            if not pre:
                for h in range(4):
                    bk = bank()
                    E("pe", lambda e, bk=bk, h=h: e.matmul(bk.t[:, 0:W], lhsT=g.wmq.t[:, h, :], rhs=cact.t[:, h, 0:W], start=True, stop=True),
                      reads=[g.wmq, cact], writes=[bk])
                    E("act", lambda e, bk=bk, h=h: e.activation(out=g.mqT.t[:, h, 0:W], in_=bk.t[:, 0:W], func=AF.Identity, scale=0.5),
                      reads=[bk], writes=[g.mqT])
                    release(bk)
                    yield
                    bk = bank()
                    E("pe", lambda e, bk=bk, h=h: e.matmul(bk.t[:, 0:W], lhsT=g.wmk.t[:, h, :], rhs=cact.t[:, h, 0:W], start=True, stop=True),
                      reads=[g.wmk, cact], writes=[bk])
                    E("dve", lambda e, bk=bk, h=h: e.tensor_scalar(out=g.mkT.t[:, h, 0:W], in0=bk.t[:, 0:W], scalar1=0.5 * KSCALE, scalar2=None,
                                                                  op0=ALU.mult), reads=[bk], writes=[g.mkT])
                    release(bk)
                    yield

            yield

        def _rows_lane():
            fall, logiall, T2, Bl, NU, RPK, msall = g.fall, g.logiall, g.T2all, g.Ball, g.NEGUall, g.RPKall, g.msall
            cl, dB = g.colsall, g.decBall
            E("act", lambda e: e.activation(out=T2.t[:, 0:W], in_=fall.t[:, 0:W], func=AF.Abs), reads=[fall], writes=[T2])
            E("act", lambda e: e.activation(out=T2.t[:, 0:W], in_=T2.t[:, 0:W], func=AF.Exp, scale=-1.0), reads=[T2], writes=[T2])
            E("act", lambda e: e.activation(out=T2.t[:, 0:W], in_=T2.t[:, 0:W], func=AF.Ln, bias=g.cst.t[0:4, 0:1], scale=1.0),
              reads=[T2, g.cst], writes=[T2])
            yield
            E("dve", lambda e: e.scalar_tensor_tensor(out=fall.t[:, 0:W], in0=fall.t[:, 0:W], scalar=0.0, in1=T2.t[:, 0:W],
                                                      op0=ALU.min, op1=ALU.subtract), reads=[fall, T2], writes=[fall])
            yield
            if pre:
                E("dve", lambda e: e.tensor_tensor(out=fall.t[:, 0:W], in0=fall.t[:, 0:W], in1=g.vrow.t[:, 0:W], op=ALU.mult),
                  reads=[fall, g.vrow], writes=[fall])
                yield
                E("dve", lambda e: e.tensor_tensor(out=logiall.t[:, 0:W], in0=logiall.t[:, 0:W], in1=g.vrow.t[:, 0:W], op=ALU.mult),
                  reads=[logiall, g.vrow], writes=[logiall])
                yield
                E("dve", lambda e: e.tensor_scalar(out=T2.t[:, 0:W], in0=g.vrow.t[:, 0:W], scalar1=30000.0, scalar2=-30000.0,
                                                   op0=ALU.mult, op1=ALU.add), reads=[g.vrow, T2], writes=[T2])
                yield
                E("dve", lambda e: e.tensor_tensor(out=logiall.t[:, 0:W], in0=logiall.t[:, 0:W], in1=T2.t[:, 0:W], op=ALU.add),
                  reads=[logiall, T2], writes=[logiall])
                yield
            for b in range(nb):
                sl = slice(b * 128, (b + 1) * 128)
                E("dve", lambda e, sl=sl: e.tensor_tensor_scan(out=Bl.t[:, sl], data0=rst, data1=fall.t[:, sl], initial=0.0,
                                                               op0=ALU.mult, op1=ALU.add), reads=[fall, g.rowc], writes=[Bl])
                yield
            E("dve", lambda e: e.tensor_tensor(out=RPK.t[:, 0, 0:W], in0=logiall.t[:, 0:W], in1=Bl.t[:, 0:W], op=ALU.subtract),
              reads=[logiall, Bl], writes=[RPK])
            for b in range(nb):
                sl = slice(b * 128, (b + 1) * 128)
                E("dve", lambda e, sl=sl: e.tensor_tensor_scan(out=T2.t[:, sl], data0=rsn, data1=RPK.t[:, 0, sl], initial=-1.0e30,
                                                               op0=ALU.add, op1=ALU.max), reads=[RPK, g.rowc], writes=[T2])
                yield
            for b in range(nb):
                sl = slice(b * 128, (b + 1) * 128)
                if smp:
                    E("dve", lambda e: e.tensor_tensor(out=NU.t[:, 0:128].rearrange("p (j l) -> p j l", j=16),
                                                       in0=T2.t[:, 0:128].rearrange("p (j l) -> p j l", j=16),
                                                       in1=g.ms16.t[:].unsqueeze(2).to_broadcast([4, 16, 8]), op=ALU.max),
                      reads=[T2, g.ms16], writes=[NU])
                    yield
                    E("dve", lambda e: e.tensor_scalar(out=NU.t[:, 0:128], in0=NU.t[:, 0:128], scalar1=-1.0, scalar2=None, op0=ALU.mult),
                      reads=[NU], writes=[NU])
                    yield
                else:
                    E("dve", lambda e, sl=sl, b=b: e.tensor_scalar(out=NU.t[:, sl], in0=T2.t[:, sl], scalar1=msall.t[:, b:b + 1], scalar2=-1.0,
                                                                  op0=ALU.max, op1=ALU.mult), reads=[T2, msall], writes=[NU])
                    yield
                E("dve", lambda e, sl=sl: e.tensor_tensor(out=RPK.t[:, 4, sl], in0=Bl.t[:, sl], in1=NU.t[:, sl], op=ALU.subtract),
                  reads=[Bl, NU], writes=[RPK])
                if not smp:
                    E("dve", lambda e, b=b: e.tensor_copy(out=msall.t[:, b + 1:b + 2], in_=RPK.t[:, 4, b * 128 + 127:b * 128 + 128]),
                      reads=[RPK], writes=[msall])
                    yield
            for b in range(nb):
                sl = slice(b * 128, (b + 1) * 128)
                if smp:
                    E("dve", lambda e: e.tensor_tensor(out=g.RTMP.t[:].rearrange("p (j l) -> p j l", j=16),
                                                       in0=NU.t[:, 0:128].rearrange("p (j l) -> p j l", j=16),
                                                       in1=g.ms16.t[:].unsqueeze(2).to_broadcast([4, 16, 8]), op=ALU.add),
                      reads=[NU, g.ms16], writes=[g.RTMP])
                    yield
                    E("act", lambda e: e.activation(out=RPK.t[:, 1, 0:128], in_=g.RTMP.t[:], func=AF.Exp), reads=[g.RTMP], writes=[RPK])
                    E("dve", lambda e: e.tensor_tensor(out=g.RTMP.t[:].rearrange("p (j l) -> p j l", j=16),
                                                       in0=RPK.t[:, 0, 0:128].rearrange("p (j l) -> p j l", j=16),
                                                       in1=NU.t[:, 0:128].rearrange("p (j l) -> p j l", j=16)[:, :, 7:8].to_broadcast([4, 16, 8]),
                                                       op=ALU.add), reads=[RPK, NU], writes=[g.RTMP])
                    yield
                    E("act", lambda e: e.activation(out=RPK.t[:, 3, 0:128], in_=g.RTMP.t[:], func=AF.Exp), reads=[g.RTMP], writes=[RPK])
                else:
                    E("act", lambda e, sl=sl, b=b: e.activation(out=RPK.t[:, 1, sl], in_=NU.t[:, sl], func=AF.Exp, bias=msall.t[:, b:b + 1], scale=1.0),
                      reads=[NU, msall], writes=[RPK])
                    E("act", lambda e, sl=sl, b=b: e.activation(out=RPK.t[:, 3, sl], in_=RPK.t[:, 0, sl], func=AF.Exp,
                                                                bias=NU.t[:, b * 128 + 127:b * 128 + 128], scale=1.0),
                      reads=[RPK, NU], writes=[RPK])
            E("act", lambda e: e.activation(out=RPK.t[:, 2, 0:W], in_=RPK.t[:, 4, 0:W], func=AF.Exp, scale=-1.0), reads=[RPK], writes=[RPK])
            yield

        lanes = [_rows_lane(), _conv_lane()]
        if not pre:
            lanes.insert(0, _proj_lane())
        else:
            lanes.insert(0, _proj_lane())
        while lanes:
            for ln in list(lanes):
                try:
                    next(ln)
                except StopIteration:
                    lanes.remove(ln)

        bk = bank()
        for b in range(nb):
            for q in range(5):
                E("pe", lambda e, bk=bk, q=q, b=b: e.transpose(out=bk.t[:, b * 20 + q * 4:b * 20 + (q + 1) * 4], in_=RPK.t[:, q, b * 128:(b + 1) * 128],
                                                              identity=identf.t[0:4, 0:4]), reads=[RPK, identf], writes=[bk])
        E("dve", lambda e, bk=bk: e.tensor_copy(out=cl.t[:, 0:nb, :], in_=bk.t[:, 0:nb * 20].rearrange("p (b c) -> p b c", b=nb)),
          reads=[bk], writes=[cl])
        release(bk)
        if smp:
            wl = RPK.t[:, 1, 0:128].rearrange("p (j l) -> p j l", j=16)[:, :, 7:8].rearrange("p j o -> p o j")
            E("dve", lambda e, wl=wl: e.tensor_tensor(out=g.dexp.t[:].rearrange("p (h j) -> p h j", h=4),
                                                     in0=wl.to_broadcast([4, 4, 16]),
                                                     in1=g.i4.t[:].unsqueeze(2).to_broadcast([4, 4, 16]), op=ALU.mult),
              reads=[RPK, g.i4], writes=[g.dexp])
            nd_ = 64
        else:
            for b in range(nb):
                E("dve", lambda e, b=b: e.tensor_scalar(out=g.dexp.t[:, b * 4:(b + 1) * 4], in0=g.i4.t[:], scalar1=RPK.t[:, 1, b * 128 + 127:b * 128 + 128],
                                                       scalar2=None, op0=ALU.mult), reads=[RPK, g.i4], writes=[g.dexp])
            nd_ = nb * 4
        bk = bank()
        E("pe", lambda e, bk=bk, nd_=nd_: e.matmul(bk.t[:, 0:nd_], lhsT=g.ones4.t[:], rhs=g.dexp.t[:, 0:nd_], start=True, stop=True),
          reads=[g.ones4, g.dexp], writes=[bk])
        E("dve", lambda e, bk=bk, nd_=nd_: e.tensor_copy(out=dB.t[:, 0:nd_], in_=bk.t[:, 0:nd_]), reads=[bk], writes=[dB])
        release(bk)
        if not smp:
            E("dve", lambda e: e.tensor_copy(out=msall.t[:, 0:1], in_=msall.t[:, nb:nb + 1]), reads=[msall], writes=[msall])

        lanes = []
        if not pre:
            lanes.append(_attention_lane(g, kind, nb, ndt_first, hcount))
        lanes.append(_mlstm_lane(g, kind, nb, mT))
        while lanes:
            for ln in list(lanes):
                try:
                    next(ln)
                except StopIteration:
                    lanes.remove(ln)

        if not smp:
            for h in range(4):
                E("act", lambda e, h=h: e.copy(out=cT.t[:, h * 515:h * 515 + 3], in_=cT.t[:, h * 515 + W:h * 515 + W + 3]),
                  reads=[cT], writes=[cT])
            if want_kv:
                E("act", lambda e: e.copy(out=kT.t[:, 0:128], in_=kT.t[:, W:W + 128]), reads=[kT], writes=[kT])
                E("act", lambda e: e.copy(out=g.v_bf.t[:, 0, :], in_=g.v_bf.t[:, nb, :]), reads=[g.v_bf], writes=[g.v_bf])
        if pre:
            return
        if smp:
            def reload_ln1(e, s):
                for i_, nm in enumerate(("ln1_g", "ln1_b")):
                    e.dma_start(out=g.lnp.t[:, i_, :], in_=IA[nm].to_broadcast([128, 1024])).then_inc(s, 16)

            def reload_ln2(e, s):
                for i_, nm in enumerate(("ln2_g", "ln2_b")):
                    e.dma_start(out=g.lnp.t[:, 2 + i_, :], in_=IA[nm].to_broadcast([128, 1024])).then_inc(s, 16)
            DMA("sp", reload_ln1, 2, writes=[g.lnp], ds=g.lnp.ds)
            DMA("sp", reload_ln2, 2, writes=[g.lnpB], ds=g.lnpB.ds)
        _dense_tail(g, kind, nb, ydst, layernorm)

    g.run_group = run_group
    g.headnorm = headnorm
    _schedule(g)


HS_ORDER = [0, 4, 1, 5, 2, 6, 3, 7]


def _acquire(g, n):
    while len(g.free_banks) < n:
        yield
    return [g.bank() for _ in range(n)]


def _attention_lane(g, kind, nb, ndt_first, hcount):
    E, release = g.E, g.release
    smp = kind == "smp"
    qT, kT = g.qT, g.kT
    for b in range(nb):
        ndt = ndt_first if (b == 0 and ndt_first is not None) else (g.nd_smp if smp else g.nd_std)
        st = g.stat[b % 2]
        sl = slice(b * 128, (b + 1) * 128)
        (bO,) = yield from _acquire(g, 1)

        def s1(h, bk, hp, b=b, sl=sl, st=st, ndt=ndt):
            c = h % 4
            base = (h // 4) * 64
            pe_ = g.pexp[hp]
            E("pe", lambda e: e.matmul(bk.t[:, 0:256], lhsT=g.identb.t[:], rhs=ndt.t[:], start=True, stop=False),
              reads=[g.identb, ndt], writes=[bk])
            if smp:
                if h < 4:
                    E("dve", lambda e: e.tensor_copy(out=g.ZA.dg, in_=qT.t[:, c, 0:128].rearrange("p (j l) -> p j l", j=16)),
                      reads=[qT], writes=[g.ZA])
                for j in range(16):
                    E("pe", lambda e, j=j: e.matmul(bk.t[:, 0:128], lhsT=g.ZA.t[base:base + 64, j * 128:(j + 1) * 128],
                                                   rhs=g.cKT.t[base:base + 64, j, :], start=False, stop=(j == 15)),
                      reads=[g.ZA, g.cKT], writes=[bk])
                E("pe", lambda e: e.matmul(bk.t[:, 128:256], lhsT=qT.t[base:base + 64, c, 0:128], rhs=kT.t[base:base + 64, 128:256],
                                           start=False, stop=True), reads=[qT, kT], writes=[bk])
            else:
                E("pe", lambda e: e.matmul(bk.t[:, 0:256], lhsT=qT.t[base:base + 64, c, sl], rhs=kT.t[base:base + 64, b * 128:b * 128 + 256],
                                           start=False, stop=True), reads=[qT, kT], writes=[bk])
            E("dve", lambda e: e.reduce_max(out=st.t[:, 0, h:h + 1], in_=bk.t[:, 0:256], axis=AX.X), reads=[bk], writes=[st])
            E("dve", lambda e: e.tensor_scalar(out=st.t[:, 1, h:h + 1], in0=st.t[:, 0, h:h + 1], scalar1=g.sinkq.t[:, h:h + 1], scalar2=-SLOPES[h],
                                               op0=ALU.max, op1=ALU.mult), reads=[st, g.sinkq], writes=[st])
            E("act", lambda e: e.activation(out=pe_.t[:], in_=bk.t[:, 0:256], func=AF.Exp, bias=st.t[:, 1, h:h + 1], scale=SLOPES[h],
                                            accum_out=st.t[:, 2, h:h + 1]), reads=[bk, st], writes=[pe_, st])

        def s2a(h, bk, hp):
            pe_, pt_ = g.pexp[hp], g.PT[hp]
            bPb = bk.t[:].bitcast(BF16)[:, 512:768]
            E("pe", lambda e: e.transpose(out=bPb[:, 0:128], in_=pe_.t[:, 0:128], identity=g.identb.t[:]), reads=[pe_, g.identb], writes=[bk])
            E("pe", lambda e: e.transpose(out=bPb[:, 128:256], in_=pe_.t[:, 128:256], identity=g.identb.t[:]), reads=[pe_, g.identb], writes=[bk])
            if smp:
                E("act", lambda e: e.copy(out=g.PZ.dg, in_=bPb[:, 0:128].rearrange("p (j l) -> p j l", j=16)), reads=[bk], writes=[g.PZ])
                E("act", lambda e: e.copy(out=pt_.t[:, 128:256], in_=bPb[:, 128:256]), reads=[bk], writes=[pt_])
            else:
                E("act", lambda e: e.copy(out=pt_.t[:], in_=bPb[:, 0:256]), reads=[bk], writes=[pt_])
            release(bk)

        def s2b(h, bk, hp, b=b, bO=bO):
            kvh = h // 4
            pt_ = g.PT[hp]
            ocol = bO.t[:, h * 64:(h + 1) * 64]
            if smp:
                for j in range(16):
                    E("pe", lambda e, j=j: e.matmul(ocol, lhsT=g.PZ.t[:, j * 128:(j + 1) * 128], rhs=g.cVb.t[:, j, kvh * 64:(kvh + 1) * 64],
                                                   start=(j == 0), stop=False), reads=[g.PZ, g.cVb], writes=[bO])
                E("pe", lambda e: e.matmul(ocol, lhsT=pt_.t[:, 128:256], rhs=g.v_bf.t[:, 1, kvh * 64:(kvh + 1) * 64], start=False, stop=True),
                  reads=[pt_, g.v_bf], writes=[bO])
            else:
                E("pe", lambda e: e.matmul(ocol, lhsT=pt_.t[:, 0:128], rhs=g.v_bf.t[:, b, kvh * 64:(kvh + 1) * 64], start=True, stop=False),
                  reads=[pt_, g.v_bf], writes=[bO])
                E("pe", lambda e: e.matmul(ocol, lhsT=pt_.t[:, 128:256], rhs=g.v_bf.t[:, b + 1, kvh * 64:(kvh + 1) * 64], start=False, stop=True),
                  reads=[pt_, g.v_bf], writes=[bO])

        heads = []
        nsteps = len(HS_ORDER) + 4
        for i in range(nsteps):
            if i - 4 >= 0 and i - 4 < len(heads):
                s2b(*heads[i - 4])
            if i - 3 >= 0 and i - 3 < len(HS_ORDER):
                s2a(*heads[i - 3])
            if i < len(HS_ORDER):
                (bk,) = yield from _acquire(g, 1)
                hp = hcount[0] % 3
                hcount[0] += 1
                heads.append((HS_ORDER[i], bk, hp))
                s1(*heads[i])
            yield
        E("dve", lambda e, st=st: e.tensor_tensor(out=st.t[:, 3, :], in0=g.sinkb.t[:], in1=st.t[:, 1, :], op=ALU.add), reads=[st, g.sinkb], writes=[st])
        E("act", lambda e, st=st: e.activation(out=st.t[:, 3, :], in_=st.t[:, 3, :], func=AF.Exp), reads=[st], writes=[st])
        yield
        E("dve", lambda e, st=st: e.tensor_tensor(out=st.t[:, 4, :], in0=st.t[:, 2, :], in1=st.t[:, 3, :], op=ALU.add), reads=[st], writes=[st])
        E("dve", lambda e, st=st: e.reciprocal(out=st.t[:, 5, :], in_=st.t[:, 4, :]), reads=[st], writes=[st])
        yield
        E("dve", lambda e, st=st, bO=bO: e.tensor_tensor(out=g.att.t[:].rearrange("p (h d) -> p h d", h=8),
                                                        in0=bO.t[:, 0:512].rearrange("p (h d) -> p h d", h=8),
                                                        in1=st.t[:, 5, :].unsqueeze(2).to_broadcast([128, 8, 64]), op=ALU.mult),
          reads=[bO, st], writes=[g.att])
        release(bO)
        yield
        yield from g.headnorm(g.att, 8, 64, g.ybuf.t[:, b, 0:512], g.sq, g.hn_a)
        yield


def _mlstm_lane(g, kind, nb, mT):
    pre = kind == "pre"
    for b in range(nb):
        if not pre:
            yield from _mlstm_out(g, kind, b, mT)
        yield from _mlstm_state(g, kind, b, nb)


def _mlstm_out(g, kind, b, mT):
    E, DMA, release = g.E, g.DMA, g.release
    smp = kind == "smp"
    sl = slice(b * 128, (b + 1) * 128)
    mqT, mkT = g.mqT, g.mkT
    cl = g.colsall
    clb = cl.t[:, b, :]
    bST, bE = yield from _acquire(g, 2)
    for h in range(4):
        hs = slice(h * 128, (h + 1) * 128)
        E("pe", lambda e, h=h, hs=hs: e.matmul(bST.t[:, hs], lhsT=mkT.t[:, h, sl], rhs=mqT.t[:, h, sl], start=True, stop=True),
          reads=[mkT, mqT], writes=[bST])
        E("pe", lambda e, h=h, hs=hs: e.matmul(bE.t[:, hs], lhsT=g.selrows.t[:, hs], rhs=g.NEGUall.t[:, sl], start=True, stop=False),
          reads=[g.selrows, g.NEGUall], writes=[bE])
        E("pe", lambda e, h=h, hs=hs: e.matmul(bE.t[:, hs], lhsT=g.identb.t[:], rhs=mT.t[:], start=False, stop=True),
          reads=[g.identb, mT], writes=[bE])
    yield
    yield
    for h in range(4):
        hs = slice(h * 128, (h + 1) * 128)
        E("act", lambda e, h=h, hs=hs: e.activation(out=g.DTt.t[:, hs], in_=bE.t[:, hs], func=AF.Exp, bias=clb[:, h:h + 1], scale=1.0),
          reads=[bE, cl], writes=[g.DTt])
        if h % 2 == 1:
            yield
    release(bE)
    E("dve", lambda e: e.tensor_tensor(out=g.scT.t[:], in0=bST.t[:, 0:512], in1=g.DTt.t[:], op=ALU.mult), reads=[bST, g.DTt], writes=[g.scT])
    release(bST)
    yield
    yield
    for k in range(2):
        bIk, bJk = yield from _acquire(g, 2)
        for h in (2 * k, 2 * k + 1):
            hs = slice(h * 128, (h + 1) * 128)
            oc = slice((h % 2) * 129, (h % 2) * 129 + 129)
            E("pe", lambda e, h=h, hs=hs, oc=oc, bIk=bIk: e.matmul(bIk.t[:, oc], lhsT=g.scT.t[:, hs], rhs=g.vm_ext.t[:, b, h, :], start=True, stop=True),
              reads=[g.scT, g.vm_ext], writes=[bIk])
            if smp:
                cn = g.Cnat_view(h % 2)
                cb = g.cnb[h % 2]
                DMA("sp", lambda e, s, cn=cn, h=h: e.dma_start(out=cn, in_=g.IA["sC"][:, h].rearrange("j v k -> v j k")).then_inc(s, 16),
                    1, writes=[cb], ds=cb.ds)
                for jq in range(4):
                    (bk,) = yield from _acquire(g, 1)
                    for jj in range(4):
                        j = jq * 4 + jj
                        E("pe", lambda e, bk=bk, jj=jj, j=j, cn=cn: e.transpose(out=bk.t[:, jj * 128:(jj + 1) * 128], in_=cn[:, j, :], identity=g.identf.t[:]),
                          reads=[cb, g.identf], writes=[bk])
                    if jq % 2 == 0:
                        E("act", lambda e, bk=bk, jq=jq: e.copy(out=g.CTb.t[:, jq * 4:(jq + 1) * 4, 0:128], in_=bk.t[:, 0:512].rearrange("p (j k) -> p j k", j=4)),
                          reads=[bk], writes=[g.CTb])
                    else:
                        E("dve", lambda e, bk=bk, jq=jq: e.tensor_copy(out=g.CTb.t[:, jq * 4:(jq + 1) * 4, 0:128], in_=bk.t[:, 0:512].rearrange("p (j k) -> p j k", j=4)),
                          reads=[bk], writes=[g.CTb])
                    release(bk)
                    yield
                E("dve", lambda e, h=h: e.tensor_copy(out=g.CTb.t[:, :, 128:129], in_=g.snT.t[:].rearrange("p (j h) -> p j h", h=4)[:, :, h:h + 1]),
                  reads=[g.snT], writes=[g.CTb])
                E("dve", lambda e, h=h: e.tensor_copy(out=g.ZM.dg, in_=mqT.t[:, h, 0:128].rearrange("p (j l) -> p j l", j=16)),
                  reads=[mqT], writes=[g.ZM])
                yield
                for j in range(16):
                    E("pe", lambda e, h=h, j=j, oc=oc, bJk=bJk: e.matmul(bJk.t[:, oc], lhsT=g.ZM.t[:, j * 128:(j + 1) * 128], rhs=g.CTb.t[:, j, :],
                                                                        start=(j == 0), stop=(j == 15)), reads=[g.ZM, g.CTb], writes=[bJk])
            else:
                E("pe", lambda e, h=h, oc=oc, bJk=bJk: e.matmul(bJk.t[:, oc], lhsT=mqT.t[:, h, sl], rhs=g.Cbf.t[:, h, :], start=True, stop=True),
                  reads=[mqT, g.Cbf], writes=[bJk])
        yield
        yield
        for h in (2 * k, 2 * k + 1):
            oc = slice((h % 2) * 129, (h % 2) * 129 + 129)
            E("act", lambda e, h=h, oc=oc, bJk=bJk: e.activation(out=g.intsb.t[:, h, :], in_=bJk.t[:, oc], func=AF.Identity, scale=clb[:, 4 + h:5 + h]),
              reads=[bJk, cl], writes=[g.intsb])
        yield
        yield
        E("dve", lambda e, k=k, bIk=bIk: e.tensor_tensor(out=g.tot.t[:, 2 * k:2 * k + 2, :], in0=g.intsb.t[:, 2 * k:2 * k + 2, :],
                                                        in1=bIk.t[:, 0:258].rearrange("p (h d) -> p h d", h=2), op=ALU.add),
          reads=[g.intsb, bIk], writes=[g.tot])
        release(bIk, bJk)
        yield
    mst = g.mst
    E("act", lambda e: e.activation(out=mst.t[:, 0, :], in_=g.tot.t[:, :, 128], func=AF.Abs), reads=[g.tot], writes=[mst])
    E("dve", lambda e: e.tensor_tensor(out=mst.t[:, 1, :], in0=mst.t[:, 0, :], in1=clb[:, 8:12], op=ALU.max), reads=[mst, cl], writes=[mst])
    E("dve", lambda e: e.reciprocal(out=mst.t[:, 2, :], in_=mst.t[:, 1, :]), reads=[mst], writes=[mst])
    yield
    E("pool", lambda e: e.tensor_tensor(out=g.hm.t[:].rearrange("p (h d) -> p h d", h=4), in0=g.tot.t[:, :, 0:128],
                                        in1=mst.t[:, 2, :].unsqueeze(2).to_broadcast([128, 4, 128]), op=ALU.mult),
      reads=[g.tot, mst], writes=[g.hm])
    yield
    E("pool", lambda e: e.tensor_tensor(out=g.hm.t[:], in0=g.hm.t[:], in1=g.osig.t[:, b, :], op=ALU.mult), reads=[g.hm, g.osig], writes=[g.hm])
    yield
    yield from g.headnorm(g.hm, 4, 128, g.ybuf.t[:, b, 512:1024], g.xc, g.hn_m)
    yield


def _mlstm_state(g, kind, b, nb):
    E, DMA, release = g.E, g.DMA, g.release
    smp = kind == "smp"
    pre = kind == "pre"
    sl = slice(b * 128, (b + 1) * 128)
    mw = g.mkw[b % 2]
    cl = g.colsall
    clb = cl.t[:, b, :]
    dB = g.decBall
    (bK,) = yield from _acquire(g, 1)
    for h in range(4):
        hs = slice(h * 128, (h + 1) * 128)
        E("pe", lambda e, h=h, hs=hs: e.matmul(bK.t[:, hs], lhsT=g.cact.t[:, h, sl], rhs=g.wmk.t[:, h, :], start=True, stop=True),
          reads=[g.cact, g.wmk], writes=[bK])
    yield
    yield
    for h in range(4):
        hs = slice(h * 128, (h + 1) * 128)
        E("dve", lambda e, h=h, hs=hs: e.tensor_scalar(out=mw.t[:, h, :], in0=bK.t[:, hs], scalar1=clb[:, 12 + h:13 + h], scalar2=0.5 * KSCALE,
                                                      op0=ALU.mult, op1=ALU.mult), reads=[bK, cl], writes=[mw])
        if h % 2 == 1:
            yield
    release(bK)
    if not smp:
        bD0, bD1 = yield from _acquire(g, 2)
        bD = [bD0, bD1]
        for h in range(4):
            oc = slice((h % 2) * 129, (h % 2) * 129 + 129)
            E("pe", lambda e, h=h, oc=oc: e.matmul(bD[h // 2].t[:, oc], lhsT=mw.t[:, h, :], rhs=g.vm_ext.t[:, b, h, :], start=True, stop=True),
              reads=[mw, g.vm_ext], writes=[bD[h // 2]])
        yield
        yield
        for h in range(4):
            oc = slice((h % 2) * 129, (h % 2) * 129 + 129)
            E("dve", lambda e, h=h, oc=oc: e.scalar_tensor_tensor(out=g.Cext.t[:, h, :], in0=g.Cext.t[:, h, :], scalar=dB.t[:, b * 4 + h:b * 4 + h + 1],
                                                                 in1=bD[h // 2].t[:, oc], op0=ALU.mult, op1=ALU.add),
              reads=[g.Cext, dB, bD[h // 2]], writes=[g.Cext])
            if h % 2 == 1:
                yield
        release(bD[0], bD[1])
        if (not pre) or b == nb - 1:
            E("pool", lambda e: e.tensor_copy(out=g.Cbf.t[:], in_=g.Cext.t[:]), reads=[g.Cext], writes=[g.Cbf])
        yield
        return
    (bN,) = yield from _acquire(g, 1)
    for h in range(4):
        hs = slice(h * 128, (h + 1) * 128)
        cn = g.Cnat_view(h % 2)
        cb = g.cnb[h % 2]
        DMA("sp", lambda e, s, cn=cn, h=h: e.dma_start(out=cn, in_=g.IA["sC"][:, h].rearrange("j v k -> v j k")).then_inc(s, 16),
            1, writes=[cb], ds=cb.ds)
        E("dve", lambda e, h=h: e.tensor_tensor(out=g.VZ.t[:], in0=g.vm_ext.t[:, 0, h, 0:128].unsqueeze(1).to_broadcast([128, 16, 128]),
                                               in1=g.m16f.t[:].unsqueeze(2).to_broadcast([128, 16, 128]), op=ALU.mult),
          reads=[g.vm_ext, g.m16f], writes=[g.VZ])
        yield
        for jq in range(4):
            (bk,) = yield from _acquire(g, 1)
            for jj in range(4):
                j = jq * 4 + jj
                E("pe", lambda e, bk=bk, jj=jj, j=j, h=h: e.matmul(bk.t[:, jj * 128:(jj + 1) * 128], lhsT=g.VZ.t[:, j, :], rhs=mw.t[:, h, :],
                                                                  start=True, stop=True), reads=[g.VZ, mw], writes=[bk])
            for jj in range(4):
                j = jq * 4 + jj
                E("dve", lambda e, bk=bk, jj=jj, j=j, h=h, cn=cn: e.scalar_tensor_tensor(out=cn[:, j, :], in0=cn[:, j, :],
                                                                                       scalar=dB.t[:, h * 16 + j:h * 16 + j + 1],
                                                                                       in1=bk.t[:, jj * 128:(jj + 1) * 128], op0=ALU.mult, op1=ALU.add),
                  reads=[cb, dB, bk], writes=[cb])
            release(bk)
            yield
        DMA("sp", lambda e, s, cn=cn, h=h: e.dma_start(out=g.OA["sCo"][:, h].rearrange("j v k -> v j k"), in_=cn).then_inc(s, 16),
            1, reads=[cb], ds=cb.ds, is_out=True)
        E("pe", lambda e, h=h, hs=hs: e.matmul(bN.t[0:16, hs], lhsT=g.m16b.t[:], rhs=mw.t[:, h, :], start=True, stop=True),
          reads=[g.m16b, mw], writes=[bN])
        yield
    (bk,) = yield from _acquire(g, 1)
    E("pe", lambda e: e.matmul(bk.t[0:16, 0:4], lhsT=g.sellast.t[:], rhs=clb[:, 4:8], start=True, stop=True), reads=[g.sellast, cl], writes=[bk])
    E("pe", lambda e: e.matmul(bk.t[0:16, 4:8], lhsT=g.sellast.t[:], rhs=clb[:, 16:20], start=True, stop=True), reads=[g.sellast, cl], writes=[bk])
    E("dve", lambda e: e.tensor_copy(out=g.s16.t[:], in_=bk.t[0:16, 0:8]), reads=[bk], writes=[g.s16])
    release(bk)
    yield
    E("dve", lambda e: e.tensor_tensor(out=g.n16.t[:], in0=g.n16.t[:], in1=g.s16.t[:, 0:4].unsqueeze(2).to_broadcast([16, 4, 128]), op=ALU.mult),
      reads=[g.n16, g.s16], writes=[g.n16])
    E("dve", lambda e: e.tensor_tensor(out=g.n16.t[:], in0=g.n16.t[:], in1=bN.t[0:16, 0:512].rearrange("p (h k) -> p h k", h=4), op=ALU.add),
      reads=[g.n16, bN], writes=[g.n16])
    release(bN)
    DMA("sp", lambda e, s: e.dma_start(out=g.OA["sno"], in_=g.n16.t[:]).then_inc(s, 16), 1, reads=[g.n16], ds=g.n16.ds, is_out=True)
    DMA("sp", lambda e, s: e.dma_start(out=g.OA["smo"], in_=g.s16.t[:, 4:8]).then_inc(s, 16), 1, reads=[g.s16], ds=g.s16.ds, is_out=True)
    yield


def _dense_tail(g, kind, nb, ydst, layernorm):
    E, DMA, bank, release, v3 = g.E, g.DMA, g.bank, g.release, g.v3
    xf, actT = g.xf, g.actT
    W = nb * 128
    ls = g.lnst
    for c in range(8):
        bk = bank()
        bkb = bk.t[:].bitcast(BF16)
        for b in range(nb):
            E("pe", lambda e, bkb=bkb, b=b, c=c: e.transpose(out=bkb[:, b * 128:(b + 1) * 128], in_=g.ybuf.t[:, b, c * 128:(c + 1) * 128],
                                                            identity=g.identb.t[:]), reads=[g.ybuf, g.identb], writes=[bk])
        if c % 2 == 0:
            E("act", lambda e, bkb=bkb, c=c: e.activation(out=actT.t[:, c, 0:W], in_=bkb[:, 0:W], func=AF.Identity, scale=g.gcol.t[:, c:c + 1]),
              reads=[bk, g.gcol], writes=[actT])
        else:
            E("dve", lambda e, bkb=bkb, c=c: e.tensor_scalar(out=actT.t[:, c, 0:W], in0=bkb[:, 0:W], scalar1=g.gcol.t[:, c:c + 1], scalar2=None,
                                                            op0=ALU.mult), reads=[bk, g.gcol], writes=[actT])
        release(bk)
    slots = [g.load_cols512(g.w_out_r, half * 512) for half in range(2)]
    svs = [v3(sl_.t, 0, 8, 512) for sl_ in slots]
    for b in range(nb):
        for half in range(2):
            slot, sv = slots[half], svs[half]
            bk = bank()
            for kc in range(8):
                E("pe", lambda e, bk=bk, kc=kc, b=b, sv=sv: e.matmul(bk.t[:, 0:512], lhsT=actT.t[:, kc, b * 128:(b + 1) * 128], rhs=sv[:, kc, :],
                                                                    start=(kc == 0), stop=(kc == 7)), reads=[slot, actT], writes=[bk])
            hsl = slice(half * 512, (half + 1) * 512)
            E("dve", lambda e, bk=bk, b=b, hsl=hsl: e.scalar_tensor_tensor(out=xf.t[:, b, hsl], in0=xf.t[:, b, hsl], scalar=ALPHA, in1=bk.t[:, 0:512],
                                                                          op0=ALU.mult, op1=ALU.add), reads=g.XB(b) + [bk], writes=g.XB(b))
            release(bk)
        layernorm(b, 0, b)
    for c in range(8):
        bk = bank()
        for b in range(nb):
            E("pe", lambda e, bk=bk, b=b, c=c: e.transpose(out=bk.t[:, b * 128:(b + 1) * 128], in_=xf.t[:, b, c * 128:(c + 1) * 128],
                                                          identity=g.identf.t[:]), reads=g.XB(b) + [g.identf], writes=[bk])
        if c % 2 == 0:
            E("act", lambda e, bk=bk, c=c: e.copy(out=actT.t[:, c, 0:W], in_=bk.t[:, 0:W]), reads=[bk], writes=[actT])
        else:
            E("dve", lambda e, bk=bk, c=c: e.tensor_copy(out=actT.t[:, c, 0:W], in_=bk.t[:, 0:W]), reads=[bk], writes=[actT])
        release(bk)
    for jj in range(11):
        slot = g.load_gu(jj)
        gv = v3(slot.t, 0, 8, 256)
        uv = v3(slot.t, 2048, 8, 256)
        for k2 in range(2):
            j = jj * 2 + k2
            bG = bank()
            bU = bank()
            for kc in range(8):
                E("pe", lambda e, bG=bG, kc=kc, k2=k2, gv=gv: e.matmul(bG.t[:, 0:W], lhsT=gv[:, kc, k2 * 128:(k2 + 1) * 128], rhs=actT.t[:, kc, 0:W],
                                                                      start=(kc == 0), stop=(kc == 7)), reads=[slot, actT], writes=[bG])
            for kc in range(8):
                E("pe", lambda e, bU=bU, kc=kc, k2=k2, uv=uv: e.matmul(bU.t[:, 0:W], lhsT=uv[:, kc, k2 * 128:(k2 + 1) * 128], rhs=actT.t[:, kc, 0:W],
                                                                      start=(kc == 0), stop=(kc == 7)), reads=[slot, actT], writes=[bU])
            th = g.wtmp[j % 2]
            ga = g.wtmp2[j % 2]
            E("act", lambda e, bG=bG, th=th: e.activation(out=th.t[:, 0:W], in_=bG.t[:, 0:W], func=AF.Tanh, scale=0.5), reads=[bG], writes=[th])
            E("dve", lambda e, bG=bG, th=th, ga=ga: e.scalar_tensor_tensor(out=ga.t[:, 0:W], in0=th.t[:, 0:W], scalar=1.0, in1=bG.t[:, 0:W],
                                                                          op0=ALU.add, op1=ALU.mult), reads=[th, bG], writes=[ga])
            E("dve", lambda e, bU=bU, ga=ga, j=j: e.scalar_tensor_tensor(out=g.h1T.t[:, j, 0:W], in0=ga.t[:, 0:W], scalar=0.5, in1=bU.t[:, 0:W],
                                                                        op0=ALU.mult, op1=ALU.mult), reads=[ga, bU], writes=[g.h1T])
            release(bG, bU)
    bks = [bank() for _ in range(2 * nb)]
    for p in range(6):
        nj = 4 if p < 5 else 2
        slot = g.load_wd_rows(p, nj)
        wv = v3(slot.t, 0, nj, 1024)
        for jj in range(nj):
            j = p * 4 + jj
            for b in range(nb):
                for half in range(2):
                    bk = bks[b * 2 + half]
                    E("pe", lambda e, bk=bk, b=b, j=j, jj=jj, half=half, wv=wv: e.matmul(bk.t[:, 0:512], lhsT=g.h1T.t[:, j, b * 128:(b + 1) * 128],
                                                                                        rhs=wv[:, jj, half * 512:(half + 1) * 512],
                                                                                        start=(j == 0), stop=(j == 21)),
                      reads=[slot, g.h1T], writes=[bk])
    for b in range(nb):
        for half in range(2):
            bk = bks[b * 2 + half]
            hsl = slice(half * 512, (half + 1) * 512)
            xh = [g.xfB[b]] if half == 0 else [g.xfH[b]]
            E("dve", lambda e, bk=bk, b=b, hsl=hsl: e.scalar_tensor_tensor(out=xf.t[:, b, hsl], in0=xf.t[:, b, hsl], scalar=ALPHA, in1=bk.t[:, 0:512],
                                                                          op0=ALU.mult, op1=ALU.add), reads=xh + [bk], writes=xh)
            release(bk)
    for b in range(nb):
        layernorm(b, 1, b)
        DMA("sp", lambda e, s, b=b: e.dma_start(out=ydst[b * 128:(b + 1) * 128, :], in_=xf.t[:, b, :]).then_inc(s, 16),
            1, reads=g.XB(b), ds=g.xfB[b].ds, is_out=True)


def _schedule(g):
    E, DMA, bank, release = g.E, g.DMA, g.bank, g.release
    IA, OA, I, O = g.IA, g.OA, g.I, g.O
    cT = g.cT
    ps_cT = cT.t[:].ap[0][0]
    KSTOP = int(os.environ.get("KSTOP", "99"))
    off = 0
    for gi, nbp in enumerate([4, 4, 4, 4, 1]):
        if KSTOP < 1 or (KSTOP == 1 and gi > 0):
            break
        g.run_group("pre", IA["xpre"][off * 128:(off + nbp) * 128, :], nbp, vr_off=off * 128, want_kv=(gi == 4), fence_i=gi)
        if gi < 4:
            g.convert_gu(gi)
        off += nbp
    if KSTOP < 3:
        return
    for gi in range(4):
        if KSTOP == 3 and gi > 0:
            break
        if 30 <= KSTOP < 40 and gi >= KSTOP - 30:
            break
        g.run_group("full", IA["xs"][gi * 512:(gi + 1) * 512, :], 4, ndt_first=(g.nd_first if gi == 0 else None),
                    save_kv=(gi == 3 and os.environ.get("NOSAVE") is None), ydst=OA["y"][gi * 512:(gi + 1) * 512, :])
    if 30 <= KSTOP < 40:
        return
    DMA("sp", lambda e, s: (e.dma_start(out=OA["pk"], in_=g.kvf.t[:, 0:128]).then_inc(s, 16),
                            e.dma_start(out=OA["pv"], in_=g.kvf.t[:, 128:256]).then_inc(s, 16)),
        2, reads=[g.kvf], ds=g.kvf.ds, is_out=True)
    bk = bank()
    for h in range(4):
        E("pe", lambda e, bk=bk, h=h: e.transpose(out=bk.t[:, h * 128:(h + 1) * 128], in_=cT.t[:, h * 515 + 3 + 384:h * 515 + 3 + 512],
                                                 identity=g.identf.t[:]), reads=[cT, g.identf], writes=[bk])
    E("dve", lambda e, bk=bk: e.tensor_copy(out=g.ctok.t[:], in_=bk.t[:, 0:512]), reads=[bk], writes=[g.ctok])
    release(bk)
    DMA("sp", lambda e, s: e.dma_start(out=OA["pconv"], in_=g.ctok.t[125:128, :]).then_inc(s, 16), 1, reads=[g.ctok], ds=g.ctok.ds, is_out=True)
    bk = bank()
    for h in range(4):
        E("pe", lambda e, bk=bk, h=h: e.transpose(out=bk.t[:, h * 128:(h + 1) * 128], in_=g.Cext.t[:, h, 0:128], identity=g.identf.t[:]),
          reads=[g.Cext, g.identf], writes=[bk])
    E("dve", lambda e, bk=bk: e.tensor_copy(out=g.cfin.t[:], in_=bk.t[:, 0:512].rearrange("p (h k) -> p h k", h=4)), reads=[bk], writes=[g.cfin])
    release(bk)
    DMA("sp", lambda e, s: e.dma_start(out=OA["pC"].rearrange("h v k -> v h k"), in_=g.cfin.t[:]).then_inc(s, 16), 1,
        reads=[g.cfin], ds=g.cfin.ds, is_out=True)
    DMA("sp", lambda e, s: (e.dma_start(out=bass.AP(O["pn"], 0, [[1, 128], [128, 4]]), in_=g.Cext.t[:, :, 128],
                                        allow_slow_non_contiguous=True).then_inc(s, 16),),
        1, reads=[g.Cext], ds=g.Cext.ds, is_out=True)
    DMA("sp", lambda e, s: e.dma_start(out=OA["pm"], in_=g.msall.t[:, 0:1]).then_inc(s, 16), 1, reads=[g.msall], ds=g.msall.ds, is_out=True)

    if KSTOP < 5:
        return
    def ldc(e, s):
        e.dma_start(out=g.ckb.t[:], in_=IA["ck"].rearrange("j k f -> k j f")).then_inc(s, 16)
    DMA("pool", ldc, 1, writes=[g.ckb], ds=g.ckb.ds)
    DMA("pool", lambda e, s: e.dma_start(out=g.cVb.t[:], in_=IA["cv"].rearrange("j k f -> k j f")).then_inc(s, 16), 1, writes=[g.cVb], ds=g.cVb.ds)
    for j8 in range(2):
        bk = bank()
        bkb = bk.t[:].bitcast(BF16)
        for jj in range(8):
            j = j8 * 8 + jj
            E("pe", lambda e, bkb=bkb, jj=jj, j=j: e.transpose(out=bkb[:, jj * 128:(jj + 1) * 128], in_=g.ckb.t[:, j, :], identity=g.identb.t[:]),
              reads=[g.ckb, g.identb], writes=[bk])
        E("act", lambda e, bkb=bkb, j8=j8: e.copy(out=g.cKT.t[:, j8 * 8:(j8 + 1) * 8, :], in_=bkb[:, 0:1024].rearrange("p (j k) -> p j k", j=8)),
          reads=[bk], writes=[g.cKT])
        release(bk)
    d2d = DSem(g.new_sem("d_d2d"))
    DMA("sp", lambda e, s: (e.dma_start(out=OA["sk"][:, 0:120, :], in_=IA["ck"][:, 8:128, :]).then_inc(s, 16),
                            e.dma_start(out=OA["sv"][:, 0:120, :], in_=IA["cv"][:, 8:128, :]).then_inc(s, 16)),
        2, ds=d2d, is_out=True)

    def lds(e, s):
        e.dma_start(out=g.sc48.t[:], in_=IA["scv"]).then_inc(s, 16)
    DMA("sp", lds, 1, writes=[g.sc48], ds=g.sc48.ds)
    DMA("sp", lambda e, s: e.dma_start(out=g.ms16.t[:], in_=IA["smm"].rearrange("j h -> h j"), allow_slow_non_contiguous=True).then_inc(s, 16),
        1, writes=[g.ms16], ds=g.ms16.ds)
    DMA("sp", lambda e, s: e.dma_start(out=g.snat.t[:], in_=IA["sn"]).then_inc(s, 16), 1, writes=[g.snat], ds=g.snat.ds)
    DMA("sp", lambda e, s: e.dma_start(out=g.n16.t[:], in_=IA["sn"].rearrange("(j h) k -> j h k", h=4)).then_inc(s, 16), 1, writes=[g.n16], ds=g.n16.ds)
    bk = bank()
    for h in range(4):
        E("pe", lambda e, bk=bk, h=h: e.transpose(out=bk.t[:, h * 48:(h + 1) * 48], in_=g.sc48.t[:, h * 128:(h + 1) * 128],
                                                 identity=g.identf.t[0:48, 0:48]), reads=[g.sc48, g.identf], writes=[bk])
    for h in range(4):
        E("dve", lambda e, bk=bk, h=h: e.tensor_copy(out=bass.AP(cT.t, h * 176, [[ps_cT, 128], [11, 16], [1, 3]]),
                                                    in_=bk.t[:, h * 48:(h + 1) * 48].rearrange("p (j r) -> p j r", j=16)),
          reads=[bk], writes=[cT])
    release(bk)
    bk = bank()
    E("pe", lambda e, bk=bk: e.transpose(out=bk.t[:, 0:64], in_=g.snat.t[:], identity=g.identf.t[0:64, 0:64]), reads=[g.snat, g.identf], writes=[bk])
    E("dve", lambda e, bk=bk: e.tensor_copy(out=g.snT.t[:], in_=bk.t[:, 0:64]), reads=[bk], writes=[g.snT])
    release(bk)

    if KSTOP == 5:
        return
    E("dve", lambda e: e.memset(g.ZA.t[:], 0.0), writes=[g.ZA])
    E("dve", lambda e: e.memset(g.PZ.t[:], 0.0), writes=[g.PZ])
    g.run_group("smp", IA["xsm"], 1, save_kv=True, ydst=OA["ys"])
    if KSTOP == 6:
        return

    ps_kv = g.kvf.t[:].ap[0][0]

    def st_kv(e, s):
        for l in range(8):
            e.dma_start(out=OA["sk"][:, 120 + l, :], in_=bass.AP(g.kvf.t, l * ps_kv, [[8 * ps_kv, 16], [1, 128]])).then_inc(s, 16)
            e.dma_start(out=OA["sv"][:, 120 + l, :], in_=bass.AP(g.kvf.t, l * ps_kv + 128, [[8 * ps_kv, 16], [1, 128]])).then_inc(s, 16)
    DMA("sp", st_kv, 16, reads=[g.kvf], ds=g.kvf.ds, is_out=True)
    bk = bank()
    for h in range(4):
        E("pe", lambda e, bk=bk, h=h: e.transpose(out=bk.t[:, h * 128:(h + 1) * 128], in_=g.csm.t[:, h, :], identity=g.identf.t[:]),
          reads=[g.csm, g.identf], writes=[bk])
    E("dve", lambda e, bk=bk: e.tensor_copy(out=g.ctok.t[:], in_=bk.t[:, 0:512]), reads=[bk], writes=[g.ctok])
    release(bk)
    ps_ct = g.ctok.t[:].ap[0][0]

    def st_cv(e, s):
        for r in range(3):
            e.dma_start(out=OA["sconv"][:, r, :], in_=bass.AP(g.ctok.t, (5 + r) * ps_ct, [[8 * ps_ct, 16], [1, 512]])).then_inc(s, 16)
    DMA("sp", st_cv, 3, reads=[g.ctok], ds=g.ctok.ds, is_out=True)


_NC_CACHE = {}


def kernel(x_prompt, x_sample, cache_k, cache_v, state_conv, state_C, state_n, state_m, meta_tokens,
           w_in, w_conv, b_conv, w_mq, w_mk, b_i, b_f, attn_sinks, g_attn, g_mlstm, w_out,
           ln1_g, ln1_b, w_gate, w_up, w_down, ln2_g, ln2_b):
    f = lambda a: np.ascontiguousarray(np.asarray(a), dtype=np.float32)
    x_prompt, x_sample, cache_k, cache_v = f(x_prompt), f(x_sample), f(cache_k), f(cache_v)
    state_conv, state_C, state_n, state_m, meta_tokens = f(state_conv), f(state_C), f(state_n), f(state_m), f(meta_tokens)
    tabs = _tables()
    shared = {
        "w_in": f(w_in)[0], "w_conv": f(w_conv)[0], "b_conv": f(b_conv)[0].reshape(1, 512), "w_mq": f(w_mq)[0], "w_mk": f(w_mk)[0],
        "b_i": f(b_i)[0].reshape(4, 1), "b_f": f(b_f)[0].reshape(4, 1), "sinks": f(attn_sinks)[0].reshape(1, 8),
        "g_attn": f(g_attn)[0].reshape(1, 512), "g_mlstm": f(g_mlstm)[0].reshape(1, 512), "w_out": f(w_out)[0],
        "ln1_g": f(ln1_g)[0].reshape(1, D), "ln1_b": f(ln1_b)[0].reshape(1, D), "w_gate": f(w_gate)[0], "w_up": f(w_up)[0],
        "w_down": f(w_down)[0], "ln2_g": f(ln2_g)[0].reshape(1, D), "ln2_b": f(ln2_b)[0].reshape(1, D),
    }
    for n in TABLE_SHAPES:
        shared[n] = tabs[n]
    blk0 = np.zeros((128, D), np.float32)
    blk0[112:128] = meta_tokens
    in_maps = []
    for core in range(8):
        bq, half = core // 2, core % 2
        m = dict(shared)
        m["xs"] = np.ascontiguousarray(x_prompt[bq, half * 2048:(half + 1) * 2048])
        vr = np.zeros((4, NPRE * 128), np.float32)
        if half == 0:
            xpre = np.zeros((NPRE * 128, D), np.float32)
            xpre[16 * 128:] = blk0
            vr[:, 16 * 128 + 112:] = 1.0
            m["nd_first"] = tabs["nd_meta"]
        else:
            xpre = np.concatenate([blk0, x_prompt[bq, 0:2048]], axis=0)
            vr[:, 112:] = 1.0
            m["nd_first"] = tabs["nd_std"]
        m["xpre"] = np.ascontiguousarray(xpre)
        m["vrow"] = vr
        sq_ = slice(core * 16, (core + 1) * 16)
        m["xsm"] = np.ascontiguousarray(x_sample[sq_].reshape(128, D))
        m["ck"] = np.ascontiguousarray(cache_k[0, sq_].reshape(16, 128, 128))
        m["cv"] = np.ascontiguousarray(cache_v[0, sq_].reshape(16, 128, 128))
        m["scv"] = np.ascontiguousarray(state_conv[0, sq_].reshape(48, 512))
        m["sC"] = np.ascontiguousarray(state_C[0, sq_])
        m["sn"] = np.ascontiguousarray(state_n[0, sq_].reshape(64, 128))
        m["smm"] = np.ascontiguousarray(state_m[0, sq_])
        in_maps.append(m)
    if "nc" not in _NC_CACHE:
        _NC_CACHE["nc"] = build_program()
    nc = _NC_CACHE["nc"]
    res = run_bass_kernel_spmd(nc, in_maps, core_ids=list(range(8)))
    R = res.results
    y_prompt = np.stack([np.concatenate([R[2 * b]["y"], R[2 * b + 1]["y"]], axis=0) for b in range(4)], axis=0)
    y_sample = np.concatenate([R[c]["ys"].reshape(16, 8, D) for c in range(8)], axis=0)
    B = [R[2 * b + 1] for b in range(4)]
    pk = np.stack([r["pk"].reshape(128, 2, 64) for r in B])[None]
    pv = np.stack([r["pv"].reshape(128, 2, 64) for r in B])[None]
    pconv = np.stack([r["pconv"] for r in B])[None]
    pC = np.stack([r["pC"] for r in B])[None]
    pn = np.stack([r["pn"] for r in B])[None]
    pm = np.stack([r["pm"].reshape(4) for r in B])[None]
    sk = np.concatenate([R[c]["sk"].reshape(16, 128, 2, 64) for c in range(8)], axis=0)[None]
    sv = np.concatenate([R[c]["sv"].reshape(16, 128, 2, 64) for c in range(8)], axis=0)[None]
    sconv = np.concatenate([R[c]["sconv"] for c in range(8)], axis=0)[None]
    sC = np.concatenate([R[c]["sCo"] for c in range(8)], axis=0)[None]
    sn = np.concatenate([R[c]["sno"] for c in range(8)], axis=0)[None]
    sm = np.concatenate([R[c]["smo"] for c in range(8)], axis=0)[None]
    outs = (y_prompt, y_sample, pk, pv, pconv, pC, pn, pm, sk, sv, sconv, sC, sn, sm)
    return tuple(np.ascontiguousarray(o, dtype=np.float32) for o in outs)
```

```python
import contextlib
import os
import numpy as np
import ml_dtypes
import concourse.bass as bass
import concourse.mybir as mybir
from concourse.bass_utils import run_bass_kernel_spmd

F32 = mybir.dt.float32
BF16 = mybir.dt.bfloat16
AF = mybir.ActivationFunctionType
ALU = mybir.AluOpType
AX = mybir.AxisListType

ENGS = ("pe", "act", "dve", "pool", "sp")

D = 1024
KC = 8
T = 128
IN_W = 2312
OFF_Q, OFF_K, OFF_V, OFF_C, OFF_VM, OFF_O, OFF_I, OFF_F = 0, 512, 640, 768, 1280, 1792, 2304, 2308
DFF = 2816
NJ = 22
ALPHA = float(2.0 ** 0.25)
EPS = 1e-5
NEGM = -30000.0
NPRE = 17
NFULL = 16
SLOPES = [2.0 ** (-(h + 1)) for h in range(8)]
KSCALE = float(128.0 ** -0.5)


class TT:
    __slots__ = ("name", "lw", "rd", "rd_dma")

    def __init__(self, name):
        self.name = name
        self.lw = None
        self.rd = []
        self.rd_dma = []


class DSem:
    __slots__ = ("h", "count")

    def __init__(self, h):
        self.h = h
        self.count = 0


class Op:
    __slots__ = ("eng", "fn", "deps", "odeps", "signal", "sem", "semval", "is_dma", "ndma", "idx")


class Prog:
    def __init__(self, nc):
        self.nc = nc
        self.ops = {e: [] for e in ENGS}
        self.out_dmas = []
        self.nops = 0

    def emit(self, eng, fn, reads=(), writes=(), dsem=None, ndma=0, is_out=False):
        op = Op()
        op.eng = eng
        op.fn = fn
        op.signal = False
        op.is_dma = dsem is not None
        op.ndma = ndma
        op.sem = None
        op.semval = 0
        raw = set()
        oth = set()
        for t in reads:
            if t.lw is not None:
                raw.add(t.lw)
        for t in writes:
            if t.lw is not None:
                oth.add(t.lw)
            for r in t.rd:
                oth.add(r)
            for r in t.rd_dma:
                oth.add(r)
        deps = []
        odeps = []
        for d in raw | oth:
            if (not d.is_dma) and (not op.is_dma) and d.eng == eng and eng == "pe":
                odeps.append(d)
                continue
            deps.append(d)
            d.signal = True
        op.deps = deps
        op.odeps = odeps
        op.idx = self.nops
        self.nops += 1
        if op.is_dma:
            dsem.count += 16 * ndma
            op.sem = dsem
            op.semval = dsem.count
        for t in reads:
            if op.is_dma:
                t.rd_dma.append(op)
            else:
                t.rd.append(op)
        for t in writes:
            t.lw = op
            t.rd = []
            t.rd_dma = []
        self.ops[eng].append(op)
        if is_out:
            self.out_dmas.append(op)
        return op

    def schedule(self):
        if os.environ.get("KSCHED", "1") == "0":
            return
        allops = [o for e in ENGS for o in self.ops[e]]
        allops.sort(key=lambda o: o.idx)

        class _Ret:
            def then_inc(self, *a, **k):
                return self

        class _Mock:
            def __init__(self):
                self.n = 0
                self.f32 = False
                self.nd = 0

            def __getattr__(self, name):
                def f(*a, **k):
                    for cand in (k.get("out", a[0] if a else None), k.get("in_", None), k.get("in0", None)):
                        try:
                            n = 1
                            for v in cand.shape[1:]:
                                n *= int(v)
                            self.n = max(self.n, n)
                        except Exception:
                            pass
                    src = k.get("lhsT", k.get("in_", None))
                    try:
                        if name in ("matmul", "transpose") and src is not None and src.dtype == F32:
                            self.f32 = True
                    except Exception:
                        pass
                    self.nd += 1
                    return _Ret()
                return f

        cost = {}
        dsize = {}
        for o in allops:
            m = _Mock()
            try:
                if o.is_dma:
                    o.fn(m, None)
                else:
                    o.fn(m)
            except Exception:
                pass
            n = m.n if m.n else 128
            if o.is_dma:
                c = 0.15 * max(1, o.ndma)
                dsize[id(o)] = n * max(1, o.ndma)
            elif o.eng == "pe":
                c = 0.035 + n / 2400.0 * (4.0 if m.f32 else 1.0)
            elif o.eng == "act":
                c = 0.25 + n * 0.00085
            elif o.eng == "dve":
                c = 0.17 + n * 0.0011
            else:
                c = 0.35 + n * 0.0022
            cost[id(o)] = c
        succ = {}
        indeg = {}
        for o in allops:
            ds = set(o.deps) | set(o.odeps)
            indeg[id(o)] = len(ds)
            for d in ds:
                succ.setdefault(id(d), []).append(o)
        finish = {}
        ready_at = {id(o): 0.0 for o in allops}
        free = {e: 0.0 for e in ENGS}
        newq = {e: [] for e in ENGS}
        ready = [o for o in allops if indeg[id(o)] == 0]
        nleft = len(allops)
        while nleft:
            best = None
            bkey = None
            for o in ready:
                st = max(ready_at[id(o)], free[o.eng])
                key = (st, o.idx)
                if bkey is None or key < bkey:
                    best, bkey = o, key
            o = best
            ready.remove(o)
            st = bkey[0]
            if o.is_dma:
                free[o.eng] = st + cost[id(o)]
                fin_t = st + 2.5 + dsize.get(id(o), 128) * 0.0017
            else:
                free[o.eng] = st + cost[id(o)]
                fin_t = free[o.eng]
            finish[id(o)] = fin_t
            newq[o.eng].append(o)
            nleft -= 1
            for sct in succ.get(id(o), ()):
                lat = 0.05 if (sct.eng == o.eng and not o.is_dma) else 0.2
                t = fin_t + lat
                if t > ready_at[id(sct)]:
                    ready_at[id(sct)] = t
                indeg[id(sct)] -= 1
                if indeg[id(sct)] == 0:
                    ready.append(sct)
        self.ops = newq

    def lower(self, esems):
        nc = self.nc
        self.schedule()
        fin = Op()
        fin.eng = "sp"
        fin.fn = None
        fin.signal = False
        fin.is_dma = False
        fin.ndma = 0
        fin.deps = list(self.out_dmas)
        fin.odeps = []
        fin.idx = self.nops
        fin.sem = None
        fin.semval = 0
        self.ops["sp"].append(fin)
        for e in ENGS:
            c = 0
            for op in self.ops[e]:
                if op.is_dma:
                    continue
                if op.signal:
                    c += 1
                    op.semval = c
                    op.sem = esems[e]

        def replay(ename, e):
            waited = {}
            for op in self.ops[ename]:
                need = {}
                for d in op.deps:
                    h = d.sem.h if d.is_dma else d.sem
                    v = d.semval
                    key = id(h)
                    if key not in need or need[key][1] < v:
                        need[key] = (h, v)
                for key, (h, v) in need.items():
                    if waited.get(key, 0) < v:
                        e.wait_ge(h, v)
                        waited[key] = v
                if op.fn is None:
                    continue
                if op.is_dma:
                    op.fn(e, op.sem.h)
                else:
                    ins = op.fn(e)
                    if op.signal:
                        ins.then_inc(op.sem, 1)

        with nc.Block() as block:
            @block.tensor
            def _(e):
                replay("pe", e)

            @block.scalar
            def _(e):
                replay("act", e)

            @block.vector
            def _(e):
                replay("dve", e)

            @block.gpsimd
            def _(e):
                replay("pool", e)

            @block.sync
            def _(e):
                replay("sp", e)


def _tables():
    t = {}
    t["identf"] = np.eye(128, dtype=np.float32)
    BIG = -1.0e6
    i = np.arange(128)[:, None]
    s = np.arange(256)[None, :]
    dist = 128 + i - s
    nd = np.where((dist >= 0) & (dist < 128), -dist.astype(np.float32), BIG).astype(np.float32)
    t["nd_std"] = nd
    ndf = nd.copy()
    ndf[:, 0:112] = BIG
    t["nd_meta"] = ndf
    tt = np.arange(128)
    l = (tt % 8)[:, None]
    j = (tt // 8)[:, None]
    sc = np.arange(128)[None, :]
    dc = 128 + l - sc
    ndc = np.where(dc < 128, -dc.astype(np.float32), BIG)
    lp = (tt % 8)[None, :]
    jp = (tt // 8)[None, :]
    dn = l - lp
    ndn = np.where((j == jp) & (dn >= 0), -dn.astype(np.float32), BIG)
    t["nd_smp"] = np.concatenate([ndc, ndn], axis=1).astype(np.float32)
    ss = np.arange(128)[:, None]
    tq = np.arange(128)[None, :]
    t["mT_std"] = np.where(ss <= tq, 0.0, NEGM).astype(np.float32)
    t["mT_smp"] = np.where((ss // 8 == tq // 8) & (ss <= tq), 0.0, NEGM).astype(np.float32)
    r = np.ones((4, 128), np.float32)
    r[:, 0] = 0.0
    t["rst_p"] = r
    rn = np.zeros((4, 128), np.float32)
    rn[:, 0] = -1.0e30
    t["rsn_p"] = rn
    r = np.ones((4, 128), np.float32)
    r[:, 0::8] = 0.0
    t["rst_s"] = r
    rn = np.zeros((4, 128), np.float32)
    rn[:, 0::8] = -1.0e30
    t["rsn_s"] = rn
    sel = np.zeros((4, 4, 128), np.float32)
    for h in range(4):
        sel[h, h, :] = 1.0
    t["selrows"] = sel.reshape(4, 512)
    t["i4"] = np.eye(4, dtype=np.float32)
    m16 = np.zeros((128, 16), np.float32)
    m16[np.arange(128), np.arange(128) // 8] = 1.0
    t["m16"] = m16
    sl = np.zeros((128, 16), np.float32)
    sl[np.arange(16) * 8 + 7, np.arange(16)] = 1.0
    t["sellast"] = sl
    qs = np.zeros((128, 4), np.float32)
    for c in range(4):
        qs[0:64, c] = 0.125 / SLOPES[c]
        qs[64:128, c] = 0.125 / SLOPES[4 + c]
    t["qscale"] = qs
    t["invslope"] = np.tile(np.array([1.0 / v for v in SLOPES], np.float32)[None, :], (128, 1))
    return t


TABLE_SHAPES = {
    "identf": (128, 128), "nd_std": (128, 256), "nd_smp": (128, 256),
    "mT_std": (128, 128), "mT_smp": (128, 128), "rst_p": (4, 128), "rsn_p": (4, 128),
    "rst_s": (4, 128), "rsn_s": (4, 128), "selrows": (4, 512), "i4": (4, 4), "m16": (128, 16),
    "sellast": (128, 16), "qscale": (128, 4), "invslope": (128, 8),
}

IN_SHAPES = {
    "xs": (NFULL * 128, D), "xpre": (NPRE * 128, D), "vrow": (4, NPRE * 128),
    "nd_first": (128, 256),
    "xsm": (128, D), "ck": (16, 128, 128), "cv": (16, 128, 128), "scv": (48, 512),
    "sC": (16, 4, 128, 128), "sn": (64, 128), "smm": (16, 4),
    "w_in": (D, IN_W), "w_conv": (4, 512), "b_conv": (1, 512), "w_mq": (4, 128, 128), "w_mk": (4, 128, 128),
    "b_i": (4, 1), "b_f": (4, 1), "sinks": (1, 8), "g_attn": (1, 512), "g_mlstm": (1, 512), "w_out": (D, D),
    "ln1_g": (1, D), "ln1_b": (1, D), "w_gate": (D, DFF), "w_up": (D, DFF), "w_down": (DFF, D),
    "ln2_g": (1, D), "ln2_b": (1, D),
}
IN_SHAPES.update(TABLE_SHAPES)

OUT_SHAPES = {
    "y": (NFULL * 128, D), "ys": (128, D), "pk": (128, 128), "pv": (128, 128), "pconv": (3, 512),
    "pC": (4, 128, 128), "pn": (4, 128), "pm": (4, 1),
    "sk": (16, 128, 128), "sv": (16, 128, 128), "sconv": (16, 3, 512), "sCo": (16, 4, 128, 128),
    "sno": (16, 4, 128), "smo": (16, 4),
}


class Buf:
    __slots__ = ("t", "tt", "ds", "dg")

    def __init__(self, t, tt, ds=None, dg=None):
        self.t = t
        self.tt = tt
        self.ds = ds
        self.dg = dg


def build_program():
    nc = bass.Bass("TRN2", target_bir_lowering=False)
    P = Prog(nc)
    es = contextlib.ExitStack()
    with es:
        I = {n: nc.dram_tensor(n, list(s), F32, kind="ExternalInput") for n, s in IN_SHAPES.items()}
        O = {n: nc.dram_tensor(n, list(s), F32, kind="ExternalOutput") for n, s in OUT_SHAPES.items()}
        IA = {n: h.ap() for n, h in I.items()}
        OA = {n: h.ap() for n, h in O.items()}

        def new_sem(name):
            return es.enter_context(nc.semaphore(name))

        esems = {e: new_sem("s_" + e) for e in ENGS}

        def mk(name, shape, dt=F32, dma=False):
            t = es.enter_context(nc.sbuf_tensor("sb_" + name, list(shape), dt))
            return Buf(t, TT(name), DSem(new_sem("d_" + name)) if dma else None)

        def pstride(buf):
            return buf.t[:].ap[0][0]

        pbs = []
        for i in range(8):
            t = es.enter_context(nc.psum_tensor("pb%d" % i, [128, 512], F32))
            pbs.append(Buf(t, TT("pb%d" % i)))
        free_banks = list(pbs)

        def bank():
            assert free_banks, "PSUM banks exhausted"
            return free_banks.pop(0)

        def release(*bks):
            for b in bks:
                assert b not in free_banks
                free_banks.append(b)

        identf = mk("identf", [128, 128])
        identb = mk("identb", [128, 128], BF16)
        nd_std = mk("nd_std", [128, 256], BF16)
        nd_first = mk("nd_first", [128, 256], BF16)
        nd_smp = mk("nd_smp", [128, 256], BF16)
        mT_std = mk("mT_std", [128, 128], BF16)
        mT_smp = mk("mT_smp", [128, 128], BF16)
        invslope = mk("invslope", [128, 8])
        sinkq = mk("sinkq", [128, 8])
        rowc = mk("rowc", [4, 4, 128])
        selrows = mk("selrows", [4, 512])
        i4 = mk("i4", [4, 4])
        ones4 = mk("ones4", [4, 128])
        m16f = mk("m16f", [128, 16])
        m16b = mk("m16b", [128, 16], BF16)
        sellast = mk("sellast", [128, 16])
        qscale = mk("qscale", [128, 4])
        wconv = mk("wconv", [128, 4, 4])
        bconv = mk("bconv", [128, 4])
        bif = mk("bif", [4, 2])
        sinkb = mk("sinkb", [128, 8])
        gcol = mk("gcol", [128, 8])
        lnp = mk("lnp", [128, 4, 1024], F32, dma=True)
        wmq = mk("wmq", [128, 4, 128], BF16)
        wmk = mk("wmk", [128, 4, 128], BF16)
        cst = mk("cst", [128, 4])
        cds = DSem(new_sem("d_const"))
        cds2 = DSem(new_sem("d_const2"))

        def load_consts(e, s):
            def d(out, in_, slow=False):
                if slow:
                    e.dma_start(out=out, in_=in_, allow_slow_non_contiguous=True).then_inc(s, 16)
                else:
                    e.dma_start(out=out, in_=in_).then_inc(s, 16)
            d(identf.t[:], IA["identf"])
            d(invslope.t[:], IA["invslope"])
            d(rowc.t[:, 0, :], IA["rst_p"])
            d(rowc.t[:, 1, :], IA["rsn_p"])
            d(rowc.t[:, 2, :], IA["rst_s"])
            d(rowc.t[:, 3, :], IA["rsn_s"])
            d(selrows.t[:], IA["selrows"])
            d(i4.t[:], IA["i4"])
            d(m16f.t[:], IA["m16"])
            d(sellast.t[:], IA["sellast"])
            d(qscale.t[:], IA["qscale"])
            for j in range(4):
                d(wconv.t[:, :, j], bass.AP(I["w_conv"], j * 512, [[1, 128], [128, 4]]), slow=True)
            d(bconv.t[:], bass.AP(I["b_conv"], 0, [[1, 128], [128, 4]]), slow=True)
            d(bif.t[:, 0:1], IA["b_i"])
            d(bif.t[:, 1:2], IA["b_f"])
            d(sinkb.t[:], IA["sinks"].to_broadcast([128, 8]))
            d(gcol.t[:, 0:4], bass.AP(I["g_attn"], 0, [[1, 128], [128, 4]]), slow=True)
            d(gcol.t[:, 4:8], bass.AP(I["g_mlstm"], 0, [[1, 128], [128, 4]]), slow=True)
            d(lnp.t[:, 0, :], IA["ln1_g"].to_broadcast([128, 1024]))
            d(lnp.t[:, 1, :], IA["ln1_b"].to_broadcast([128, 1024]))
            d(lnp.t[:, 2, :], IA["ln2_g"].to_broadcast([128, 1024]))
            d(lnp.t[:, 3, :], IA["ln2_b"].to_broadcast([128, 1024]))
        NCONST = 25
        const_tts = [b.tt for b in (identf, invslope, rowc, selrows, i4, m16f,
                                    sellast, qscale, wconv, bconv, bif, sinkb, gcol, lnp)]
        P.emit("sp", load_consts, writes=const_tts, dsem=cds, ndma=NCONST)

        def load_consts2(e, s):
            e.dma_start(out=wmq.t[:], in_=IA["w_mq"].rearrange("h d e -> d h e")).then_inc(s, 16)
            e.dma_start(out=wmk.t[:], in_=IA["w_mk"].rearrange("h d e -> d h e")).then_inc(s, 16)
            e.dma_start(out=m16b.t[:], in_=IA["m16"]).then_inc(s, 16)
            e.dma_start(out=nd_std.t[:], in_=IA["nd_std"]).then_inc(s, 16)
            e.dma_start(out=nd_first.t[:], in_=IA["nd_first"]).then_inc(s, 16)
            e.dma_start(out=nd_smp.t[:], in_=IA["nd_smp"]).then_inc(s, 16)
            e.dma_start(out=mT_std.t[:], in_=IA["mT_std"]).then_inc(s, 16)
            e.dma_start(out=mT_smp.t[:], in_=IA["mT_smp"]).then_inc(s, 16)
        P.emit("pool", load_consts2, writes=[wmq.tt, wmk.tt, m16b.tt, nd_std.tt, nd_first.tt, nd_smp.tt, mT_std.tt, mT_smp.tt], dsem=cds2, ndma=8)
        P.emit("dve", lambda e: e.tensor_tensor(out=sinkq.t[:], in0=sinkb.t[:], in1=invslope.t[:], op=ALU.mult),
               reads=[sinkb.tt, invslope.tt], writes=[sinkq.tt])
        P.emit("dve", lambda e: e.tensor_copy(out=identb.t[:], in_=identf.t[:]), reads=[identf.tt], writes=[identb.tt])
        P.emit("dve", lambda e: e.memset(ones4.t[:], 1.0), writes=[ones4.tt])
        P.emit("dve", lambda e: e.memset(cst.t[:, 0:1], 1.0), writes=[cst.tt])
        P.emit("dve", lambda e: e.memset(cst.t[:, 1:2], EPS), writes=[cst.tt])

        xf = mk("xf", [128, 4, 1024], F32, dma=True)
        xfB = [Buf(xf.t, TT("xf_b%d" % i), DSem(new_sem("d_xfb%d" % i))) for i in range(4)]
        xfH = [Buf(xf.t, TT("xf_h%d" % i)) for i in range(4)]

        def XB(b):
            return [xfB[b], xfH[b]]

        def XBall(n):
            return xfB[0:n] + xfH[0:n]
        actT = mk("actT", [128, 8, 512], BF16)
        qT = mk("qT", [128, 4, 512], BF16)
        kT = mk("kT", [128, 640], BF16)
        cT = mk("cT", [128, 4 * 515], F32)
        cact = mk("cact", [128, 4, 512], BF16)
        v_bf = mk("v_bf", [128, 5, 128], BF16)
        vm_ext = mk("vm_ext", [128, 4, 4, 129], BF16)
        osig = mk("osig", [128, 4, 512], BF16)
        kvf = mk("kvf", [128, 256], F32, dma=True)
        mqT = mk("mqT", [128, 4, 512], BF16)
        mkT = mk("mkT", [128, 4, 512], BF16)
        mkw = [mk("mkw%d" % i, [128, 4, 128], BF16) for i in range(2)]
        fall = mk("fall", [4, 512], F32)
        logiall = mk("logiall", [4, 512], F32)
        vrow = mk("vrow", [4, 512], F32, dma=True)
        T2all = mk("T2all", [4, 512], F32)
        Ball = mk("Ball", [4, 512], F32)
        NEGUall = mk("NEGUall", [4, 512], F32)
        RPKall = mk("RPKall", [4, 5, 512], F32)
        RTMP = mk("RTMP", [4, 128])
        msall = mk("msall", [4, 8], F32, dma=True)
        ms16 = mk("ms16", [4, 16], F32, dma=True)
        colsall = mk("colsall", [128, 4, 20])
        decBall = mk("decBall", [128, 64])
        dexp = mk("dexp", [4, 64])
        pexp = [mk("pexp%d" % i, [128, 256], BF16) for i in range(3)]
        PT = [mk("PT%d" % i, [128, 256], BF16) for i in range(3)]
        stat = [mk("stat%d" % i, [128, 6, 8]) for i in range(2)]
        att = mk("att", [128, 512])
        hn_a = mk("hn_a", [128, 4, 8])
        hn_m = mk("hn_m", [128, 4, 8])
        xc = mk("xc", [128, 512])
        sq = mk("sq", [128, 512])
        tmpE = mk("tmpE", [128, 512], F32, dma=True)
        DTt = mk("DTt", [128, 512], F32, dma=True)
        scT = mk("scT", [128, 512], BF16)
        intsb = mk("intsb", [128, 4, 129])
        tot = mk("tot", [128, 4, 129])
        hm = mk("hm", [128, 512], F32, dma=True)
        mst = mk("mst", [128, 3, 4])
        Cext = mk("Cext", [128, 4, 129], F32, dma=True)
        Cbf = mk("Cbf", [128, 4, 129], BF16)
        ybuf = mk("ybuf", [128, 4, 1024], BF16)
        lnst = mk("lnst", [128, 4, 12])
        lnstB = [Buf(lnst.t, TT("lnst_b%d" % i)) for i in range(4)]
        h1T = mk("h1T", [128, 22, 512], BF16, dma=True)
        NS = 3
        ring = [mk("ring%d" % i, [128, 4096], BF16, dma=True) for i in range(NS)]
        ring_i = [0]
        wtmp = [xc, sq]
        wtmp2 = [tmpE, DTt]
        junkB = [Buf(sq.t[:].bitcast(BF16), sq.tt), Buf(xc.t[:].bitcast(BF16), xc.tt)]
        cfin = tmpE
        ctok = DTt
        sc48 = Buf(hm.t[0:48, :], hm.tt, hm.ds)
        xfb = xf.t[:].rearrange("p b d -> p (b d)").bitcast(BF16)
        ckb = Buf(xfb[:, 2048:4096].rearrange("p (j k) -> p j k", j=16), xfB[1].tt, DSem(new_sem("d_ckb")))
        cVb = Buf(xfb[:, 4096:6144].rearrange("p (j k) -> p j k", j=16), xfB[2].tt, DSem(new_sem("d_cvb")))
        cKT = Buf(xfb[:, 6144:8192].rearrange("p (j k) -> p j k", j=16), xfB[3].tt, None)
        ybf = ybuf.t[:].rearrange("p b d -> p (b d)")
        ZA = Buf(ybf[:, 1024:3072], ybuf.tt, None,
                 bass.AP(ybuf.t, 1024, [[ybuf.t[:].ap[0][0], 128], [136, 16], [1, 8]]))
        h1f = h1T.t[:].rearrange("p j t -> p (j t)")
        PZ = Buf(h1f[:, 8192:10240], h1T.tt, None,
                 bass.AP(h1T.t, 8192, [[h1T.t[:].ap[0][0], 128], [136, 16], [1, 8]]))
        ZM = mk("ZM", [128, 16 * 128], BF16)
        ZM.dg = bass.AP(ZM.t, 0, [[ZM.t[:].ap[0][0], 128], [136, 16], [1, 8]])
        VZ = mk("VZ", [128, 16, 128], BF16)
        CTb = Buf(cT.t[:].bitcast(BF16)[:, 2048:4112].rearrange("p (j k) -> p j k", j=16), cT.tt)
        snat = mk("snat", [64, 128], F32, dma=True)
        snT = mk("snT", [128, 64], F32)
        n16 = mk("n16", [16, 4, 128], F32, dma=True)
        s16 = mk("s16", [16, 8], F32, dma=True)
        csm = mk("csm", [128, 4, 128], F32)
        lnp_flat = lnp.t[:].rearrange("p a d -> p (a d)")
        lnpB = Buf(lnp.t, TT("lnpB"), DSem(new_sem("d_lnpB")))
        cnb = [lnp, lnpB]

        def Cnat_view(i):
            return lnp_flat[:, i * 2048:(i + 1) * 2048].rearrange("p (j k) -> p j k", j=16)

        def E(eng, fn, reads=(), writes=()):
            return P.emit(eng, fn, reads=[b.tt for b in reads], writes=[b.tt for b in writes])

        def DMA(eng, fn, n, reads=(), writes=(), ds=None, is_out=False):
            return P.emit(eng, fn, reads=[b.tt for b in reads], writes=[b.tt for b in writes], dsem=ds, ndma=n,
                          is_out=is_out)

        def next_slot():
            s = ring[ring_i[0]]
            ring_i[0] = (ring_i[0] + 1) % NS
            return s

        w_in_r = IA["w_in"].rearrange("(c p) n -> p c n", p=128)
        w_out_r = IA["w_out"].rearrange("(c p) n -> p c n", p=128)
        w_gate_r = IA["w_gate"].rearrange("(c p) n -> p c n", p=128)
        w_up_r = IA["w_up"].rearrange("(c p) n -> p c n", p=128)
        w_down_r = IA["w_down"].rearrange("(j p) n -> p j n", p=128)

        def load_piece(pairs, extra_writes=()):
            slot = next_slot()

            def fn(e, s, slot=slot, pairs=pairs):
                for dfn, src in pairs:
                    e.dma_start(out=dfn(slot.t), in_=src).then_inc(s, 16)
            DMA("pool", fn, len(pairs), writes=[slot] + list(extra_writes), ds=slot.ds)
            return slot

        def v3(t, off, a, b_, c=None):
            ps = t[:].ap[0][0]
            if c is None:
                return bass.AP(t, off, [[ps, 128], [b_, a], [1, b_]])
            return bass.AP(t, off, [[ps, 128], [b_ * c, a], [c, b_], [1, c]])

        def load_q():
            pairs = []
            for hf in range(2):
                for c in range(4):
                    def dfn(t, hf=hf, c=c):
                        return bass.AP(t, c * 128 + hf * 64, [[t[:].ap[0][0], 128], [512, 8], [1, 64]])
                    src = bass.AP(I["w_in"], OFF_Q + hf * 256 + c * 64, [[IN_W, 128], [128 * IN_W, 8], [1, 64]])
                    pairs.append((dfn, src))
            return load_piece(pairs)

        def load_kvif():
            def d0(t):
                return v3(t, 0, 8, 264)[:, :, 0:256]

            def d1(t):
                return v3(t, 0, 8, 264)[:, :, 256:264]
            return load_piece([(d0, w_in_r[:, :, OFF_K:OFF_K + 256]), (d1, w_in_r[:, :, OFF_I:OFF_I + 8])])

        def load_cols512(src_r, off, extra_writes=()):
            return load_piece([(lambda t: v3(t, 0, 8, 512), src_r[:, :, off:off + 512])], extra_writes)

        wgb = nc.dram_tensor("w_gate_bf", [D, DFF], BF16, kind="Internal")
        wub = nc.dram_tensor("w_up_bf", [D, DFF], BF16, kind="Internal")
        wgb_r = wgb.ap().rearrange("(c p) n -> p c n", p=128)
        wub_r = wub.ap().rearrange("(c p) n -> p c n", p=128)
        conv_bufs = [Buf(None, TT("wcv%d" % k), DSem(new_sem("d_wcv%d" % k))) for k in range(4)]
        fences = [Buf(None, TT("fence%d" % k)) for k in range(5)]

        def convert_gu(k):
            r0, r1 = 256 * k, 256 * (k + 1)

            def fn(e, s):
                e.dma_start(out=wgb.ap()[r0:r1, :], in_=IA["w_gate"][r0:r1, :]).then_inc(s, 16)
                e.dma_start(out=wub.ap()[r0:r1, :], in_=IA["w_up"][r0:r1, :]).then_inc(s, 16)
            DMA("pool", fn, 2, reads=[fences[k]], writes=[conv_bufs[k]], ds=conv_bufs[k].ds)

        def load_gu(jj):
            slot = next_slot()

            def fn(e, s, slot=slot):
                e.dma_start(out=v3(slot.t, 0, 8, 256), in_=wgb_r[:, :, jj * 256:(jj + 1) * 256]).then_inc(s, 16)
                e.dma_start(out=v3(slot.t, 2048, 8, 256), in_=wub_r[:, :, jj * 256:(jj + 1) * 256]).then_inc(s, 16)
            DMA("sp", fn, 2, reads=conv_bufs, writes=[slot], ds=slot.ds)
            return slot

        def load_wd(e8):
            return load_piece([(lambda t: v3(t, 0, 22, 128), w_down_r[:, :, e8 * 128:(e8 + 1) * 128])])

        def load_wd_rows(p, nj):
            return load_piece([(lambda t: v3(t, 0, nj, 1024), w_down_r[:, p * 4:p * 4 + nj, :])])

        E("dve", lambda e: e.memset(Cext.t[:], 0.0), writes=[Cext])
        E("dve", lambda e: e.memset(Cbf.t[:], 0.0), writes=[Cbf])
        E("dve", lambda e: e.memset(msall.t[:], 0.0), writes=[msall])
        E("dve", lambda e: e.memset(cT.t[:], 0.0), writes=[cT])
        E("dve", lambda e: e.memset(vm_ext.t[:], 1.0), writes=[vm_ext])
        E("dve", lambda e: e.memset(kT.t[:], 0.0), writes=[kT])
        E("dve", lambda e: e.memset(v_bf.t[:], 0.0), writes=[v_bf])
        E("pool", lambda e: e.memset(ZM.t[:], 0.0), writes=[ZM])

        g = type("G", (), {})()
        g.__dict__.update(dict(locals()))
        _emit_groups(g)
        assert len(free_banks) == 8
        P.lower(esems)
    return nc


def _emit_groups(g):
    P, E, DMA, bank, release, v3 = g.P, g.E, g.DMA, g.bank, g.release, g.v3
    I, IA, OA = g.I, g.IA, g.OA
    xf, actT, qT, kT, cT, cact = g.xf, g.actT, g.qT, g.kT, g.cT, g.cact
    identf, identb = g.identf, g.identb
    HS = [0, 4, 1, 5, 2, 6, 3, 7]

    def diag(buf):
        return buf.dg

    def zs(buf, j):
        return buf.t[:, j * 128:(j + 1) * 128]

    ps_cT = cT.t[:].ap[0][0]
    hcount = [0]

    def headnorm(srcbuf, nh, hd, dst_ap, sqb, hn):
        n = nh * hd
        x3 = srcbuf.t[:, 0:n].rearrange("p (h d) -> p h d", h=nh)
        sq3 = sqb.t[:, 0:n].rearrange("p (h d) -> p h d", h=nh)
        E("dve", lambda e: e.tensor_reduce(out=hn.t[:, 0, 0:nh], in_=x3, axis=AX.X, op=ALU.add), reads=[srcbuf], writes=[hn])
        E("dve", lambda e: e.tensor_scalar(out=hn.t[:, 1, 0:nh], in0=hn.t[:, 0, 0:nh], scalar1=1.0 / hd, scalar2=None,
                                           op0=ALU.mult), reads=[hn], writes=[hn])
        yield
        E("pool", lambda e: e.tensor_tensor(out=x3, in0=x3, in1=hn.t[:, 1, 0:nh].unsqueeze(2).to_broadcast([128, nh, hd]),
                                            op=ALU.subtract), reads=[srcbuf, hn], writes=[srcbuf])
        E("act", lambda e: e.activation(out=sqb.t[:, 0:n], in_=srcbuf.t[:, 0:n], func=AF.Square), reads=[srcbuf], writes=[sqb])
        yield
        E("dve", lambda e: e.tensor_reduce(out=hn.t[:, 2, 0:nh], in_=sq3, axis=AX.X, op=ALU.add), reads=[sqb], writes=[hn])
        E("act", lambda e: e.activation(out=hn.t[:, 3, 0:nh], in_=hn.t[:, 2, 0:nh], func=AF.Ln, bias=g.cst.t[:, 1:2],
                                        scale=1.0 / hd), reads=[hn, g.cst], writes=[hn])
        E("act", lambda e: e.activation(out=hn.t[:, 3, 0:nh], in_=hn.t[:, 3, 0:nh], func=AF.Exp, scale=-0.5),
          reads=[hn], writes=[hn])
        yield
        E("pool", lambda e: e.tensor_tensor(out=dst_ap.rearrange("p (h d) -> p h d", h=nh), in0=x3,
                                            in1=hn.t[:, 3, 0:nh].unsqueeze(2).to_broadcast([128, nh, hd]), op=ALU.mult),
          reads=[srcbuf, hn], writes=[g.ybuf])

    def layernorm(b, gi, lnst_slot):
        ls = g.lnst
        lsb = g.lnstB[b]
        xb = g.XB(b)
        k = lnst_slot
        row = xf.t[:, b, :]
        jk = g.junkB[b % 2]
        E("act", lambda e: e.activation(out=jk.t[:], in_=row, func=AF.Identity, accum_out=ls.t[:, k, 0:1]),
          reads=xb, writes=[jk, lsb])
        E("act", lambda e: e.activation(out=jk.t[:], in_=row, func=AF.Square, accum_out=ls.t[:, k, 2:3]),
          reads=xb, writes=[jk, lsb])
        E("dve", lambda e: e.tensor_scalar(out=ls.t[:, k, 3:4], in0=ls.t[:, k, 0:1], scalar1=1.0 / 1024, scalar2=None,
                                           op0=ALU.mult), reads=[lsb], writes=[lsb])
        E("dve", lambda e: e.tensor_tensor(out=ls.t[:, k, 4:5], in0=ls.t[:, k, 3:4], in1=ls.t[:, k, 3:4], op=ALU.mult),
          reads=[lsb], writes=[lsb])
        E("dve", lambda e: e.scalar_tensor_tensor(out=ls.t[:, k, 5:6], in0=ls.t[:, k, 2:3], scalar=1.0 / 1024,
                                                  in1=ls.t[:, k, 4:5], op0=ALU.mult, op1=ALU.subtract),
          reads=[lsb], writes=[lsb])
        E("act", lambda e: e.activation(out=ls.t[:, k, 6:7], in_=ls.t[:, k, 5:6], func=AF.Ln, bias=g.cst.t[:, 1:2], scale=1.0),
          reads=[lsb, g.cst], writes=[lsb])
        E("act", lambda e: e.activation(out=ls.t[:, k, 7:8], in_=ls.t[:, k, 6:7], func=AF.Exp, scale=-0.5),
          reads=[lsb], writes=[lsb])
        E("dve", lambda e: e.scalar_tensor_tensor(out=ls.t[:, k, 8:9], in0=ls.t[:, k, 3:4], scalar=-1.0,
                                                  in1=ls.t[:, k, 7:8], op0=ALU.mult, op1=ALU.mult),
          reads=[lsb], writes=[lsb])
        E("act", lambda e: e.activation(out=row, in_=row, func=AF.Identity, bias=ls.t[:, k, 8:9], scale=ls.t[:, k, 7:8]),
          reads=xb + [lsb], writes=xb)
        E("dve", lambda e: e.tensor_tensor(out=row, in0=row, in1=g.lnp.t[:, 2 * gi, :], op=ALU.mult),
          reads=xb + [g.lnp, g.lnpB], writes=xb)
        E("pool", lambda e: e.tensor_tensor(out=xf.t[:, b, 0:512], in0=xf.t[:, b, 0:512], in1=g.lnp.t[:, 2 * gi + 1, 0:512], op=ALU.add),
          reads=[g.xfB[b], g.lnp, g.lnpB], writes=[g.xfB[b]])
        E("dve", lambda e: e.tensor_tensor(out=xf.t[:, b, 512:1024], in0=xf.t[:, b, 512:1024], in1=g.lnp.t[:, 2 * gi + 1, 512:1024], op=ALU.add),
          reads=[g.xfH[b], g.lnp, g.lnpB], writes=[g.xfH[b]])

    def run_group(kind, xsrc, nb, vr_off=0, want_kv=True, ndt_first=None, save_kv=False, ydst=None, fence_i=None):
        W = nb * 128
        smp = kind == "smp"
        pre = kind == "pre"
        rst = g.rowc.t[:, 2, :] if smp else g.rowc.t[:, 0, :]
        rsn = g.rowc.t[:, 3, :] if smp else g.rowc.t[:, 1, :]
        mT = g.mT_smp if smp else g.mT_std

        for b in range(nb):
            DMA("sp", lambda e, s, b=b: e.dma_start(out=xf.t[:, b, :], in_=xsrc[b * 128:(b + 1) * 128, :]).then_inc(s, 16),
                1, writes=g.XB(b), ds=g.xfB[b].ds)
        if pre:
            def ldv(e, s):
                e.dma_start(out=g.vrow.t[:, 0:W], in_=IA["vrow"][:, vr_off:vr_off + W]).then_inc(s, 16)
            DMA("sp", ldv, 1, writes=[g.vrow], ds=g.vrow.ds)
        for c in range(8):
            bk = bank()
            for b in range(nb):
                E("pe", lambda e, bk=bk, b=b, c=c: e.transpose(out=bk.t[:, b * 128:(b + 1) * 128],
                                                               in_=xf.t[:, b, c * 128:(c + 1) * 128], identity=identf.t[:]),
                  reads=g.XB(b) + [identf], writes=[bk])
            E("act", lambda e, bk=bk, c=c: e.copy(out=actT.t[:, c, 0:W], in_=bk.t[:, 0:W]), reads=[bk], writes=[actT])
            release(bk)

        slot = g.load_cols512(g.w_in_r, OFF_C)
        sv = v3(slot.t, 0, 8, 512)
        for h in range(4):
            bk = bank()
            for kc in range(8):
                E("pe", lambda e, bk=bk, kc=kc, h=h, sv=sv: e.matmul(bk.t[:, 0:W], lhsT=sv[:, kc, h * 128:(h + 1) * 128],
                                                                    rhs=actT.t[:, kc, 0:W], start=(kc == 0), stop=(kc == 7)),
                  reads=[slot, actT], writes=[bk])
            if smp:
                E("dve", lambda e, bk=bk, h=h: e.tensor_copy(out=bass.AP(cT.t, h * 176 + 3, [[ps_cT, 128], [11, 16], [1, 8]]),
                                                             in_=bk.t[:, 0:128].rearrange("p (j l) -> p j l", j=16)),
                  reads=[bk], writes=[cT])
                E("dve", lambda e, bk=bk, h=h: e.tensor_copy(out=g.csm.t[:, h, :], in_=bk.t[:, 0:128]), reads=[bk], writes=[g.csm])
            else:
                E("dve", lambda e, bk=bk, h=h: e.tensor_copy(out=cT.t[:, h * 515 + 3:h * 515 + 3 + W], in_=bk.t[:, 0:W]),
                  reads=[bk], writes=[cT])
            release(bk)
        slot = g.load_kvif()
        sv = v3(slot.t, 0, 8, 264)
        if want_kv:
            bk = bank()
            for kc in range(8):
                E("pe", lambda e, bk=bk, kc=kc, sv=sv: e.matmul(bk.t[:, 0:W], lhsT=sv[:, kc, 0:128], rhs=actT.t[:, kc, 0:W],
                                                               start=(kc == 0), stop=(kc == 7)), reads=[slot, actT], writes=[bk])
            E("dve", lambda e, bk=bk: e.tensor_copy(out=kT.t[:, 128:128 + W], in_=bk.t[:, 0:W]), reads=[bk], writes=[kT])
            release(bk)
        bki = bank()
        bkf = bank()
        for kc in range(8):
            E("pe", lambda e, kc=kc, sv=sv, bki=bki: e.matmul(bki.t[0:4, 0:W], lhsT=sv[:, kc, 256:260], rhs=actT.t[:, kc, 0:W],
                                                             start=(kc == 0), stop=(kc == 7)), reads=[slot, actT], writes=[bki])
        for kc in range(8):
            E("pe", lambda e, kc=kc, sv=sv, bkf=bkf: e.matmul(bkf.t[0:4, 0:W], lhsT=sv[:, kc, 260:264], rhs=actT.t[:, kc, 0:W],
                                                             start=(kc == 0), stop=(kc == 7)), reads=[slot, actT], writes=[bkf])
        E("act", lambda e: e.activation(out=g.logiall.t[:, 0:W], in_=bki.t[0:4, 0:W], func=AF.Identity, bias=g.bif.t[:, 0:1], scale=1.0),
          reads=[bki, g.bif], writes=[g.logiall])
        E("act", lambda e: e.activation(out=g.fall.t[:, 0:W], in_=bkf.t[0:4, 0:W], func=AF.Identity, bias=g.bif.t[:, 1:2], scale=1.0),
          reads=[bkf, g.bif], writes=[g.fall])
        release(bki, bkf)
        if want_kv:
            for b in range(nb):
                bk = bank()
                for kc in range(8):
                    E("pe", lambda e, bk=bk, kc=kc, b=b, sv=sv: e.matmul(bk.t[:, 0:256], lhsT=actT.t[:, kc, b * 128:(b + 1) * 128],
                                                                        rhs=sv[:, kc, 0:256], start=(kc == 0), stop=(kc == 7)),
                      reads=[slot, actT], writes=[bk])
                E("dve", lambda e, bk=bk, b=b: e.tensor_copy(out=g.v_bf.t[:, b + 1, :], in_=bk.t[:, 128:256]), reads=[bk], writes=[g.v_bf])
                if save_kv and b == nb - 1:
                    E("dve", lambda e, bk=bk: e.tensor_copy(out=g.kvf.t[:], in_=bk.t[:, 0:256]), reads=[bk], writes=[g.kvf])
                release(bk)
        fall, logiall, T2, Bl, NU, RPK, msall = g.fall, g.logiall, g.T2all, g.Ball, g.NEGUall, g.RPKall, g.msall
        cl, dB = g.colsall, g.decBall

        def _proj_lane():
            if not pre:
                slot = g.load_q()
                sv = v3(slot.t, 0, 8, 512)
                for c in range(4):
                    bk = bank()
                    for kc in range(8):
                        E("pe", lambda e, bk=bk, kc=kc, c=c, sv=sv: e.matmul(bk.t[:, 0:W], lhsT=sv[:, kc, c * 128:(c + 1) * 128],
                                                                            rhs=actT.t[:, kc, 0:W], start=(kc == 0), stop=(kc == 7)),
                          reads=[slot, actT], writes=[bk])
                    E("act", lambda e, bk=bk, c=c: e.activation(out=qT.t[:, c, 0:W], in_=bk.t[:, 0:W], func=AF.Identity,
                                                                scale=g.qscale.t[:, c:c + 1]),
                      reads=[bk, g.qscale], writes=[qT])
                    release(bk)
                    yield
            slot = g.load_cols512(g.w_in_r, OFF_VM, [g.fences[fence_i]] if fence_i is not None else [])
            sv = v3(slot.t, 0, 8, 512)
            for b in range(nb):
                bk = bank()
                for kc in range(8):
                    E("pe", lambda e, bk=bk, kc=kc, b=b, sv=sv: e.matmul(bk.t[:, 0:512], lhsT=actT.t[:, kc, b * 128:(b + 1) * 128],
                                                                        rhs=sv[:, kc, :], start=(kc == 0), stop=(kc == 7)),
                      reads=[slot, actT], writes=[bk])
                eng = "act" if b % 2 == 0 else "dve"
                if eng == "act":
                    E("act", lambda e, bk=bk, b=b: e.copy(out=g.vm_ext.t[:, b, :, 0:128], in_=bk.t[:, 0:512].rearrange("p (h d) -> p h d", h=4)),
                      reads=[bk], writes=[g.vm_ext])
                else:
                    E("dve", lambda e, bk=bk, b=b: e.tensor_copy(out=g.vm_ext.t[:, b, :, 0:128], in_=bk.t[:, 0:512].rearrange("p (h d) -> p h d", h=4)),
                      reads=[bk], writes=[g.vm_ext])
                release(bk)
                yield
            if not pre:
                slot = g.load_cols512(g.w_in_r, OFF_O)
                sv = v3(slot.t, 0, 8, 512)
                for b in range(nb):
                    bk = bank()
                    for kc in range(8):
                        E("pe", lambda e, bk=bk, kc=kc, b=b, sv=sv: e.matmul(bk.t[:, 0:512], lhsT=actT.t[:, kc, b * 128:(b + 1) * 128],
                                                                            rhs=sv[:, kc, :], start=(kc == 0), stop=(kc == 7)),
                          reads=[slot, actT], writes=[bk])
                    wt = (g.att, g.hm)[b % 2]
                    E("act", lambda e, bk=bk, wt=wt: e.activation(out=wt.t[:], in_=bk.t[:, 0:512], func=AF.Tanh, scale=0.5),
                      reads=[bk], writes=[wt])
                    E("dve", lambda e, wt=wt, b=b: e.tensor_scalar(out=g.osig.t[:, b, :], in0=wt.t[:], scalar1=0.5, scalar2=0.5,
                                                                  op0=ALU.mult, op1=ALU.add), reads=[wt], writes=[g.osig])
                    release(bk)
                    yield

            yield

        def _conv_lane():
            def csrc(h, j):
                if smp:
                    return bass.AP(cT.t, h * 176 + j, [[ps_cT, 128], [11, 16], [1, 8]])
                return cT.t[:, h * 515 + j:h * 515 + j + W]

            def tv(buf):
                if smp:
                    return buf.t[:, 0:128].rearrange("p (j l) -> p j l", j=16)
                return buf.t[:, 0:W]

            for h in range(4):
                acc = g.wtmp[h % 2]
                th = g.wtmp2[h % 2]
                E("act", lambda e, h=h, acc=acc: e.activation(out=tv(acc), in_=csrc(h, 3), func=AF.Identity,
                                                              bias=g.bconv.t[:, h:h + 1], scale=g.wconv.t[:, h, 3:4]),
                  reads=[cT, g.bconv, g.wconv], writes=[acc])
                yield
                for j in range(3):
                    E("dve", lambda e, h=h, j=j, acc=acc: e.scalar_tensor_tensor(out=tv(acc), in0=csrc(h, j), scalar=g.wconv.t[:, h, j:j + 1],
                                                                                in1=tv(acc), op0=ALU.mult, op1=ALU.add),
                      reads=[cT, g.wconv, acc], writes=[acc])
                    yield
                E("act", lambda e, acc=acc, th=th: e.activation(out=th.t[:, 0:W], in_=acc.t[:, 0:W], func=AF.Tanh, scale=0.5),
                  reads=[acc], writes=[th])
                yield
                E("dve", lambda e, h=h, acc=acc, th=th: e.scalar_tensor_tensor(out=cact.t[:, h, 0:W], in0=th.t[:, 0:W], scalar=1.0,
                                                                              in1=acc.t[:, 0:W], op0=ALU.add, op1=ALU.mult),
                  reads=[th, acc], writes=[cact])
                yield

            if not pre:
                for h in range(4):
                    bk = bank()
                    E("pe", lambda e, bk=bk, h=h: e.matmul(bk.t[:, 0:W], lhsT=g.wmq.t[:, h, :], rhs=cact.t[:, h, 0:W], start=True, stop=True),
                      reads=[g.wmq, cact], writes=[bk])
                    E("act", lambda e, bk=bk, h=h: e.activation(out=g.mqT.t[:, h, 0:W], in_=bk.t[:, 0:W], func=AF.Identity, scale=0.5),
                      reads=[bk], writes=[g.mqT])
                    release(bk)
                    yield
                    bk = bank()
                    E("pe", lambda e, bk=bk, h=h: e.matmul(bk.t[:, 0:W], lhsT=g.wmk.t[:, h, :], rhs=cact.t[:, h, 0:W], start=True, stop=True),
                      reads=[g.wmk, cact], writes=[bk])
                    E("dve", lambda e, bk=bk, h=h: e.tensor_scalar(out=g.mkT.t[:, h, 0:W], in0=bk.t[:, 0:W], scalar1=0.5 * KSCALE, scalar2=None,
                                                                  op0=ALU.mult), reads=[bk], writes=[g.mkT])
                    release(bk)
                    yield

            yield

        def _rows_lane():
            fall, logiall, T2, Bl, NU, RPK, msall = g.fall, g.logiall, g.T2all, g.Ball, g.NEGUall, g.RPKall, g.msall
            cl, dB = g.colsall, g.decBall
            E("act", lambda e: e.activation(out=T2.t[:, 0:W], in_=fall.t[:, 0:W], func=AF.Abs), reads=[fall], writes=[T2])
            E("act", lambda e: e.activation(out=T2.t[:, 0:W], in_=T2.t[:, 0:W], func=AF.Exp, scale=-1.0), reads=[T2], writes=[T2])
            E("act", lambda e: e.activation(out=T2.t[:, 0:W], in_=T2.t[:, 0:W], func=AF.Ln, bias=g.cst.t[0:4, 0:1], scale=1.0),
              reads=[T2, g.cst], writes=[T2])
            yield
            E("dve", lambda e: e.scalar_tensor_tensor(out=fall.t[:, 0:W], in0=fall.t[:, 0:W], scalar=0.0, in1=T2.t[:, 0:W],
                                                      op0=ALU.min, op1=ALU.subtract), reads=[fall, T2], writes=[fall])
            yield
            if pre:
                E("dve", lambda e: e.tensor_tensor(out=fall.t[:, 0:W], in0=fall.t[:, 0:W], in1=g.vrow.t[:, 0:W], op=ALU.mult),
                  reads=[fall, g.vrow], writes=[fall])
                yield
                E("dve", lambda e: e.tensor_tensor(out=logiall.t[:, 0:W], in0=logiall.t[:, 0:W], in1=g.vrow.t[:, 0:W], op=ALU.mult),
                  reads=[logiall, g.vrow], writes=[logiall])
                yield
                E("dve", lambda e: e.tensor_scalar(out=T2.t[:, 0:W], in0=g.vrow.t[:, 0:W], scalar1=30000.0, scalar2=-30000.0,
                                                   op0=ALU.mult, op1=ALU.add), reads=[g.vrow, T2], writes=[T2])
                yield
                E("dve", lambda e: e.tensor_tensor(out=logiall.t[:, 0:W], in0=logiall.t[:, 0:W], in1=T2.t[:, 0:W], op=ALU.add),
                  reads=[logiall, T2], writes=[logiall])
                yield
            for b in range(nb):
                sl = slice(b * 128, (b + 1) * 128)
                E("dve", lambda e, sl=sl: e.tensor_tensor_scan(out=Bl.t[:, sl], data0=rst, data1=fall.t[:, sl], initial=0.0,
                                                               op0=ALU.mult, op1=ALU.add), reads=[fall, g.rowc], writes=[Bl])
                yield
            E("dve", lambda e: e.tensor_tensor(out=RPK.t[:, 0, 0:W], in0=logiall.t[:, 0:W], in1=Bl.t[:, 0:W], op=ALU.subtract),
              reads=[logiall, Bl], writes=[RPK])
            for b in range(nb):
                sl = slice(b * 128, (b + 1) * 128)
                E("dve", lambda e, sl=sl: e.tensor_tensor_scan(out=T2.t[:, sl], data0=rsn, data1=RPK.t[:, 0, sl], initial=-1.0e30,
                                                               op0=ALU.add, op1=ALU.max), reads=[RPK, g.rowc], writes=[T2])
                yield
            for b in range(nb):
                sl = slice(b * 128, (b + 1) * 128)
                if smp:
                    E("dve", lambda e: e.tensor_tensor(out=NU.t[:, 0:128].rearrange("p (j l) -> p j l", j=16),
                                                       in0=T2.t[:, 0:128].rearrange("p (j l) -> p j l", j=16),
                                                       in1=g.ms16.t[:].unsqueeze(2).to_broadcast([4, 16, 8]), op=ALU.max),
                      reads=[T2, g.ms16], writes=[NU])
                    yield
                    E("dve", lambda e: e.tensor_scalar(out=NU.t[:, 0:128], in0=NU.t[:, 0:128], scalar1=-1.0, scalar2=None, op0=ALU.mult),
                      reads=[NU], writes=[NU])
                    yield
                else:
                    E("dve", lambda e, sl=sl, b=b: e.tensor_scalar(out=NU.t[:, sl], in0=T2.t[:, sl], scalar1=msall.t[:, b:b + 1], scalar2=-1.0,
                                                                  op0=ALU.max, op1=ALU.mult), reads=[T2, msall], writes=[NU])
                    yield
                E("dve", lambda e, sl=sl: e.tensor_tensor(out=RPK.t[:, 4, sl], in0=Bl.t[:, sl], in1=NU.t[:, sl], op=ALU.subtract),
                  reads=[Bl, NU], writes=[RPK])
                if not smp:
                    E("dve", lambda e, b=b: e.tensor_copy(out=msall.t[:, b + 1:b + 2], in_=RPK.t[:, 4, b * 128 + 127:b * 128 + 128]),
                      reads=[RPK], writes=[msall])
                    yield
            for b in range(nb):
                sl = slice(b * 128, (b + 1) * 128)
                if smp:
                    E("dve", lambda e: e.tensor_tensor(out=g.RTMP.t[:].rearrange("p (j l) -> p j l", j=16),
                                                       in0=NU.t[:, 0:128].rearrange("p (j l) -> p j l", j=16),
                                                       in1=g.ms16.t[:].unsqueeze(2).to_broadcast([4, 16, 8]), op=ALU.add),
                      reads=[NU, g.ms16], writes=[g.RTMP])
                    yield
                    E("act", lambda e: e.activation(out=RPK.t[:, 1, 0:128], in_=g.RTMP.t[:], func=AF.Exp), reads=[g.RTMP], writes=[RPK])
                    E("dve", lambda e: e.tensor_tensor(out=g.RTMP.t[:].rearrange("p (j l) -> p j l", j=16),
                                                       in0=RPK.t[:, 0, 0:128].rearrange("p (j l) -> p j l", j=16),
                                                       in1=NU.t[:, 0:128].rearrange("p (j l) -> p j l", j=16)[:, :, 7:8].to_broadcast([4, 16, 8]),
                                                       op=ALU.add), reads=[RPK, NU], writes=[g.RTMP])
                    yield
                    E("act", lambda e: e.activation(out=RPK.t[:, 3, 0:128], in_=g.RTMP.t[:], func=AF.Exp), reads=[g.RTMP], writes=[RPK])
                else:
                    E("act", lambda e, sl=sl, b=b: e.activation(out=RPK.t[:, 1, sl], in_=NU.t[:, sl], func=AF.Exp, bias=msall.t[:, b:b + 1], scale=1.0),
                      reads=[NU, msall], writes=[RPK])
                    E("act", lambda e, sl=sl, b=b: e.activation(out=RPK.t[:, 3, sl], in_=RPK.t[:, 0, sl], func=AF.Exp,
                                                                bias=NU.t[:, b * 128 + 127:b * 128 + 128], scale=1.0),
                      reads=[RPK, NU], writes=[RPK])
            E("act", lambda e: e.activation(out=RPK.t[:, 2, 0:W], in_=RPK.t[:, 4, 0:W], func=AF.Exp, scale=-1.0), reads=[RPK], writes=[RPK])
            yield

        lanes = [_rows_lane(), _conv_lane()]
        if not pre:
            lanes.insert(0, _proj_lane())
        else:
            lanes.insert(0, _proj_lane())
        while lanes:
            for ln in list(lanes):
                try:
                    next(ln)
                except StopIteration:
                    lanes.remove(ln)

        bk = bank()
        for b in range(nb):
            for q in range(5):
                E("pe", lambda e, bk=bk, q=q, b=b: e.transpose(out=bk.t[:, b * 20 + q * 4:b * 20 + (q + 1) * 4], in_=RPK.t[:, q, b * 128:(b + 1) * 128],
                                                              identity=identf.t[0:4, 0:4]), reads=[RPK, identf], writes=[bk])
        E("dve", lambda e, bk=bk: e.tensor_copy(out=cl.t[:, 0:nb, :], in_=bk.t[:, 0:nb * 20].rearrange("p (b c) -> p b c", b=nb)),
          reads=[bk], writes=[cl])
        release(bk)
        if smp:
            wl = RPK.t[:, 1, 0:128].rearrange("p (j l) -> p j l", j=16)[:, :, 7:8].rearrange("p j o -> p o j")
            E("dve", lambda e, wl=wl: e.tensor_tensor(out=g.dexp.t[:].rearrange("p (h j) -> p h j", h=4),
                                                     in0=wl.to_broadcast([4, 4, 16]),
                                                     in1=g.i4.t[:].unsqueeze(2).to_broadcast([4, 4, 16]), op=ALU.mult),
              reads=[RPK, g.i4], writes=[g.dexp])
            nd_ = 64
        else:
            for b in range(nb):
                E("dve", lambda e, b=b: e.tensor_scalar(out=g.dexp.t[:, b * 4:(b + 1) * 4], in0=g.i4.t[:], scalar1=RPK.t[:, 1, b * 128 + 127:b * 128 + 128],
                                                       scalar2=None, op0=ALU.mult), reads=[RPK, g.i4], writes=[g.dexp])
            nd_ = nb * 4
        bk = bank()
        E("pe", lambda e, bk=bk, nd_=nd_: e.matmul(bk.t[:, 0:nd_], lhsT=g.ones4.t[:], rhs=g.dexp.t[:, 0:nd_], start=True, stop=True),
          reads=[g.ones4, g.dexp], writes=[bk])
        E("dve", lambda e, bk=bk, nd_=nd_: e.tensor_copy(out=dB.t[:, 0:nd_], in_=bk.t[:, 0:nd_]), reads=[bk], writes=[dB])
        release(bk)
        if not smp:
            E("dve", lambda e: e.tensor_copy(out=msall.t[:, 0:1], in_=msall.t[:, nb:nb + 1]), reads=[msall], writes=[msall])

        lanes = []
        if not pre:
            lanes.append(_attention_lane(g, kind, nb, ndt_first, hcount))
        lanes.append(_mlstm_lane(g, kind, nb, mT))
        while lanes:
            for ln in list(lanes):
                try:
                    next(ln)
                except StopIteration:
                    lanes.remove(ln)

        if not smp:
            for h in range(4):
                E("act", lambda e, h=h: e.copy(out=cT.t[:, h * 515:h * 515 + 3], in_=cT.t[:, h * 515 + W:h * 515 + W + 3]),
                  reads=[cT], writes=[cT])
            if want_kv:
                E("act", lambda e: e.copy(out=kT.t[:, 0:128], in_=kT.t[:, W:W + 128]), reads=[kT], writes=[kT])
                E("act", lambda e: e.copy(out=g.v_bf.t[:, 0, :], in_=g.v_bf.t[:, nb, :]), reads=[g.v_bf], writes=[g.v_bf])
        if pre:
            return
        if smp:
            def reload_ln1(e, s):
                for i_, nm in enumerate(("ln1_g", "ln1_b")):
                    e.dma_start(out=g.lnp.t[:, i_, :], in_=IA[nm].to_broadcast([128, 1024])).then_inc(s, 16)

            def reload_ln2(e, s):
                for i_, nm in enumerate(("ln2_g", "ln2_b")):
                    e.dma_start(out=g.lnp.t[:, 2 + i_, :], in_=IA[nm].to_broadcast([128, 1024])).then_inc(s, 16)
            DMA("sp", reload_ln1, 2, writes=[g.lnp], ds=g.lnp.ds)
            DMA("sp", reload_ln2, 2, writes=[g.lnpB], ds=g.lnpB.ds)
        _dense_tail(g, kind, nb, ydst, layernorm)

    g.run_group = run_group
    g.headnorm = headnorm
    _schedule(g)


HS_ORDER = [0, 4, 1, 5, 2, 6, 3, 7]


def _acquire(g, n):
    while len(g.free_banks) < n:
        yield
    return [g.bank() for _ in range(n)]


def _attention_lane(g, kind, nb, ndt_first, hcount):
    E, release = g.E, g.release
    smp = kind == "smp"
    qT, kT = g.qT, g.kT
    for b in range(nb):
        ndt = ndt_first if (b == 0 and ndt_first is not None) else (g.nd_smp if smp else g.nd_std)
        st = g.stat[b % 2]
        sl = slice(b * 128, (b + 1) * 128)
        (bO,) = yield from _acquire(g, 1)

        def s1(h, bk, hp, b=b, sl=sl, st=st, ndt=ndt):
            c = h % 4
            base = (h // 4) * 64
            pe_ = g.pexp[hp]
            E("pe", lambda e: e.matmul(bk.t[:, 0:256], lhsT=g.identb.t[:], rhs=ndt.t[:], start=True, stop=False),
              reads=[g.identb, ndt], writes=[bk])
            if smp:
                if h < 4:
                    E("dve", lambda e: e.tensor_copy(out=g.ZA.dg, in_=qT.t[:, c, 0:128].rearrange("p (j l) -> p j l", j=16)),
                      reads=[qT], writes=[g.ZA])
                for j in range(16):
                    E("pe", lambda e, j=j: e.matmul(bk.t[:, 0:128], lhsT=g.ZA.t[base:base + 64, j * 128:(j + 1) * 128],
                                                   rhs=g.cKT.t[base:base + 64, j, :], start=False, stop=(j == 15)),
                      reads=[g.ZA, g.cKT], writes=[bk])
                E("pe", lambda e: e.matmul(bk.t[:, 128:256], lhsT=qT.t[base:base + 64, c, 0:128], rhs=kT.t[base:base + 64, 128:256],
                                           start=False, stop=True), reads=[qT, kT], writes=[bk])
            else:
                E("pe", lambda e: e.matmul(bk.t[:, 0:256], lhsT=qT.t[base:base + 64, c, sl], rhs=kT.t[base:base + 64, b * 128:b * 128 + 256],
                                           start=False, stop=True), reads=[qT, kT], writes=[bk])
            E("dve", lambda e: e.reduce_max(out=st.t[:, 0, h:h + 1], in_=bk.t[:, 0:256], axis=AX.X), reads=[bk], writes=[st])
            E("dve", lambda e: e.tensor_scalar(out=st.t[:, 1, h:h + 1], in0=st.t[:, 0, h:h + 1], scalar1=g.sinkq.t[:, h:h + 1], scalar2=-SLOPES[h],
                                               op0=ALU.max, op1=ALU.mult), reads=[st, g.sinkq], writes=[st])
            E("act", lambda e: e.activation(out=pe_.t[:], in_=bk.t[:, 0:256], func=AF.Exp, bias=st.t[:, 1, h:h + 1], scale=SLOPES[h],
                                            accum_out=st.t[:, 2, h:h + 1]), reads=[bk, st], writes=[pe_, st])

        def s2a(h, bk, hp):
            pe_, pt_ = g.pexp[hp], g.PT[hp]
            bPb = bk.t[:].bitcast(BF16)[:, 512:768]
            E("pe", lambda e: e.transpose(out=bPb[:, 0:128], in_=pe_.t[:, 0:128], identity=g.identb.t[:]), reads=[pe_, g.identb], writes=[bk])
            E("pe", lambda e: e.transpose(out=bPb[:, 128:256], in_=pe_.t[:, 128:256], identity=g.identb.t[:]), reads=[pe_, g.identb], writes=[bk])
            if smp:
                E("act", lambda e: e.copy(out=g.PZ.dg, in_=bPb[:, 0:128].rearrange("p (j l) -> p j l", j=16)), reads=[bk], writes=[g.PZ])
                E("act", lambda e: e.copy(out=pt_.t[:, 128:256], in_=bPb[:, 128:256]), reads=[bk], writes=[pt_])
            else:
                E("act", lambda e: e.copy(out=pt_.t[:], in_=bPb[:, 0:256]), reads=[bk], writes=[pt_])
            release(bk)

        def s2b(h, bk, hp, b=b, bO=bO):
            kvh = h // 4
            pt_ = g.PT[hp]
            ocol = bO.t[:, h * 64:(h + 1) * 64]
            if smp:
                for j in range(16):
                    E("pe", lambda e, j=j: e.matmul(ocol, lhsT=g.PZ.t[:, j * 128:(j + 1) * 128], rhs=g.cVb.t[:, j, kvh * 64:(kvh + 1) * 64],
                                                   start=(j == 0), stop=False), reads=[g.PZ, g.cVb], writes=[bO])
                E("pe", lambda e: e.matmul(ocol, lhsT=pt_.t[:, 128:256], rhs=g.v_bf.t[:, 1, kvh * 64:(kvh + 1) * 64], start=False, stop=True),
                  reads=[pt_, g.v_bf], writes=[bO])
            else:
                E("pe", lambda e: e.matmul(ocol, lhsT=pt_.t[:, 0:128], rhs=g.v_bf.t[:, b, kvh * 64:(kvh + 1) * 64], start=True, stop=False),
                  reads=[pt_, g.v_bf], writes=[bO])
                E("pe", lambda e: e.matmul(ocol, lhsT=pt_.t[:, 128:256], rhs=g.v_bf.t[:, b + 1, kvh * 64:(kvh + 1) * 64], start=False, stop=True),
                  reads=[pt_, g.v_bf], writes=[bO])

        heads = []
        nsteps = len(HS_ORDER) + 4
        for i in range(nsteps):
            if i - 4 >= 0 and i - 4 < len(heads):
                s2b(*heads[i - 4])
            if i - 3 >= 0 and i - 3 < len(HS_ORDER):
                s2a(*heads[i - 3])
            if i < len(HS_ORDER):
                (bk,) = yield from _acquire(g, 1)
                hp = hcount[0] % 3
                hcount[0] += 1
                heads.append((HS_ORDER[i], bk, hp))
                s1(*heads[i])
            yield
        E("dve", lambda e, st=st: e.tensor_tensor(out=st.t[:, 3, :], in0=g.sinkb.t[:], in1=st.t[:, 1, :], op=ALU.add), reads=[st, g.sinkb], writes=[st])
        E("act", lambda e, st=st: e.activation(out=st.t[:, 3, :], in_=st.t[:, 3, :], func=AF.Exp), reads=[st], writes=[st])
        yield
        E("dve", lambda e, st=st: e.tensor_tensor(out=st.t[:, 4, :], in0=st.t[:, 2, :], in1=st.t[:, 3, :], op=ALU.add), reads=[st], writes=[st])
        E("dve", lambda e, st=st: e.reciprocal(out=st.t[:, 5, :], in_=st.t[:, 4, :]), reads=[st], writes=[st])
        yield
        E("dve", lambda e, st=st, bO=bO: e.tensor_tensor(out=g.att.t[:].rearrange("p (h d) -> p h d", h=8),
                                                        in0=bO.t[:, 0:512].rearrange("p (h d) -> p h d", h=8),
                                                        in1=st.t[:, 5, :].unsqueeze(2).to_broadcast([128, 8, 64]), op=ALU.mult),
          reads=[bO, st], writes=[g.att])
        release(bO)
        yield
        yield from g.headnorm(g.att, 8, 64, g.ybuf.t[:, b, 0:512], g.sq, g.hn_a)
        yield


def _mlstm_lane(g, kind, nb, mT):
    pre = kind == "pre"
    smp = kind == "smp"
    for b in range(nb):
        if smp:
            yield from _mlstm_state(g, kind, b, nb, part="mkw")
            yield from _mlstm_out(g, kind, b, mT, inline_state=True)
            yield from _mlstm_state(g, kind, b, nb, part="tail")
            continue
        if not pre:
            yield from _mlstm_out(g, kind, b, mT)
        yield from _mlstm_state(g, kind, b, nb)


def _mlstm_out(g, kind, b, mT, inline_state=False):
    E, DMA, release = g.E, g.DMA, g.release
    smp = kind == "smp"
    sl = slice(b * 128, (b + 1) * 128)
    mqT, mkT = g.mqT, g.mkT
    cl = g.colsall
    clb = cl.t[:, b, :]
    bST, bE = yield from _acquire(g, 2)
    for h in range(4):
        hs = slice(h * 128, (h + 1) * 128)
        E("pe", lambda e, h=h, hs=hs: e.matmul(bST.t[:, hs], lhsT=mkT.t[:, h, sl], rhs=mqT.t[:, h, sl], start=True, stop=True),
          reads=[mkT, mqT], writes=[bST])
        E("pe", lambda e, h=h, hs=hs: e.matmul(bE.t[:, hs], lhsT=g.selrows.t[:, hs], rhs=g.NEGUall.t[:, sl], start=True, stop=False),
          reads=[g.selrows, g.NEGUall], writes=[bE])
        E("pe", lambda e, h=h, hs=hs: e.matmul(bE.t[:, hs], lhsT=g.identb.t[:], rhs=mT.t[:], start=False, stop=True),
          reads=[g.identb, mT], writes=[bE])
    yield
    yield
    for h in range(4):
        hs = slice(h * 128, (h + 1) * 128)
        E("act", lambda e, h=h, hs=hs: e.activation(out=g.DTt.t[:, hs], in_=bE.t[:, hs], func=AF.Exp, bias=clb[:, h:h + 1], scale=1.0),
          reads=[bE, cl], writes=[g.DTt])
        if h % 2 == 1:
            yield
    release(bE)
    E("dve", lambda e: e.tensor_tensor(out=g.scT.t[:], in0=bST.t[:, 0:512], in1=g.DTt.t[:], op=ALU.mult), reads=[bST, g.DTt], writes=[g.scT])
    release(bST)
    yield
    yield
    for k in range(2):
        bIk, bJk = yield from _acquire(g, 2)
        for h in (2 * k, 2 * k + 1):
            hs = slice(h * 128, (h + 1) * 128)
            oc = slice((h % 2) * 129, (h % 2) * 129 + 129)
            E("pe", lambda e, h=h, hs=hs, oc=oc, bIk=bIk: e.matmul(bIk.t[:, oc], lhsT=g.scT.t[:, hs], rhs=g.vm_ext.t[:, b, h, :], start=True, stop=True),
              reads=[g.scT, g.vm_ext], writes=[bIk])
            if smp:
                cn = g.Cnat_view(h % 2)
                cb = g.cnb[h % 2]
                DMA("sp", lambda e, s, cn=cn, h=h: e.dma_start(out=cn, in_=g.IA["sC"][:, h].rearrange("j v k -> v j k")).then_inc(s, 16),
                    1, writes=[cb], ds=cb.ds)
                for jq in range(4):
                    (bk,) = yield from _acquire(g, 1)
                    for jj in range(4):
                        j = jq * 4 + jj
                        E("pe", lambda e, bk=bk, jj=jj, j=j, cn=cn: e.transpose(out=bk.t[:, jj * 128:(jj + 1) * 128], in_=cn[:, j, :], identity=g.identf.t[:]),
                          reads=[cb, g.identf], writes=[bk])
                    if jq % 2 == 0:
                        E("act", lambda e, bk=bk, jq=jq: e.copy(out=g.CTb.t[:, jq * 4:(jq + 1) * 4, 0:128], in_=bk.t[:, 0:512].rearrange("p (j k) -> p j k", j=4)),
                          reads=[bk], writes=[g.CTb])
                    else:
                        E("dve", lambda e, bk=bk, jq=jq: e.tensor_copy(out=g.CTb.t[:, jq * 4:(jq + 1) * 4, 0:128], in_=bk.t[:, 0:512].rearrange("p (j k) -> p j k", j=4)),
                          reads=[bk], writes=[g.CTb])
                    release(bk)
                    yield
                E("dve", lambda e, h=h: e.tensor_copy(out=g.CTb.t[:, :, 128:129], in_=g.snT.t[:].rearrange("p (j h) -> p j h", h=4)[:, :, h:h + 1]),
                  reads=[g.snT], writes=[g.CTb])
                E("dve", lambda e, h=h: e.tensor_copy(out=g.ZM.dg, in_=mqT.t[:, h, 0:128].rearrange("p (j l) -> p j l", j=16)),
                  reads=[mqT], writes=[g.ZM])
                yield
                for j in range(16):
                    E("pe", lambda e, h=h, j=j, oc=oc, bJk=bJk: e.matmul(bJk.t[:, oc], lhsT=g.ZM.t[:, j * 128:(j + 1) * 128], rhs=g.CTb.t[:, j, :],
                                                                        start=(j == 0), stop=(j == 15)), reads=[g.ZM, g.CTb], writes=[bJk])
                if inline_state:
                    mw = g.mkw[b % 2]
                    dB = g.decBall
                    E("dve", lambda e, h=h: e.tensor_tensor(out=g.VZ.t[:], in0=g.vm_ext.t[:, 0, h, 0:128].unsqueeze(1).to_broadcast([128, 16, 128]),
                                                           in1=g.m16f.t[:].unsqueeze(2).to_broadcast([128, 16, 128]), op=ALU.mult),
                      reads=[g.vm_ext, g.m16f], writes=[g.VZ])
                    yield
                    for jq in range(4):
                        (bk,) = yield from _acquire(g, 1)
                        for jj in range(4):
                            j = jq * 4 + jj
                            E("pe", lambda e, bk=bk, jj=jj, j=j, h=h: e.matmul(bk.t[:, jj * 128:(jj + 1) * 128], lhsT=g.VZ.t[:, j, :], rhs=mw.t[:, h, :],
                                                                              start=True, stop=True), reads=[g.VZ, mw], writes=[bk])
                        for jj in range(4):
                            j = jq * 4 + jj
                            E("dve", lambda e, bk=bk, jj=jj, j=j, h=h, cn=cn: e.scalar_tensor_tensor(out=cn[:, j, :], in0=cn[:, j, :],
                                                                                                   scalar=dB.t[:, h * 16 + j:h * 16 + j + 1],
                                                                                                   in1=bk.t[:, jj * 128:(jj + 1) * 128], op0=ALU.mult, op1=ALU.add),
                              reads=[cb, dB, bk], writes=[cb])
                        release(bk)
                        yield
                    DMA("sp", lambda e, s, cn=cn, h=h: e.dma_start(out=g.OA["sCo"][:, h].rearrange("j v k -> v j k"), in_=cn).then_inc(s, 16),
                        1, reads=[cb], ds=cb.ds, is_out=True)
            else:
                E("pe", lambda e, h=h, oc=oc, bJk=bJk: e.matmul(bJk.t[:, oc], lhsT=mqT.t[:, h, sl], rhs=g.Cbf.t[:, h, :], start=True, stop=True),
                  reads=[mqT, g.Cbf], writes=[bJk])
        yield
        yield
        for h in (2 * k, 2 * k + 1):
            oc = slice((h % 2) * 129, (h % 2) * 129 + 129)
            E("act", lambda e, h=h, oc=oc, bJk=bJk: e.activation(out=g.intsb.t[:, h, :], in_=bJk.t[:, oc], func=AF.Identity, scale=clb[:, 4 + h:5 + h]),
              reads=[bJk, cl], writes=[g.intsb])
        yield
        yield
        E("dve", lambda e, k=k, bIk=bIk: e.tensor_tensor(out=g.tot.t[:, 2 * k:2 * k + 2, :], in0=g.intsb.t[:, 2 * k:2 * k + 2, :],
                                                        in1=bIk.t[:, 0:258].rearrange("p (h d) -> p h d", h=2), op=ALU.add),
          reads=[g.intsb, bIk], writes=[g.tot])
        release(bIk, bJk)
        yield
    mst = g.mst
    E("act", lambda e: e.activation(out=mst.t[:, 0, :], in_=g.tot.t[:, :, 128], func=AF.Abs), reads=[g.tot], writes=[mst])
    E("dve", lambda e: e.tensor_tensor(out=mst.t[:, 1, :], in0=mst.t[:, 0, :], in1=clb[:, 8:12], op=ALU.max), reads=[mst, cl], writes=[mst])
    E("dve", lambda e: e.reciprocal(out=mst.t[:, 2, :], in_=mst.t[:, 1, :]), reads=[mst], writes=[mst])
    yield
    E("dve", lambda e: e.tensor_tensor(out=g.hm.t[:].rearrange("p (h d) -> p h d", h=4), in0=g.tot.t[:, :, 0:128],
                                       in1=mst.t[:, 2, :].unsqueeze(2).to_broadcast([128, 4, 128]), op=ALU.mult),
      reads=[g.tot, mst], writes=[g.hm])
    yield
    E("pool", lambda e: e.tensor_tensor(out=g.hm.t[:], in0=g.hm.t[:], in1=g.osig.t[:, b, :], op=ALU.mult), reads=[g.hm, g.osig], writes=[g.hm])
    yield
    yield from g.headnorm(g.hm, 4, 128, g.ybuf.t[:, b, 512:1024], g.xc, g.hn_m)
    yield


def _mlstm_state(g, kind, b, nb, part="all"):
    E, DMA, release = g.E, g.DMA, g.release
    smp = kind == "smp"
    pre = kind == "pre"
    sl = slice(b * 128, (b + 1) * 128)
    mw = g.mkw[b % 2]
    cl = g.colsall
    clb = cl.t[:, b, :]
    dB = g.decBall
    if part == "tail":
        yield from _mlstm_state_tail(g, b, mw, cl, clb)
        return
    (bK,) = yield from _acquire(g, 1)
    for h in range(4):
        hs = slice(h * 128, (h + 1) * 128)
        E("pe", lambda e, h=h, hs=hs: e.matmul(bK.t[:, hs], lhsT=g.cact.t[:, h, sl], rhs=g.wmk.t[:, h, :], start=True, stop=True),
          reads=[g.cact, g.wmk], writes=[bK])
    yield
    yield
    for h in range(4):
        hs = slice(h * 128, (h + 1) * 128)
        E("dve", lambda e, h=h, hs=hs: e.tensor_scalar(out=mw.t[:, h, :], in0=bK.t[:, hs], scalar1=clb[:, 12 + h:13 + h], scalar2=0.5 * KSCALE,
                                                      op0=ALU.mult, op1=ALU.mult), reads=[bK, cl], writes=[mw])
        if h % 2 == 1:
            yield
    release(bK)
    if part == "mkw":
        return
    if not smp:
        bD0, bD1 = yield from _acquire(g, 2)
        bD = [bD0, bD1]
        for h in range(4):
            oc = slice((h % 2) * 129, (h % 2) * 129 + 129)
            E("pe", lambda e, h=h, oc=oc: e.matmul(bD[h // 2].t[:, oc], lhsT=mw.t[:, h, :], rhs=g.vm_ext.t[:, b, h, :], start=True, stop=True),
              reads=[mw, g.vm_ext], writes=[bD[h // 2]])
        yield
        yield
        for h in range(4):
            oc = slice((h % 2) * 129, (h % 2) * 129 + 129)
            E("dve", lambda e, h=h, oc=oc: e.scalar_tensor_tensor(out=g.Cext.t[:, h, :], in0=g.Cext.t[:, h, :], scalar=dB.t[:, b * 4 + h:b * 4 + h + 1],
                                                                 in1=bD[h // 2].t[:, oc], op0=ALU.mult, op1=ALU.add),
              reads=[g.Cext, dB, bD[h // 2]], writes=[g.Cext])
            if h % 2 == 1:
                yield
        release(bD[0], bD[1])
        if (not pre) or b == nb - 1:
            E("act", lambda e: e.copy(out=g.Cbf.t[:], in_=g.Cext.t[:]), reads=[g.Cext], writes=[g.Cbf])
        yield
        return
    (bN,) = yield from _acquire(g, 1)
    for h in range(4):
        hs = slice(h * 128, (h + 1) * 128)
        cn = g.Cnat_view(h % 2)
        cb = g.cnb[h % 2]
        DMA("sp", lambda e, s, cn=cn, h=h: e.dma_start(out=cn, in_=g.IA["sC"][:, h].rearrange("j v k -> v j k")).then_inc(s, 16),
            1, writes=[cb], ds=cb.ds)
        E("dve", lambda e, h=h: e.tensor_tensor(out=g.VZ.t[:], in0=g.vm_ext.t[:, 0, h, 0:128].unsqueeze(1).to_broadcast([128, 16, 128]),
                                               in1=g.m16f.t[:].unsqueeze(2).to_broadcast([128, 16, 128]), op=ALU.mult),
          reads=[g.vm_ext, g.m16f], writes=[g.VZ])
        yield
        for jq in range(4):
            (bk,) = yield from _acquire(g, 1)
            for jj in range(4):
                j = jq * 4 + jj
                E("pe", lambda e, bk=bk, jj=jj, j=j, h=h: e.matmul(bk.t[:, jj * 128:(jj + 1) * 128], lhsT=g.VZ.t[:, j, :], rhs=mw.t[:, h, :],
                                                                  start=True, stop=True), reads=[g.VZ, mw], writes=[bk])
            for jj in range(4):
                j = jq * 4 + jj
                E("dve", lambda e, bk=bk, jj=jj, j=j, h=h, cn=cn: e.scalar_tensor_tensor(out=cn[:, j, :], in0=cn[:, j, :],
                                                                                       scalar=dB.t[:, h * 16 + j:h * 16 + j + 1],
                                                                                       in1=bk.t[:, jj * 128:(jj + 1) * 128], op0=ALU.mult, op1=ALU.add),
                  reads=[cb, dB, bk], writes=[cb])
            release(bk)
            yield
        DMA("sp", lambda e, s, cn=cn, h=h: e.dma_start(out=g.OA["sCo"][:, h].rearrange("j v k -> v j k"), in_=cn).then_inc(s, 16),
            1, reads=[cb], ds=cb.ds, is_out=True)
        E("pe", lambda e, h=h, hs=hs: e.matmul(bN.t[0:16, hs], lhsT=g.m16b.t[:], rhs=mw.t[:, h, :], start=True, stop=True),
          reads=[g.m16b, mw], writes=[bN])
        yield
    (bk,) = yield from _acquire(g, 1)
    E("pe", lambda e: e.matmul(bk.t[0:16, 0:4], lhsT=g.sellast.t[:], rhs=clb[:, 4:8], start=True, stop=True), reads=[g.sellast, cl], writes=[bk])
    E("pe", lambda e: e.matmul(bk.t[0:16, 4:8], lhsT=g.sellast.t[:], rhs=clb[:, 16:20], start=True, stop=True), reads=[g.sellast, cl], writes=[bk])
    E("dve", lambda e: e.tensor_copy(out=g.s16.t[:], in_=bk.t[0:16, 0:8]), reads=[bk], writes=[g.s16])
    release(bk)
    yield
    E("dve", lambda e: e.tensor_tensor(out=g.n16.t[:], in0=g.n16.t[:], in1=g.s16.t[:, 0:4].unsqueeze(2).to_broadcast([16, 4, 128]), op=ALU.mult),
      reads=[g.n16, g.s16], writes=[g.n16])
    E("dve", lambda e: e.tensor_tensor(out=g.n16.t[:], in0=g.n16.t[:], in1=bN.t[0:16, 0:512].rearrange("p (h k) -> p h k", h=4), op=ALU.add),
      reads=[g.n16, bN], writes=[g.n16])
    release(bN)
    DMA("sp", lambda e, s: e.dma_start(out=g.OA["sno"], in_=g.n16.t[:]).then_inc(s, 16), 1, reads=[g.n16], ds=g.n16.ds, is_out=True)
    DMA("sp", lambda e, s: e.dma_start(out=g.OA["smo"], in_=g.s16.t[:, 4:8]).then_inc(s, 16), 1, reads=[g.s16], ds=g.s16.ds, is_out=True)
    yield


def _mlstm_state_tail(g, b, mw, cl, clb):
    E, DMA, release = g.E, g.DMA, g.release
    (bN,) = yield from _acquire(g, 1)
    for h in range(4):
        hs = slice(h * 128, (h + 1) * 128)
        E("pe", lambda e, h=h, hs=hs: e.matmul(bN.t[0:16, hs], lhsT=g.m16b.t[:], rhs=mw.t[:, h, :], start=True, stop=True),
          reads=[g.m16b, mw], writes=[bN])
    yield
    (bk,) = yield from _acquire(g, 1)
    E("pe", lambda e: e.matmul(bk.t[0:16, 0:4], lhsT=g.sellast.t[:], rhs=clb[:, 4:8], start=True, stop=True), reads=[g.sellast, cl], writes=[bk])
    E("pe", lambda e: e.matmul(bk.t[0:16, 4:8], lhsT=g.sellast.t[:], rhs=clb[:, 16:20], start=True, stop=True), reads=[g.sellast, cl], writes=[bk])
    E("dve", lambda e: e.tensor_copy(out=g.s16.t[:], in_=bk.t[0:16, 0:8]), reads=[bk], writes=[g.s16])
    release(bk)
    yield
    E("dve", lambda e: e.tensor_tensor(out=g.n16.t[:], in0=g.n16.t[:], in1=g.s16.t[:, 0:4].unsqueeze(2).to_broadcast([16, 4, 128]), op=ALU.mult),
      reads=[g.n16, g.s16], writes=[g.n16])
    E("dve", lambda e: e.tensor_tensor(out=g.n16.t[:], in0=g.n16.t[:], in1=bN.t[0:16, 0:512].rearrange("p (h k) -> p h k", h=4), op=ALU.add),
      reads=[g.n16, bN], writes=[g.n16])
    release(bN)
    DMA("sp", lambda e, s: e.dma_start(out=g.OA["sno"], in_=g.n16.t[:]).then_inc(s, 16), 1, reads=[g.n16], ds=g.n16.ds, is_out=True)
    DMA("sp", lambda e, s: e.dma_start(out=g.OA["smo"], in_=g.s16.t[:, 4:8]).then_inc(s, 16), 1, reads=[g.s16], ds=g.s16.ds, is_out=True)
    yield


def _dense_tail(g, kind, nb, ydst, layernorm):
    E, DMA, bank, release, v3 = g.E, g.DMA, g.bank, g.release, g.v3
    xf, actT = g.xf, g.actT
    W = nb * 128
    ls = g.lnst
    for c in range(8):
        bk = bank()
        bkb = bk.t[:].bitcast(BF16)
        for b in range(nb):
            E("pe", lambda e, bkb=bkb, b=b, c=c: e.transpose(out=bkb[:, b * 128:(b + 1) * 128], in_=g.ybuf.t[:, b, c * 128:(c + 1) * 128],
                                                            identity=g.identb.t[:]), reads=[g.ybuf, g.identb], writes=[bk])
        if c % 2 == 0:
            E("act", lambda e, bkb=bkb, c=c: e.activation(out=actT.t[:, c, 0:W], in_=bkb[:, 0:W], func=AF.Identity, scale=g.gcol.t[:, c:c + 1]),
              reads=[bk, g.gcol], writes=[actT])
        else:
            E("dve", lambda e, bkb=bkb, c=c: e.tensor_scalar(out=actT.t[:, c, 0:W], in0=bkb[:, 0:W], scalar1=g.gcol.t[:, c:c + 1], scalar2=None,
                                                            op0=ALU.mult), reads=[bk, g.gcol], writes=[actT])
        release(bk)
    slots = [g.load_cols512(g.w_out_r, half * 512) for half in range(2)]
    svs = [v3(sl_.t, 0, 8, 512) for sl_ in slots]
    for b in range(nb):
        for half in range(2):
            slot, sv = slots[half], svs[half]
            bk = bank()
            for kc in range(8):
                E("pe", lambda e, bk=bk, kc=kc, b=b, sv=sv: e.matmul(bk.t[:, 0:512], lhsT=actT.t[:, kc, b * 128:(b + 1) * 128], rhs=sv[:, kc, :],
                                                                    start=(kc == 0), stop=(kc == 7)), reads=[slot, actT], writes=[bk])
            hsl = slice(half * 512, (half + 1) * 512)
            E("dve", lambda e, bk=bk, b=b, hsl=hsl: e.scalar_tensor_tensor(out=xf.t[:, b, hsl], in0=xf.t[:, b, hsl], scalar=ALPHA, in1=bk.t[:, 0:512],
                                                                          op0=ALU.mult, op1=ALU.add), reads=g.XB(b) + [bk], writes=g.XB(b))
            release(bk)
        layernorm(b, 0, b)
    for c in range(8):
        bk = bank()
        for b in range(nb):
            E("pe", lambda e, bk=bk, b=b, c=c: e.transpose(out=bk.t[:, b * 128:(b + 1) * 128], in_=xf.t[:, b, c * 128:(c + 1) * 128],
                                                          identity=g.identf.t[:]), reads=g.XB(b) + [g.identf], writes=[bk])
        if c % 2 == 0:
            E("act", lambda e, bk=bk, c=c: e.copy(out=actT.t[:, c, 0:W], in_=bk.t[:, 0:W]), reads=[bk], writes=[actT])
        else:
            E("dve", lambda e, bk=bk, c=c: e.tensor_copy(out=actT.t[:, c, 0:W], in_=bk.t[:, 0:W]), reads=[bk], writes=[actT])
        release(bk)
    for jj in range(11):
        slot = g.load_gu(jj)
        gv = v3(slot.t, 0, 8, 256)
        uv = v3(slot.t, 2048, 8, 256)
        for k2 in range(2):
            j = jj * 2 + k2
            bG = bank()
            bU = bank()
            for kc in range(8):
                E("pe", lambda e, bG=bG, kc=kc, k2=k2, gv=gv: e.matmul(bG.t[:, 0:W], lhsT=gv[:, kc, k2 * 128:(k2 + 1) * 128], rhs=actT.t[:, kc, 0:W],
                                                                      start=(kc == 0), stop=(kc == 7)), reads=[slot, actT], writes=[bG])
            for kc in range(8):
                E("pe", lambda e, bU=bU, kc=kc, k2=k2, uv=uv: e.matmul(bU.t[:, 0:W], lhsT=uv[:, kc, k2 * 128:(k2 + 1) * 128], rhs=actT.t[:, kc, 0:W],
                                                                      start=(kc == 0), stop=(kc == 7)), reads=[slot, actT], writes=[bU])
            th = g.wtmp[j % 2]
            ga = g.wtmp2[j % 2]
            E("act", lambda e, bG=bG, th=th: e.activation(out=th.t[:, 0:W], in_=bG.t[:, 0:W], func=AF.Tanh, scale=0.5), reads=[bG], writes=[th])
            E("dve", lambda e, bG=bG, th=th, ga=ga: e.scalar_tensor_tensor(out=ga.t[:, 0:W], in0=th.t[:, 0:W], scalar=1.0, in1=bG.t[:, 0:W],
                                                                          op0=ALU.add, op1=ALU.mult), reads=[th, bG], writes=[ga])
            E("dve", lambda e, bU=bU, ga=ga, j=j: e.scalar_tensor_tensor(out=g.h1T.t[:, j, 0:W], in0=ga.t[:, 0:W], scalar=0.5, in1=bU.t[:, 0:W],
                                                                        op0=ALU.mult, op1=ALU.mult), reads=[ga, bU], writes=[g.h1T])
            release(bG, bU)
    bks = [bank() for _ in range(2 * nb)]
    for p in range(6):
        nj = 4 if p < 5 else 2
        slot = g.load_wd_rows(p, nj)
        wv = v3(slot.t, 0, nj, 1024)
        for jj in range(nj):
            j = p * 4 + jj
            for b in range(nb):
                for half in range(2):
                    bk = bks[b * 2 + half]
                    E("pe", lambda e, bk=bk, b=b, j=j, jj=jj, half=half, wv=wv: e.matmul(bk.t[:, 0:512], lhsT=g.h1T.t[:, j, b * 128:(b + 1) * 128],
                                                                                        rhs=wv[:, jj, half * 512:(half + 1) * 512],
                                                                                        start=(j == 0), stop=(j == 21)),
                      reads=[slot, g.h1T], writes=[bk])
    for b in range(nb):
        for half in range(2):
            bk = bks[b * 2 + half]
            hsl = slice(half * 512, (half + 1) * 512)
            xh = [g.xfB[b]] if half == 0 else [g.xfH[b]]
            E("dve", lambda e, bk=bk, b=b, hsl=hsl: e.scalar_tensor_tensor(out=xf.t[:, b, hsl], in0=xf.t[:, b, hsl], scalar=ALPHA, in1=bk.t[:, 0:512],
                                                                          op0=ALU.mult, op1=ALU.add), reads=xh + [bk], writes=xh)
            release(bk)
    for b in range(nb):
        layernorm(b, 1, b)
        DMA("sp", lambda e, s, b=b: e.dma_start(out=ydst[b * 128:(b + 1) * 128, :], in_=xf.t[:, b, :]).then_inc(s, 16),
            1, reads=g.XB(b), ds=g.xfB[b].ds, is_out=True)


def _schedule(g):
    E, DMA, bank, release = g.E, g.DMA, g.bank, g.release
    IA, OA, I, O = g.IA, g.OA, g.I, g.O
    cT = g.cT
    ps_cT = cT.t[:].ap[0][0]
    KSTOP = int(os.environ.get("KSTOP", "99"))
    off = 0
    for gi, nbp in enumerate([4, 4, 4, 4, 1]):
        if KSTOP < 1 or (KSTOP == 1 and gi > 0):
            break
        g.run_group("pre", IA["xpre"][off * 128:(off + nbp) * 128, :], nbp, vr_off=off * 128, want_kv=(gi == 4), fence_i=gi)
        if gi < 4:
            g.convert_gu(gi)
        off += nbp
    if KSTOP < 3:
        return
    for gi in range(4):
        if KSTOP == 3 and gi > 0:
            break
        if 30 <= KSTOP < 40 and gi >= KSTOP - 30:
            break
        g.run_group("full", IA["xs"][gi * 512:(gi + 1) * 512, :], 4, ndt_first=(g.nd_first if gi == 0 else None),
                    save_kv=(gi == 3 and os.environ.get("NOSAVE") is None), ydst=OA["y"][gi * 512:(gi + 1) * 512, :])
    if 30 <= KSTOP < 40:
        return
    DMA("sp", lambda e, s: (e.dma_start(out=OA["pk"], in_=g.kvf.t[:, 0:128]).then_inc(s, 16),
                            e.dma_start(out=OA["pv"], in_=g.kvf.t[:, 128:256]).then_inc(s, 16)),
        2, reads=[g.kvf], ds=g.kvf.ds, is_out=True)
    bk = bank()
    for h in range(4):
        E("pe", lambda e, bk=bk, h=h: e.transpose(out=bk.t[:, h * 128:(h + 1) * 128], in_=cT.t[:, h * 515 + 3 + 384:h * 515 + 3 + 512],
                                                 identity=g.identf.t[:]), reads=[cT, g.identf], writes=[bk])
    E("dve", lambda e, bk=bk: e.tensor_copy(out=g.ctok.t[:], in_=bk.t[:, 0:512]), reads=[bk], writes=[g.ctok])
    release(bk)
    DMA("sp", lambda e, s: e.dma_start(out=OA["pconv"], in_=g.ctok.t[125:128, :]).then_inc(s, 16), 1, reads=[g.ctok], ds=g.ctok.ds, is_out=True)
    bk = bank()
    for h in range(4):
        E("pe", lambda e, bk=bk, h=h: e.transpose(out=bk.t[:, h * 128:(h + 1) * 128], in_=g.Cext.t[:, h, 0:128], identity=g.identf.t[:]),
          reads=[g.Cext, g.identf], writes=[bk])
    E("dve", lambda e, bk=bk: e.tensor_copy(out=g.cfin.t[:], in_=bk.t[:, 0:512].rearrange("p (h k) -> p h k", h=4)), reads=[bk], writes=[g.cfin])
    release(bk)
    DMA("sp", lambda e, s: e.dma_start(out=OA["pC"].rearrange("h v k -> v h k"), in_=g.cfin.t[:]).then_inc(s, 16), 1,
        reads=[g.cfin], ds=g.cfin.ds, is_out=True)
    DMA("sp", lambda e, s: (e.dma_start(out=bass.AP(O["pn"], 0, [[1, 128], [128, 4]]), in_=g.Cext.t[:, :, 128],
                                        allow_slow_non_contiguous=True).then_inc(s, 16),),
        1, reads=[g.Cext], ds=g.Cext.ds, is_out=True)
    DMA("sp", lambda e, s: e.dma_start(out=OA["pm"], in_=g.msall.t[:, 0:1]).then_inc(s, 16), 1, reads=[g.msall], ds=g.msall.ds, is_out=True)

    if KSTOP < 5:
        return
    def ldc(e, s):
        e.dma_start(out=g.ckb.t[:], in_=IA["ck"].rearrange("j k f -> k j f")).then_inc(s, 16)
    DMA("pool", ldc, 1, writes=[g.ckb], ds=g.ckb.ds)
    DMA("pool", lambda e, s: e.dma_start(out=g.cVb.t[:], in_=IA["cv"].rearrange("j k f -> k j f")).then_inc(s, 16), 1, writes=[g.cVb], ds=g.cVb.ds)
    for j8 in range(2):
        bk = bank()
        bkb = bk.t[:].bitcast(BF16)
        for jj in range(8):
            j = j8 * 8 + jj
            E("pe", lambda e, bkb=bkb, jj=jj, j=j: e.transpose(out=bkb[:, jj * 128:(jj + 1) * 128], in_=g.ckb.t[:, j, :], identity=g.identb.t[:]),
              reads=[g.ckb, g.identb], writes=[bk])
        E("act", lambda e, bkb=bkb, j8=j8: e.copy(out=g.cKT.t[:, j8 * 8:(j8 + 1) * 8, :], in_=bkb[:, 0:1024].rearrange("p (j k) -> p j k", j=8)),
          reads=[bk], writes=[g.cKT])
        release(bk)
    d2d = DSem(g.new_sem("d_d2d"))
    DMA("sp", lambda e, s: (e.dma_start(out=OA["sk"][:, 0:120, :], in_=IA["ck"][:, 8:128, :]).then_inc(s, 16),
                            e.dma_start(out=OA["sv"][:, 0:120, :], in_=IA["cv"][:, 8:128, :]).then_inc(s, 16)),
        2, ds=d2d, is_out=True)

    def lds(e, s):
        e.dma_start(out=g.sc48.t[:], in_=IA["scv"]).then_inc(s, 16)
    DMA("sp", lds, 1, writes=[g.sc48], ds=g.sc48.ds)
    DMA("sp", lambda e, s: e.dma_start(out=g.ms16.t[:], in_=IA["smm"].rearrange("j h -> h j"), allow_slow_non_contiguous=True).then_inc(s, 16),
        1, writes=[g.ms16], ds=g.ms16.ds)
    DMA("sp", lambda e, s: e.dma_start(out=g.snat.t[:], in_=IA["sn"]).then_inc(s, 16), 1, writes=[g.snat], ds=g.snat.ds)
    DMA("sp", lambda e, s: e.dma_start(out=g.n16.t[:], in_=IA["sn"].rearrange("(j h) k -> j h k", h=4)).then_inc(s, 16), 1, writes=[g.n16], ds=g.n16.ds)
    bk = bank()
    for h in range(4):
        E("pe", lambda e, bk=bk, h=h: e.transpose(out=bk.t[:, h * 48:(h + 1) * 48], in_=g.sc48.t[:, h * 128:(h + 1) * 128],
                                                 identity=g.identf.t[0:48, 0:48]), reads=[g.sc48, g.identf], writes=[bk])
    for h in range(4):
        E("dve", lambda e, bk=bk, h=h: e.tensor_copy(out=bass.AP(cT.t, h * 176, [[ps_cT, 128], [11, 16], [1, 3]]),
                                                    in_=bk.t[:, h * 48:(h + 1) * 48].rearrange("p (j r) -> p j r", j=16)),
          reads=[bk], writes=[cT])
    release(bk)
    bk = bank()
    E("pe", lambda e, bk=bk: e.transpose(out=bk.t[:, 0:64], in_=g.snat.t[:], identity=g.identf.t[0:64, 0:64]), reads=[g.snat, g.identf], writes=[bk])
    E("dve", lambda e, bk=bk: e.tensor_copy(out=g.snT.t[:], in_=bk.t[:, 0:64]), reads=[bk], writes=[g.snT])
    release(bk)

    if KSTOP == 5:
        return
    E("dve", lambda e: e.memset(g.ZA.t[:], 0.0), writes=[g.ZA])
    E("dve", lambda e: e.memset(g.PZ.t[:], 0.0), writes=[g.PZ])
    g.run_group("smp", IA["xsm"], 1, save_kv=True, ydst=OA["ys"])
    if KSTOP == 6:
        return

    ps_kv = g.kvf.t[:].ap[0][0]

    def st_kv(e, s):
        for l in range(8):
            e.dma_start(out=OA["sk"][:, 120 + l, :], in_=bass.AP(g.kvf.t, l * ps_kv, [[8 * ps_kv, 16], [1, 128]])).then_inc(s, 16)
            e.dma_start(out=OA["sv"][:, 120 + l, :], in_=bass.AP(g.kvf.t, l * ps_kv + 128, [[8 * ps_kv, 16], [1, 128]])).then_inc(s, 16)
    DMA("sp", st_kv, 16, reads=[g.kvf], ds=g.kvf.ds, is_out=True)
    bk = bank()
    for h in range(4):
        E("pe", lambda e, bk=bk, h=h: e.transpose(out=bk.t[:, h * 128:(h + 1) * 128], in_=g.csm.t[:, h, :], identity=g.identf.t[:]),
          reads=[g.csm, g.identf], writes=[bk])
    E("dve", lambda e, bk=bk: e.tensor_copy(out=g.ctok.t[:], in_=bk.t[:, 0:512]), reads=[bk], writes=[g.ctok])
    release(bk)
    ps_ct = g.ctok.t[:].ap[0][0]

    def st_cv(e, s):
        for r in range(3):
            e.dma_start(out=OA["sconv"][:, r, :], in_=bass.AP(g.ctok.t, (5 + r) * ps_ct, [[8 * ps_ct, 16], [1, 512]])).then_inc(s, 16)
    DMA("sp", st_cv, 3, reads=[g.ctok], ds=g.ctok.ds, is_out=True)


_NC_CACHE = {}


def kernel(x_prompt, x_sample, cache_k, cache_v, state_conv, state_C, state_n, state_m, meta_tokens,
           w_in, w_conv, b_conv, w_mq, w_mk, b_i, b_f, attn_sinks, g_attn, g_mlstm, w_out,
           ln1_g, ln1_b, w_gate, w_up, w_down, ln2_g, ln2_b):
    f = lambda a: np.ascontiguousarray(np.asarray(a), dtype=np.float32)
    x_prompt, x_sample, cache_k, cache_v = f(x_prompt), f(x_sample), f(cache_k), f(cache_v)
    state_conv, state_C, state_n, state_m, meta_tokens = f(state_conv), f(state_C), f(state_n), f(state_m), f(meta_tokens)
    tabs = _tables()
    shared = {
        "w_in": f(w_in)[0], "w_conv": f(w_conv)[0], "b_conv": f(b_conv)[0].reshape(1, 512), "w_mq": f(w_mq)[0], "w_mk": f(w_mk)[0],
        "b_i": f(b_i)[0].reshape(4, 1), "b_f": f(b_f)[0].reshape(4, 1), "sinks": f(attn_sinks)[0].reshape(1, 8),
        "g_attn": f(g_attn)[0].reshape(1, 512), "g_mlstm": f(g_mlstm)[0].reshape(1, 512), "w_out": f(w_out)[0],
        "ln1_g": f(ln1_g)[0].reshape(1, D), "ln1_b": f(ln1_b)[0].reshape(1, D), "w_gate": f(w_gate)[0], "w_up": f(w_up)[0],
        "w_down": f(w_down)[0], "ln2_g": f(ln2_g)[0].reshape(1, D), "ln2_b": f(ln2_b)[0].reshape(1, D),
    }
    for n in TABLE_SHAPES:
        shared[n] = tabs[n]
    blk0 = np.zeros((128, D), np.float32)
    blk0[112:128] = meta_tokens
    in_maps = []
    for core in range(8):
        bq, half = core // 2, core % 2
        m = dict(shared)
        m["xs"] = np.ascontiguousarray(x_prompt[bq, half * 2048:(half + 1) * 2048])
        vr = np.zeros((4, NPRE * 128), np.float32)
        if half == 0:
            xpre = np.zeros((NPRE * 128, D), np.float32)
            xpre[16 * 128:] = blk0
            vr[:, 16 * 128 + 112:] = 1.0
            m["nd_first"] = tabs["nd_meta"]
        else:
            xpre = np.concatenate([blk0, x_prompt[bq, 0:2048]], axis=0)
            vr[:, 112:] = 1.0
            m["nd_first"] = tabs["nd_std"]
        m["xpre"] = np.ascontiguousarray(xpre)
        m["vrow"] = vr
        sq_ = slice(core * 16, (core + 1) * 16)
        m["xsm"] = np.ascontiguousarray(x_sample[sq_].reshape(128, D))
        m["ck"] = np.ascontiguousarray(cache_k[0, sq_].reshape(16, 128, 128))
        m["cv"] = np.ascontiguousarray(cache_v[0, sq_].reshape(16, 128, 128))
        m["scv"] = np.ascontiguousarray(state_conv[0, sq_].reshape(48, 512))
        m["sC"] = np.ascontiguousarray(state_C[0, sq_])
        m["sn"] = np.ascontiguousarray(state_n[0, sq_].reshape(64, 128))
        m["smm"] = np.ascontiguousarray(state_m[0, sq_])
        in_maps.append(m)
    if "nc" not in _NC_CACHE:
        _NC_CACHE["nc"] = build_program()
    nc = _NC_CACHE["nc"]
    res = run_bass_kernel_spmd(nc, in_maps, core_ids=list(range(8)))
    R = res.results
    y_prompt = np.stack([np.concatenate([R[2 * b]["y"], R[2 * b + 1]["y"]], axis=0) for b in range(4)], axis=0)
    y_sample = np.concatenate([R[c]["ys"].reshape(16, 8, D) for c in range(8)], axis=0)
    B = [R[2 * b + 1] for b in range(4)]
    pk = np.stack([r["pk"].reshape(128, 2, 64) for r in B])[None]
    pv = np.stack([r["pv"].reshape(128, 2, 64) for r in B])[None]
    pconv = np.stack([r["pconv"] for r in B])[None]
    pC = np.stack([r["pC"] for r in B])[None]
    pn = np.stack([r["pn"] for r in B])[None]
    pm = np.stack([r["pm"].reshape(4) for r in B])[None]
    sk = np.concatenate([R[c]["sk"].reshape(16, 128, 2, 64) for c in range(8)], axis=0)[None]
    sv = np.concatenate([R[c]["sv"].reshape(16, 128, 2, 64) for c in range(8)], axis=0)[None]
    sconv = np.concatenate([R[c]["sconv"] for c in range(8)], axis=0)[None]
    sC = np.concatenate([R[c]["sCo"] for c in range(8)], axis=0)[None]
    sn = np.concatenate([R[c]["sno"] for c in range(8)], axis=0)[None]
    sm = np.concatenate([R[c]["smo"] for c in range(8)], axis=0)[None]
    outs = (y_prompt, y_sample, pk, pv, pconv, pC, pn, pm, sk, sv, sconv, sC, sn, sm)
    return tuple(np.ascontiguousarray(o, dtype=np.float32) for o in outs)
```
